# Optimizing a Trainium2 kernel written in Bass

```python
import math
import jax, jax.numpy as jnp
from jax import lax
import numpy as np

D_MODEL = 1024
BATCH = 4
SEQ = 8192
DEPTH = 4

META_TOKENS = 16
POOL_WIDTH = D_MODEL
POOL_WINDOWS = (2, 4, 8, 16)
POOL_GROUPS = len(POOL_WINDOWS)
POOL_GROUP_DIM = POOL_WIDTH // POOL_GROUPS
N_HEADS = 16
HEAD_DIM = 64
ATTN_WIDTH = N_HEADS * HEAD_DIM
Q_BLOCK = 128
ATTN_PAD = (-META_TOKENS) % Q_BLOCK
D_FF = int(math.ceil(8 * D_MODEL / 3 / 256) * 256)
RMS_EPS = 1e-6
NEG_INF = -1e30
SPLIT_SIZES = (POOL_WIDTH, ATTN_WIDTH, ATTN_WIDTH, ATTN_WIDTH, N_HEADS, D_MODEL, D_MODEL)
SPLIT_POINTS = tuple(int(v) for v in np.cumsum(SPLIT_SIZES)[:-1])
N_IN = int(sum(SPLIT_SIZES))

kernel_name = "gated_pool_forgetting_attn_hybrid"


def rms_norm(x, g):
    xf = x.astype(jnp.float32)
    y = xf * lax.rsqrt(jnp.mean(xf * xf, axis=-1, keepdims=True) + RMS_EPS)
    return (y * g.astype(jnp.float32)).astype(x.dtype)


def multiscale_pool(u, w_pool, scale):
    B, T, _ = u.shape
    ug = u.reshape(B, T, POOL_GROUPS, POOL_GROUP_DIM)
    c0 = jnp.concatenate(
        [jnp.zeros((B, 1, POOL_GROUPS, POOL_GROUP_DIM), jnp.float32),
         jnp.cumsum(ug.astype(jnp.float32), axis=1)], axis=1)
    pos1 = jnp.arange(1, T + 1)
    outs = []
    for g, w in enumerate(POOL_WINDOWS):
        lag_idx = jnp.maximum(pos1 - w, 0)
        window_sum = c0[:, 1:, g] - jnp.take(c0[:, :, g], lag_idx, axis=1)
        count = jnp.minimum(pos1, w).astype(jnp.float32)[None, :, None]
        diff = (window_sum / count).astype(u.dtype) - ug[:, :, g]
        outs.append(jnp.einsum('btc,cd->btd', diff, w_pool[g]))
    return jnp.concatenate(outs, axis=-1) * scale


def forgetting_attention(q, k, v, log_f):
    B, T, H, Dh = q.shape
    pad4 = ((0, 0), (ATTN_PAD, 0), (0, 0), (0, 0))
    q = jnp.pad(q, pad4)
    k = jnp.pad(k, pad4)
    v = jnp.pad(v, pad4)
    F = jnp.cumsum(jnp.pad(log_f.astype(jnp.float32), ((0, 0), (ATTN_PAD, 0), (0, 0))), axis=1)
    L = T + ATTN_PAD
    n_blocks = L // Q_BLOCK
    key_pos = jnp.arange(L)
    key_valid = key_pos >= ATTN_PAD
    F_k = jnp.transpose(F, (0, 2, 1))[:, :, None, :]
    scale = 1.0 / math.sqrt(Dh)

    def block(i):
        start = i * Q_BLOCK
        qb = lax.dynamic_slice_in_dim(q, start, Q_BLOCK, axis=1)
        Fq = lax.dynamic_slice_in_dim(F, start, Q_BLOCK, axis=1)
        s = jnp.einsum('bqhd,bkhd->bhqk', qb, k, preferred_element_type=jnp.float32) * scale
        s = s + jnp.transpose(Fq, (0, 2, 1))[:, :, :, None] - F_k
        q_pos = start + jnp.arange(Q_BLOCK)
        mask = (key_pos[None, :] <= q_pos[:, None]) & key_valid[None, :]
        s = jnp.where(mask[None, None], s, NEG_INF)
        p = jax.nn.softmax(s, axis=-1)
        return jnp.einsum('bhqk,bkhd->bqhd', p.astype(v.dtype), v)

    out = lax.map(block, jnp.arange(n_blocks))
    out = jnp.transpose(out, (1, 0, 2, 3, 4)).reshape(B, L, H, Dh)
    return out[:, ATTN_PAD:]


def setup_inputs(seed: int = 0) -> dict:
    key = jax.random.key(seed)
    ks = jax.random.split(key, 16)
    f32 = jnp.float32
    nrm = lambda k, shape, s: jax.random.normal(k, shape, f32) * s
    gain = lambda k: 1.0 + 0.05 * jax.random.normal(k, (DEPTH, D_MODEL), f32)
    return {
        "x": jax.random.normal(ks[0], (BATCH, SEQ, D_MODEL), f32),
        "meta_tokens": nrm(ks[1], (META_TOKENS, D_MODEL), 1.0),
        "norm_mix_pre": gain(ks[2]),
        "norm_mix_post": gain(ks[3]),
        "norm_ffn_pre": gain(ks[4]),
        "norm_ffn_post": gain(ks[5]),
        "w_in": nrm(ks[6], (DEPTH, D_MODEL, N_IN), D_MODEL ** -0.5),
        "b_forget": jax.random.uniform(ks[7], (DEPTH, N_HEADS), f32, 1.0, 4.0),
        "w_pool": nrm(ks[8], (DEPTH, POOL_GROUPS, POOL_GROUP_DIM, POOL_GROUP_DIM), POOL_GROUP_DIM ** -0.5),
        "pool_scale": 1.0 + 0.05 * jax.random.normal(ks[9], (DEPTH, D_MODEL), f32),
        "w_out": nrm(ks[10], (DEPTH, D_MODEL, D_MODEL), D_MODEL ** -0.5),
        "w_ffn_gate": nrm(ks[11], (DEPTH, D_MODEL, D_FF), D_MODEL ** -0.5),
        "w_ffn_up": nrm(ks[12], (DEPTH, D_MODEL, D_FF), D_MODEL ** -0.5),
        "w_ffn_down": nrm(ks[13], (DEPTH, D_FF, D_MODEL), D_FF ** -0.5),
    }


def reference(x, meta_tokens, norm_mix_pre, norm_mix_post, norm_ffn_pre, norm_ffn_post,
              w_in, b_forget, w_pool, pool_scale, w_out, w_ffn_gate, w_ffn_up, w_ffn_down):
    B = x.shape[0]
    meta = jnp.broadcast_to(meta_tokens[None].astype(x.dtype), (B, META_TOKENS, D_MODEL))
    h_res = jnp.concatenate([meta, x], axis=1)
    T = h_res.shape[1]
    for l in range(DEPTH):
        h = rms_norm(h_res, norm_mix_pre[l])
        proj = jnp.einsum('btd,dn->btn', h, w_in[l])
        u_pool, q, k, v, f_logit, g_pool, g_attn = jnp.split(proj, SPLIT_POINTS, axis=-1)
        y_pool = multiscale_pool(u_pool, w_pool[l], pool_scale[l])
        log_f = jax.nn.log_sigmoid(f_logit.astype(jnp.float32) + b_forget[l].astype(jnp.float32))
        y_attn = forgetting_attention(
            q.reshape(B, T, N_HEADS, HEAD_DIM), k.reshape(B, T, N_HEADS, HEAD_DIM),
            v.reshape(B, T, N_HEADS, HEAD_DIM), log_f).reshape(B, T, ATTN_WIDTH)
        merged = jax.nn.sigmoid(g_pool) * y_pool + jax.nn.sigmoid(g_attn) * y_attn
        mix_out = jnp.einsum('btd,de->bte', merged, w_out[l])
        h_res = h_res + rms_norm(mix_out, norm_mix_post[l])
        h = rms_norm(h_res, norm_ffn_pre[l])
        ff = jax.nn.silu(jnp.einsum('btd,df->btf', h, w_ffn_gate[l])) * jnp.einsum('btd,df->btf', h, w_ffn_up[l])
        ff_out = jnp.einsum('btf,fd->btd', ff, w_ffn_down[l])
        h_res = h_res + rms_norm(ff_out, norm_ffn_post[l])
    return h_res[:, META_TOKENS:]
```

```python
import numpy as np
import ml_dtypes
from contextlib import ExitStack
import concourse.bass as bass
import concourse.mybir as mybir
from concourse.bass_utils import run_bass_kernel_spmd

F32 = mybir.dt.float32
BF16 = mybir.dt.bfloat16
ALU = mybir.AluOpType
AF = mybir.ActivationFunctionType

D = 1024
NH = 16
HD = 64
DFF = 2816
NFC = DFF // 128
NIN = 6160
META = 16
KA = 70
EPS = 1e-6
C_POOL, C_Q, C_K, C_V, C_F, C_GP, C_GA = 0, 1024, 2048, 3072, 4096, 4112, 5136
N_CORES = 8


class _Op:
    __slots__ = ("eng", "fn", "deps", "signal", "sigval", "is_dma", "lane", "laneval", "id")


class Sched:
    ENGS = ("pe", "act", "dve", "pool", "sp")

    def __init__(self, nc, n_lanes=8):
        self.nc = nc
        self.ops = []
        self.by_eng = {e: [] for e in self.ENGS}
        self.last_write = {}
        self.readers = {}
        self.n_lanes = n_lanes
        self.lane_next = {}
        self.lane_count = {}
        self.lane_last = {}
        self.pending_barrier = {}

    def _deps(self, eng, reads, writes):
        deps = set()
        for r in reads:
            w = self.last_write.get(r)
            if w is not None:
                deps.add(w)
        for t in writes:
            w = self.last_write.get(t)
            if w is not None:
                deps.add(w)
            rs = self.readers.get(t)
            if rs:
                deps.update(rs)
        pb = self.pending_barrier.pop(eng, None)
        if pb:
            deps.update(pb)
        return deps

    def _commit(self, oid, reads, writes):
        for r in reads:
            self.readers.setdefault(r, []).append(oid)
        for t in writes:
            self.last_write[t] = oid
            self.readers[t] = []

    def op(self, eng, fn, reads=(), writes=()):
        o = _Op()
        o.eng = eng
        o.fn = fn
        o.is_dma = False
        o.signal = False
        o.sigval = None
        o.id = len(self.ops)
        o.deps = self._deps(eng, reads, writes)
        self.ops.append(o)
        self.by_eng[eng].append(o)
        self._commit(o.id, reads, writes)
        return o.id

    def dma(self, queue, out, in_, reads=(), writes=()):
        o = _Op()
        o.eng = queue
        o.is_dma = True
        o.signal = True
        o.sigval = None
        o.id = len(self.ops)
        o.deps = self._deps(queue, reads, writes)
        lane = self.lane_next.get(queue, 0)
        self.lane_next[queue] = (lane + 1) % self.n_lanes
        key = (queue, lane)
        prev = self.lane_last.get(key)
        if prev is not None:
            o.deps.add(prev)
        self.lane_count[key] = self.lane_count.get(key, 0) + 1
        o.lane = key
        o.laneval = 16 * self.lane_count[key]
        self.lane_last[key] = o.id
        o.fn = lambda e, out=out, in_=in_: e.dma_start(out=out, in_=in_)
        self.ops.append(o)
        self.by_eng[queue].append(o)
        self._commit(o.id, reads, writes)
        return o.id

    def dma_custom(self, queue, fn, reads=(), writes=()):
        oid = self.dma(queue, None, None, reads=reads, writes=writes)
        self.ops[oid].fn = fn
        return oid

    def barrier(self):
        pend = set()
        for e in self.ENGS:
            if self.by_eng[e]:
                pend.add(self.by_eng[e][-1].id)
        for oid in self.lane_last.values():
            pend.add(oid)
        self.pending_barrier = {e: set(pend) for e in self.ENGS}

    def emit(self, stack):
        nc = self.nc
        ops = self.ops
        for o in ops:
            for d in o.deps:
                p = ops[d]
                if p.is_dma:
                    continue
                if p.eng != o.eng or p.eng != "pe":
                    p.signal = True
        for e in self.ENGS:
            lst = [o for o in self.by_eng[e] if not o.is_dma]
            if lst:
                lst[-1].signal = True
        for e in self.ENGS:
            c = 0
            for o in self.by_eng[e]:
                if o.is_dma:
                    continue
                if o.signal:
                    c += 1
                    o.sigval = c
        esem = {e: stack.enter_context(nc.semaphore("s_" + e)) for e in self.ENGS}
        lsem = {key: stack.enter_context(nc.semaphore("l_%s%d" % key)) for key in self.lane_count}
        final_waits = [(lsem[key], 16 * cnt) for key, cnt in self.lane_count.items()]
        for e in self.ENGS:
            lst = [o for o in self.by_eng[e] if not o.is_dma and o.signal]
            if lst:
                final_waits.append((esem[e], lst[-1].sigval))

        def run(e, engine):
            waited = {}
            for o in self.by_eng[e]:
                need = {}
                for d in o.deps:
                    p = ops[d]
                    if p.is_dma:
                        s, v = lsem[p.lane], p.laneval
                    else:
                        if p.eng == e and e == "pe":
                            continue
                        s, v = esem[p.eng], p.sigval
                    if v > need.get(s, 0):
                        need[s] = v
                for s, v in need.items():
                    if waited.get(s, 0) < v:
                        engine.wait_ge(s, v)
                        waited[s] = v
                ins = o.fn(engine)
                if o.is_dma:
                    ins.then_inc(lsem[o.lane], 16)
                elif o.signal:
                    ins.then_inc(esem[e], 1)
            if e == "sp":
                for s, v in final_waits:
                    engine.wait_ge(s, v)

        block = stack.enter_context(nc.Block())
        if self.by_eng["pe"]:
            @block.tensor
            def _(eng):
                run("pe", eng)
        if self.by_eng["act"]:
            @block.scalar
            def _(eng):
                run("act", eng)
        if self.by_eng["dve"]:
            @block.vector
            def _(eng):
                run("dve", eng)
        if self.by_eng["pool"]:
            @block.gpsimd
            def _(eng):
                run("pool", eng)

        @block.sync
        def _(eng):
            run("sp", eng)


def _tiles(L, w):
    out = []
    t = 0
    while t < L:
        ww = min(w, L - t)
        out.append((t, ww))
        t += ww
    return out


def build_nc(SEQ, DEPTH, stop_after=None):
    T = SEQ + META
    L = ((T + 127) // 128) * 128
    NB = L // 128
    tiles = _tiles(L, 512)
    NT = len(tiles)
    tiles2 = _tiles(L, 256)

    nc = bass.Bass("TRN2", target_bir_lowering=False)

    def din(name, shape, dt=F32):
        return nc.dram_tensor(name, list(shape), dt, kind="ExternalInput").ap()

    def dscr(name, shape, dt):
        return nc.dram_tensor(name, list(shape), dt, kind="Internal").ap()

    res0 = din("res0", [8, 128, L])
    gv_d = din("gv", [128, 5 * DEPTH * 8])
    bfb_d = din("bfb", [128, DEPTH * 64])
    negm_d = din("negm", [128, 128], BF16)
    identb_d = din("identb", [128, 128], BF16)
    U_d = din("umat", [128, 128])
    sel_d = din("sel", [16, 16 * 128])
    invc_d = din("invc", [128, 64])
    cneg_d = din("cneg", [3, L], BF16)
    cpos_d = din("cpos", [3, L], BF16)
    cv_d = din("cv", [128, NB, 64], BF16)
    w_in_d = din("w_in", [DEPTH, D, NIN])
    w_pool_d = din("w_pool", [DEPTH, 4, 256, 256])
    w_out_d = din("w_out", [DEPTH, D, D])
    w_g_d = din("w_ffn_gate", [DEPTH, D, DFF])
    w_u_d = din("w_ffn_up", [DEPTH, D, DFF])
    w_d_d = din("w_ffn_down", [DEPTH, DFF, D])
    outT = nc.dram_tensor("outT", [8, 128, SEQ], F32, kind="ExternalOutput").ap()

    res = dscr("res", [8, 128, L], F32)
    qT = dscr("qT", [NH, KA, L], BF16)
    kT = dscr("kT", [NH, KA, L], BF16)
    vA = dscr("vA", [NH, 128, NB, 128], BF16)
    sga = dscr("sga", [NH, 64, L], BF16)
    pmT = dscr("pmT", [8, 128, L], BF16)
    amT = dscr("amT", [NH, 64, L], BF16)
    win_b = dscr("win_b", [DEPTH, D, NIN], BF16)
    wpool_b = dscr("wpool_b", [DEPTH, 4, 256, 256], BF16)
    wout_b = dscr("wout_b", [DEPTH, D, D], BF16)
    wg_b = dscr("wg_b", [DEPTH, D, DFF], BF16)
    wu_b = dscr("wu_b", [DEPTH, D, DFF], BF16)
    wd_b = dscr("wd_b", [DEPTH, DFF, D], BF16)

    def gcol(kind, l, c):
        return (kind * DEPTH + l) * 8 + c

    with ExitStack() as st:
        S = Sched(nc)

        uniq = [0]

        def sb(ctx, name, shape, dt):
            uniq[0] += 1
            return ctx.enter_context(nc.sbuf_tensor("%s_%d" % (name, uniq[0]), list(shape), dt))

        gv = sb(st, "gv_sb", [128, 5 * DEPTH * 8], F32)
        bfb = sb(st, "bfb_sb", [128, DEPTH * 64], F32)
        negm = sb(st, "negm_sb", [128, 128], BF16)
        identb = sb(st, "identb_sb", [128, 128], BF16)
        umat = sb(st, "umat_sb", [128, 128], F32)
        invc = sb(st, "invc_sb", [128, 64], F32)
        cst = sb(st, "cst_sb", [128, 4], F32)
        onesb = sb(st, "onesb_sb", [128, 128], BF16)
        onesf = sb(st, "onesf_sb", [128, 1], F32)
        LF = sb(st, "LF_sb", [128, NB, 16], F32)
        RB = sb(st, "RB_sb", [128, NH, NT + NB], F32)
        uh = sb(st, "uh_sb", [128, 8, 16], F32)
        ps = [st.enter_context(nc.psum_tensor("ps%d" % i, [128, 512], F32)) for i in range(8)]
        PS = ["ps%d" % i for i in range(8)]

        S.dma("sp", gv[:], gv_d, writes=["gv"])
        S.dma("sp", bfb[:], bfb_d, writes=["bfb"])
        S.dma("sp", negm[:], negm_d, writes=["negm"])
        S.dma("sp", identb[:], identb_d, writes=["identb"])
        S.dma("sp", umat[:], U_d, writes=["umat"])
        S.dma("sp", invc[:], invc_d, writes=["invc"])
        S.op("dve", lambda e: e.memset(cst[:, 0:1], EPS), writes=["cst0"])
        S.op("dve", lambda e: e.memset(cst[:, 1:2], 1.0), writes=["cst1"])
        S.op("dve", lambda e: e.memset(onesb[:], 1.0 / 1024.0), writes=["onesb"])
        S.op("dve", lambda e: e.memset(onesf[:], 1.0), writes=["onesf"])

        def const_rows():
            for h in range(NH):
                S.dma("pool", qT[h, 67:70, :], cneg_d, writes=[("qc", h)])
                S.dma("pool", kT[h, 64:67, :], cpos_d, writes=[("kc", h)])
                S.dma("pool", vA[h, :, :, 64:128], cv_d, writes=[("vc", h)])
        const_rows()

        def convert_layer(l):
            for r0 in range(0, D, 128):
                S.dma("pool", win_b[l, r0:r0 + 128, :], w_in_d[l, r0:r0 + 128, :], writes=[("win_b", l)])
            S.dma("pool", wpool_b[l].rearrange("g a b -> (g a) b"),
                  w_pool_d[l].rearrange("g a b -> (g a) b"), writes=[("wpool_b", l)])
            for r0 in range(0, D, 256):
                S.dma("pool", wout_b[l, r0:r0 + 256, :], w_out_d[l, r0:r0 + 256, :], writes=[("wout_b", l)])
            for r0 in range(0, D, 128):
                S.dma("pool", wg_b[l, r0:r0 + 128, :], w_g_d[l, r0:r0 + 128, :], writes=[("wg_b", l)])
                S.dma("pool", wu_b[l, r0:r0 + 128, :], w_u_d[l, r0:r0 + 128, :], writes=[("wu_b", l)])
            for r0 in range(0, DFF, 256):
                S.dma("pool", wd_b[l, r0:r0 + 256, :], w_d_d[l, r0:r0 + 256, :], writes=[("wd_b", l)])

        convert_layer(0)

        def rtok(t0, W):
            return [("res", k) for k in range(t0 // 256, (t0 + W + 255) // 256)]

        rot = {"i": 0}

        def next_ps(choices):
            i = choices[rot["i"] % len(choices)]
            rot["i"] += 1
            return i

        def rms_stats(src_sq, W, ps_i, lnv, rstd, sqtok):
            def f(e):
                ins = None
                for c in range(8):
                    ins = e.matmul(ps[ps_i][:, 0:W], onesb[:, :], src_sq[:, c, 0:W], start=(c == 0), stop=(c == 7))
                return ins
            S.op("pe", f, reads=["onesb", sqtok], writes=[PS[ps_i]])
            S.op("act", lambda e: e.activation(lnv[:, 0:W], ps[ps_i][:, 0:W], AF.Ln, bias=cst[:, 0:1]),
                 reads=[PS[ps_i], "cst0"], writes=["lnv"])
            S.op("act", lambda e: e.activation(rstd[:, 0:W], lnv[:, 0:W], AF.Exp, scale=-0.5),
                 reads=["lnv"], writes=["rstd"])

        def phase1(l):
            src = res0 if l == 0 else res
            with ExitStack() as ph:
                win = sb(ph, "win", [128, 8, NIN], BF16)
                wpl = sb(ph, "wpl", [128, 4, 2, 256], BF16)
                xt = sb(ph, "p1_xt", [128, 8, 512], F32)
                hn = sb(ph, "p1_hn", [128, 8, 512], BF16)
                lnv = sb(ph, "p1_lnv", [128, 512], F32)
                rstd = sb(ph, "p1_rstd", [128, 512], F32)
                ug = [sb(ph, "p1_ug%d" % i, [128, 2, 528], F32) for i in range(2)]
                wa = sb(ph, "p1_wa", [128, 2, 528], F32)
                wb = sb(ph, "p1_wb", [128, 2, 528], F32)
                dsb = [sb(ph, "p1_dsb%d" % i, [128, 2, 512], BF16) for i in range(2)]
                sgp = [sb(ph, "p1_sgp%d" % i, [128, 2, 512], BF16) for i in range(2)]
                pm = sb(ph, "p1_pm", [128, 8, 512], BF16)
                stg = [sb(ph, "p1_stg%d" % i, [128, 4, 512], BF16) for i in range(3)]
                vsb = sb(ph, "p1_vsb", [128, 4, 16, 64], BF16)
                tf = sb(ph, "p1_tf", [128, 64], F32)
                ROT = [1, 2, 3, 4, 5, 6]

                for kc in range(8):
                    S.dma("sp", win[:, kc, :], win_b[l, kc * 128:(kc + 1) * 128, :],
                          reads=[("win_b", l)], writes=["win"])
                S.dma("sp", wpl[:], wpool_b[l].rearrange("g (k p) n -> p g k n", p=128),
                      reads=[("wpool_b", l)], writes=["wpl"])
                S.op("dve", lambda e: e.memset(uh[:], 0.0), writes=["uh"])
                stg_i = [0]

                for ti, (t0, W) in enumerate(tiles):
                    nsb = W // 128
                    S.dma("sp", xt[:, :, 0:W], src[:, :, t0:t0 + W].rearrange("c p t -> p c t"),
                          reads=rtok(t0, W), writes=["xt"])
                    S.op("dve", lambda e, W=W: e.tensor_tensor(hn[:, :, 0:W], xt[:, :, 0:W], xt[:, :, 0:W], ALU.mult),
                         reads=["xt"], writes=["hn"])
                    rms_stats(hn, W, 0, lnv, rstd, "hn")
                    for c in range(8):
                        S.op("dve", lambda e, c=c, W=W: e.scalar_tensor_tensor(
                            hn[:, c, 0:W], xt[:, c, 0:W], gv[:, gcol(0, l, c):gcol(0, l, c) + 1], rstd[:, 0:W],
                            ALU.mult, ALU.mult), reads=["xt", "rstd", "gv", PS[0]], writes=["hn"])

                    def proj(col0, ps_i, W=W):
                        def f(e):
                            ins = None
                            for kc in range(8):
                                ins = e.matmul(ps[ps_i][:, 0:W], win[:, kc, col0:col0 + 128], hn[:, kc, 0:W],
                                               start=(kc == 0), stop=(kc == 7))
                            return ins
                        S.op("pe", f, reads=["win", "hn"], writes=[PS[ps_i]])

                    for g in range(4):
                        w = 2 << g
                        u = ug[g % 2]
                        ut = "ug%d" % (g % 2)
                        S.op("dve", lambda e, u=u, g=g: e.tensor_copy(u[:, :, 0:16], uh[:, 2 * g:2 * g + 2, :]),
                             reads=["uh"], writes=[ut])
                        for k in range(2):
                            pi = next_ps(ROT)
                            proj(C_POOL + (2 * g + k) * 128, pi)
                            S.op("dve", lambda e, u=u, k=k, pi=pi, W=W: e.tensor_copy(u[:, k, 16:16 + W], ps[pi][:, 0:W]),
                                 reads=[PS[pi]], writes=[ut])
                        S.op("dve", lambda e, u=u, g=g, W=W: e.tensor_copy(uh[:, 2 * g:2 * g + 2, :], u[:, :, W:W + 16]),
                             reads=[ut], writes=["uh"])
                        cur, curt = u, ut
                        lo = 0
                        sh = 1
                        bufs = [(wa, "wa"), (wb, "wb")]
                        bi = 0
                        while sh < w:
                            dst, dstt = bufs[bi]
                            bi ^= 1
                            lo2 = lo + sh
                            S.op("dve", lambda e, dst=dst, cur=cur, lo2=lo2, sh=sh, W=W: e.tensor_tensor(
                                dst[:, :, lo2:16 + W], cur[:, :, lo2:16 + W], cur[:, :, lo2 - sh:16 + W - sh], ALU.add),
                                reads=[curt], writes=[dstt])
                            cur, curt = dst, dstt
                            lo = lo2
                            sh *= 2
                        d_ = dsb[g % 2]
                        dt_ = "dsb%d" % (g % 2)
                        S.op("dve", lambda e, d_=d_, cur=cur, u=u, w=w, W=W: e.scalar_tensor_tensor(
                            d_[:, :, 0:W], cur[:, :, 16:16 + W], 1.0 / w, u[:, :, 16:16 + W], ALU.mult, ALU.subtract),
                            reads=[curt, ut], writes=[dt_])
                        if ti == 0:
                            S.op("dve", lambda e, cur=cur, g=g: e.tensor_tensor(
                                cur[:, 0, 16:32], cur[:, 0, 16:32], invc[:, g * 16:(g + 1) * 16], ALU.mult),
                                reads=[curt, "invc", dt_], writes=[curt])
                            S.op("dve", lambda e, cur=cur, g=g: e.tensor_tensor(
                                cur[:, 1, 16:32], cur[:, 1, 16:32], invc[:, g * 16:(g + 1) * 16], ALU.mult),
                                reads=[curt, "invc"], writes=[curt])
                            S.op("dve", lambda e, d_=d_, cur=cur, u=u: e.tensor_tensor(
                                d_[:, :, 0:16], cur[:, :, 16:32], u[:, :, 16:32], ALU.subtract),
                                reads=[curt, ut], writes=[dt_])
                        sg_ = sgp[g % 2]
                        sgt = "sgp%d" % (g % 2)
                        for k in range(2):
                            pi = next_ps(ROT)
                            proj(C_GP + (2 * g + k) * 128, pi)
                            S.op("act", lambda e, sg_=sg_, k=k, pi=pi, W=W: e.activation(
                                sg_[:, k, 0:W], ps[pi][:, 0:W], AF.Sigmoid), reads=[PS[pi]], writes=[sgt])
                        for o2 in range(2):
                            pi = next_ps(ROT)

                            def f(e, g=g, o2=o2, pi=pi, d_=d_, W=W):
                                ins = None
                                for k2 in range(2):
                                    ins = e.matmul(ps[pi][:, 0:W], wpl[:, g, k2, o2 * 128:(o2 + 1) * 128],
                                                   d_[:, k2, 0:W], start=(k2 == 0), stop=(k2 == 1))
                                return ins
                            S.op("pe", f, reads=["wpl", dt_], writes=[PS[pi]])
                            c = 2 * g + o2
                            S.op("dve", lambda e, c=c, pi=pi, sg_=sg_, o2=o2, W=W: e.scalar_tensor_tensor(
                                pm[:, c, 0:W], ps[pi][:, 0:W], gv[:, gcol(4, l, c):gcol(4, l, c) + 1],
                                sg_[:, o2, 0:W], ALU.mult, ALU.mult),
                                reads=[PS[pi], sgt, "gv"], writes=["pm"])
                    S.dma("pool", pmT[:, :, t0:t0 + W].rearrange("c p t -> p c t"), pm[:, :, 0:W],
                          reads=["pm"], writes=[("pmT", ti)])

                    def fm_group(col0, dst, kind, tokname, W=W, t0=t0, ti=ti):
                        for half in range(2):
                            sgb = stg[stg_i[0] % 3]
                            sgtok = "stg%d" % (stg_i[0] % 3)
                            stg_i[0] += 1
                            for k in range(4):
                                c = half * 4 + k
                                pi = next_ps(ROT)
                                proj(col0 + c * 128, pi)
                                if kind == "q":
                                    S.op("dve", lambda e, sgb=sgb, k=k, pi=pi: e.tensor_scalar(
                                        sgb[:, k, 0:W], ps[pi][:, 0:W], 0.125, None, ALU.mult),
                                        reads=[PS[pi]], writes=[sgtok])
                                elif kind == "k":
                                    S.op("act", lambda e, sgb=sgb, k=k, pi=pi: e.activation(
                                        sgb[:, k, 0:W], ps[pi][:, 0:W], AF.Copy), reads=[PS[pi]], writes=[sgtok])
                                else:
                                    S.op("act", lambda e, sgb=sgb, k=k, pi=pi: e.activation(
                                        sgb[:, k, 0:W], ps[pi][:, 0:W], AF.Sigmoid), reads=[PS[pi]], writes=[sgtok])
                            h0 = half * 8
                            for par in range(2):
                                dview = dst[h0 + par:h0 + 8:2, 0:64, t0:t0 + W].rearrange("c r t -> r c t")
                                S.dma("pool", dview, sgb[par * 64:(par + 1) * 64, :, 0:W],
                                      reads=[sgtok], writes=[(tokname, ti)])
                    fm_group(C_Q, qT, "q", "qT")
                    fm_group(C_K, kT, "k", "kT")
                    fm_group(C_GA, sga, "g", "sga")

                    b0 = t0 // 128
                    for s_ in range(nsb):
                        pis = []
                        for half in range(2):
                            pi = next_ps(ROT)
                            pis.append(pi)

                            def f(e, s_=s_, half=half, pi=pi):
                                ins = None
                                for kc in range(8):
                                    ins = e.matmul(ps[pi][:, :], hn[:, kc, s_ * 128:(s_ + 1) * 128],
                                                   win[:, kc, C_V + half * 512:C_V + (half + 1) * 512],
                                                   start=(kc == 0), stop=(kc == 7))
                                return ins
                            S.op("pe", f, reads=["win", "hn"], writes=[PS[pi]])
                            S.op("dve" if half == 0 else "act",
                                 (lambda e, s_=s_, half=half, pi=pi: e.tensor_copy(
                                     vsb[:, s_, half * 8:(half + 1) * 8, :],
                                     ps[pi][:, :].rearrange("p (h d) -> p h d", d=64))) if half == 0 else
                                 (lambda e, s_=s_, half=half, pi=pi: e.activation(
                                     vsb[:, s_, half * 8:(half + 1) * 8, :],
                                     ps[pi][:, :].rearrange("p (h d) -> p h d", d=64), AF.Copy)),
                                 reads=[PS[pi]], writes=["vsb"])

                        def ff_(e, s_=s_):
                            ins = None
                            for kc in range(8):
                                ins = e.matmul(ps[7][:, s_ * 16:(s_ + 1) * 16], hn[:, kc, s_ * 128:(s_ + 1) * 128],
                                               win[:, kc, C_F:C_F + 16], start=(kc == 0), stop=(kc == 7))
                            return ins
                        S.op("pe", ff_, reads=["win", "hn"], writes=[PS[7]])
                    for s_ in range(nsb):
                        S.dma("pool", vA[:, :, b0 + s_, 0:64].rearrange("h p d -> p h d"),
                              vsb[:, s_, :, :], reads=["vsb"], writes=[("vA", ti)])
                    S.op("dve", lambda e, nsb=nsb: e.tensor_tensor(
                        tf[:, 0:nsb * 16], ps[7][:, 0:nsb * 16], bfb[:, l * 64:l * 64 + nsb * 16], ALU.add),
                        reads=[PS[7], "bfb"], writes=["tf"])
                    S.op("act", lambda e, nsb=nsb: e.activation(tf[:, 0:nsb * 16], tf[:, 0:nsb * 16], AF.Exp, scale=-1.0),
                         reads=["tf"], writes=["tf"])
                    S.op("act", lambda e, nsb=nsb: e.activation(tf[:, 0:nsb * 16], tf[:, 0:nsb * 16], AF.Ln, bias=cst[:, 1:2]),
                         reads=["tf", "cst1"], writes=["tf"])
                    S.op("dve", lambda e, nsb=nsb, b0=b0: e.tensor_scalar(
                        LF[:, b0:b0 + nsb, :], tf[:, 0:nsb * 16].rearrange("p (b h) -> p b h", h=16), -1.0, None, ALU.mult),
                        reads=["tf"], writes=["LF"])
            S.barrier()

        def phaseF(l):
            with ExitStack() as ph:
                FT = sb(ph, "f_FT", [16, L], F32)
                X = sb(ph, "f_X", [16, L], F32)
                R1 = sb(ph, "f_R1", [16, L], F32)
                A = sb(ph, "f_A", [16, 3, L], BF16)
                tot = sb(ph, "f_tot", [16, NB + 1], F32)
                car = sb(ph, "f_car", [16, NB + 1], F32)
                RS = sb(ph, "f_RS", [16, NT + NB], F32)
                sel = sb(ph, "f_sel", [16, 16 * 128], F32)
                S.dma("sp", sel[:], sel_d, writes=["sel"])
                for b in range(NB):
                    S.op("pe", lambda e, b=b: e.matmul(ps[6][0:16, b:b + 1], LF[:, b, :], onesf[:, 0:1],
                                                       start=True, stop=True),
                         reads=["LF", "onesf"], writes=[PS[6]])
                S.op("dve", lambda e: e.tensor_copy(tot[:, 0:NB], ps[6][0:16, 0:NB]), reads=[PS[6]], writes=["tot"])
                S.op("dve", lambda e: e.memset(car[:, 0:1], 0.0), writes=["car"])
                for b in range(1, NB):
                    S.op("dve", lambda e, b=b: e.tensor_tensor(car[:, b:b + 1], car[:, b - 1:b], tot[:, b - 1:b], ALU.add),
                         reads=["car", "tot"], writes=["car"])
                for b in range(NB):
                    pi = b % 4
                    S.op("pe", lambda e, b=b, pi=pi: e.matmul(ps[pi][0:16, 0:128], LF[:, b, :], umat[:, :],
                                                              start=True, stop=True),
                         reads=["LF", "umat"], writes=[PS[pi]])
                    S.op("dve", lambda e, b=b, pi=pi: e.tensor_scalar(
                        FT[:, b * 128:(b + 1) * 128], ps[pi][0:16, 0:128], car[:, b:b + 1], None, ALU.add),
                        reads=[PS[pi], "car"], writes=["FT"])
                S.op("dve", lambda e: e.tensor_copy(RS[:, 0:NT], FT[:, 0:L:512]), reads=["FT"], writes=["RS"])
                S.op("dve", lambda e: e.tensor_copy(RS[:, NT:NT + NB], FT[:, 0:L:128]), reads=["FT"], writes=["RS"])
                for h in range(NH):
                    pi = 4 + (h % 2)
                    S.op("pe", lambda e, h=h, pi=pi: e.matmul(ps[pi][:, 0:NT + NB], sel[:, h * 128:(h + 1) * 128],
                                                              RS[:, :], start=True, stop=True),
                         reads=["sel", "RS"], writes=[PS[pi]])
                    S.op("dve", lambda e, h=h, pi=pi: e.tensor_copy(RB[:, h, 0:NT], ps[pi][:, 0:NT]),
                         reads=[PS[pi]], writes=["RB"])
                    S.op("dve", lambda e, h=h, pi=pi: e.tensor_scalar(RB[:, h, NT:NT + NB], ps[pi][:, NT:NT + NB],
                                                                      -1.0, None, ALU.mult),
                         reads=[PS[pi]], writes=["RB"])

                def split3(dst_dram, row0, tokname):
                    S.op("dve", lambda e: e.tensor_copy(A[:, 0, :], X[:, :]), reads=["X"], writes=["A"])
                    S.op("dve", lambda e: e.tensor_tensor(R1[:, :], X[:, :], A[:, 0, :], ALU.subtract),
                         reads=["X", "A"], writes=["R1"])
                    S.op("dve", lambda e: e.tensor_copy(A[:, 1, :], R1[:, :]), reads=["R1"], writes=["A"])
                    S.op("dve", lambda e: e.tensor_tensor(R1[:, :], R1[:, :], A[:, 1, :], ALU.subtract),
                         reads=["R1", "A"], writes=["R1"])
                    S.op("dve", lambda e: e.tensor_copy(A[:, 2, :], R1[:, :]), reads=["R1"], writes=["A"])
                    S.dma("pool", dst_dram[:, row0:row0 + 3, :], A[:, :, :], reads=["A"], writes=[tokname])

                for ti, (t0, W) in enumerate(tiles):
                    S.op("dve", lambda e, ti=ti, t0=t0, W=W: e.tensor_scalar(
                        X[:, t0:t0 + W], FT[:, t0:t0 + W], RS[:, ti:ti + 1], None, ALU.subtract),
                        reads=["FT", "RS", "A"], writes=["X"])
                split3(qT, 64, "qTa")
                for b in range(NB):
                    S.op("dve", lambda e, b=b: e.tensor_scalar(
                        X[:, b * 128:(b + 1) * 128], FT[:, b * 128:(b + 1) * 128], RS[:, NT + b:NT + b + 1], None,
                        ALU.subtract), reads=["FT", "RS", "A"], writes=["X"])
                split3(kT, 67, "kTa")
            S.barrier()

        def phase2(l):
            with ExitStack() as ph:
                Kh = [sb(ph, "a_K%d" % i, [KA, L], BF16) for i in range(2)]
                Qh = [sb(ph, "a_Q%d" % i, [KA, L], BF16) for i in range(2)]
                Vh = [sb(ph, "a_V%d" % i, [128, NB, 128], BF16) for i in range(2)]
                Gh = [sb(ph, "a_G%d" % i, [64, L], BF16) for i in range(2)]
                Bc = [sb(ph, "a_B%d" % i, [128, NT, NB], F32) for i in range(2)]
                NPB = 6
                PT = [sb(ph, "a_P%d" % i, [128, 512], BF16) for i in range(NPB)]
                rb = sb(ph, "a_rb", [64, 512], F32)
                tt = sb(ph, "a_tt", [64, 512], F32)
                amo = [sb(ph, "a_am%d" % i, [64, 512], BF16) for i in range(2)]
                alltok = lambda name: [(name, ti) for ti in range(NT)]

                def load_head(h):
                    i = h % 2
                    S.dma("sp", Kh[i][:, :], kT[h, :, :], reads=alltok("kT") + ["kTa", ("kc", h)], writes=["Kh%d" % i])
                    S.dma("sp", Qh[i][:, :], qT[h, :, :], reads=alltok("qT") + ["qTa", ("qc", h)], writes=["Qh%d" % i])
                    S.dma("sp", Vh[i][:, :, :], vA[h, :, :, :], reads=alltok("vA") + [("vc", h)], writes=["Vh%d" % i])
                    S.dma("sp", Gh[i][:, :], sga[h, :, :], reads=alltok("sga"), writes=["Gh%d" % i])
                    for ti in range(NT):
                        S.op("dve", lambda e, i=i, h=h, ti=ti: e.tensor_scalar(
                            Bc[i][:, ti, :], RB[:, h, NT:NT + NB], RB[:, h, ti:ti + 1], None, ALU.add),
                            reads=["RB"], writes=["Bc%d" % i])

                LA = 3
                items = []
                for h in range(NH):
                    for ti, (t0, W) in enumerate(tiles):
                        jmax = (t0 + W) // 128 - 1
                        for j in range(jmax + 1):
                            items.append((h, ti, t0, W, j, jmax))

                def emit_front(n):
                    h, ti, t0, W, j, jmax = items[n]
                    i = h % 2
                    k0 = j * 128
                    q0 = max(t0, k0)
                    Wj = t0 + W - q0
                    si = n % 4
                    pb = n % NPB
                    diag = k0 >= t0

                    def fS(e, i=i, si=si, k0=k0, q0=q0, Wj=Wj, diag=diag):
                        if not diag:
                            return e.matmul(ps[si][:, 0:Wj], Kh[i][:, k0:k0 + 128], Qh[i][:, q0:q0 + Wj],
                                            start=True, stop=True)
                        e.matmul(ps[si][:, 0:128], identb[:, :], negm[:, :], start=True, stop=False)
                        ins = e.matmul(ps[si][:, 0:128], Kh[i][:, k0:k0 + 128], Qh[i][:, q0:q0 + 128],
                                       start=False, stop=True)
                        if Wj > 128:
                            ins = e.matmul(ps[si][:, 128:Wj], Kh[i][:, k0:k0 + 128], Qh[i][:, q0 + 128:q0 + Wj],
                                           start=True, stop=True)
                        return ins
                    S.op("pe", fS, reads=["Kh%d" % i, "Qh%d" % i, "identb", "negm"], writes=[PS[si]])
                    S.op("act", lambda e, i=i, si=si, pb=pb, ti=ti, j=j, Wj=Wj: e.activation(
                        PT[pb][:, 0:Wj], ps[si][:, 0:Wj], AF.Exp, bias=Bc[i][:, ti, j:j + 1]),
                        reads=[PS[si], "Bc%d" % i], writes=["PT%d" % pb])

                def emit_back(n):
                    h, ti, t0, W, j, jmax = items[n]
                    i = h % 2
                    if ti == 0 and j == 0 and h + 1 < NH:
                        load_head(h + 1)
                    k0 = j * 128
                    q0 = max(t0, k0)
                    c0 = q0 - t0
                    Wj = W - c0
                    pb = n % NPB
                    tcount = h * NT + ti
                    oi = 4 + (tcount % 2)
                    S.op("pe", lambda e, i=i, oi=oi, pb=pb, j=j, c0=c0, Wj=Wj, jmax=jmax: e.matmul(
                        ps[oi][:, c0:c0 + Wj], Vh[i][:, j, :], PT[pb][:, 0:Wj], start=(j == 0), stop=(j == jmax)),
                        reads=["Vh%d" % i, "PT%d" % pb], writes=[PS[oi]])
                    if j != jmax:
                        return
                    S.op("act", lambda e, oi=oi, W=W: e.activation(rb[:, 0:W], ps[oi][64:128, 0:W], AF.Copy),
                         reads=[PS[oi]], writes=["rb"])
                    S.op("dve", lambda e, W=W: e.reciprocal(rb[:, 0:W], rb[:, 0:W]), reads=["rb"], writes=["rb"])
                    S.op("dve", lambda e, oi=oi, W=W: e.tensor_tensor(tt[:, 0:W], ps[oi][0:64, 0:W], rb[:, 0:W], ALU.mult),
                         reads=[PS[oi], "rb"], writes=["tt"])
                    ai = tcount % 2
                    S.op("dve", lambda e, ai=ai, i=i, t0=t0, W=W: e.tensor_tensor(
                        amo[ai][:, 0:W], tt[:, 0:W], Gh[i][:, t0:t0 + W], ALU.mult),
                        reads=["tt", "Gh%d" % i], writes=["amo%d" % ai])
                    S.dma("pool", amT[h, :, t0:t0 + W], amo[ai][:, 0:W], reads=["amo%d" % ai], writes=[("amT", ti)])

                load_head(0)
                for n in range(len(items) + LA):
                    if n < len(items):
                        emit_front(n)
                    if n - LA >= 0:
                        emit_back(n - LA)
            S.barrier()

        def phase3a(l):
            with ExitStack() as ph:
                wop = sb(ph, "c_wop", [128, 8, D], BF16)
                woa = sb(ph, "c_woa", [64, NH, D], BF16)
                am = sb(ph, "c_am", [64, NH, 512], BF16)
                pm = sb(ph, "c_pm", [128, 8, 512], BF16)
                xt = sb(ph, "c_xt", [128, 8, 512], F32)
                mo = sb(ph, "c_mo", [128, 8, 512], F32)
                sq = sb(ph, "c_sq", [128, 8, 512], BF16)
                lnv = sb(ph, "c_lnv", [128, 512], F32)
                rstd = sb(ph, "c_rstd", [128, 512], F32)
                xo = sb(ph, "c_xo", [128, 8, 512], F32)
                ROT = [1, 2, 3, 4]
                S.dma("sp", wop[:], wout_b[l].rearrange("(c p) n -> p c n", p=128), reads=[("wout_b", l)], writes=["wop"])
                S.dma("sp", woa[:], wout_b[l].rearrange("(h p) n -> p h n", p=64), reads=[("wout_b", l)], writes=["woa"])
                src = res0 if l == 0 else res
                for ti, (t0, W) in enumerate(tiles):
                    S.dma("sp", am[:, :, 0:W], amT[:, :, t0:t0 + W].rearrange("h r t -> r h t"),
                          reads=[("amT", ti)], writes=["am"])
                    S.dma("sp", pm[:, :, 0:W], pmT[:, :, t0:t0 + W].rearrange("c p t -> p c t"),
                          reads=[("pmT", ti)], writes=["pm3"])
                    S.dma("sp", xt[:, :, 0:W], src[:, :, t0:t0 + W].rearrange("c p t -> p c t"),
                          reads=rtok(t0, W), writes=["xt3"])
                    for oc in range(8):
                        pi = next_ps(ROT)

                        def f(e, oc=oc, pi=pi, W=W):
                            ins = None
                            for c in range(8):
                                ins = e.matmul(ps[pi][:, 0:W], wop[:, c, oc * 128:(oc + 1) * 128], pm[:, c, 0:W],
                                               start=(c == 0), stop=False)
                            for h in range(NH):
                                ins = e.matmul(ps[pi][:, 0:W], woa[:, h, oc * 128:(oc + 1) * 128], am[:, h, 0:W],
                                               start=False, stop=(h == NH - 1))
                            return ins
                        S.op("pe", f, reads=["wop", "woa", "pm3", "am"], writes=[PS[pi]])
                        S.op("act", lambda e, oc=oc, pi=pi, W=W: e.activation(mo[:, oc, 0:W], ps[pi][:, 0:W], AF.Copy),
                             reads=[PS[pi]], writes=["mo"])
                    S.op("dve", lambda e, W=W: e.tensor_tensor(sq[:, :, 0:W], mo[:, :, 0:W], mo[:, :, 0:W], ALU.mult),
                         reads=["mo"], writes=["sq3"])
                    rms_stats(sq, W, 0, lnv, rstd, "sq3")
                    for c in range(8):
                        S.op("dve", lambda e, c=c, W=W: e.scalar_tensor_tensor(
                            mo[:, c, 0:W], mo[:, c, 0:W], gv[:, gcol(1, l, c):gcol(1, l, c) + 1], rstd[:, 0:W],
                            ALU.mult, ALU.mult), reads=["mo", "rstd", "gv"], writes=["mo"])
                    S.op("dve", lambda e, W=W: e.tensor_tensor(xo[:, :, 0:W], mo[:, :, 0:W], xt[:, :, 0:W], ALU.add),
                         reads=["mo", "xt3"], writes=["xo3"])
                    S.dma("pool", res[:, :, t0:t0 + W].rearrange("c p t -> p c t"), xo[:, :, 0:W],
                          reads=["xo3"], writes=rtok(t0, W))
            S.barrier()

        def phase3b(l, last):
            with ExitStack() as ph:
                wg = sb(ph, "d_wg", [128, 8, DFF], BF16)
                wu = sb(ph, "d_wu", [128, 8, DFF], BF16)
                wd = sb(ph, "d_wd", [128, NFC, D], BF16)
                xt = [sb(ph, "d_xt%d" % i, [128, 8, 256], F32) for i in range(2)]
                hn = [sb(ph, "d_hn%d" % i, [128, 8, 256], BF16) for i in range(2)]
                lnvA = sb(ph, "d_lnvA", [128, 256], F32)
                rstdA = sb(ph, "d_rstdA", [128, 256], F32)
                lnvB = sb(ph, "d_lnvB", [128, 256], F32)
                rstdB = sb(ph, "d_rstdB", [128, 256], F32)
                sg = [sb(ph, "d_sg%d" % i, [128, 256], F32) for i in range(2)]
                ff = sb(ph, "d_ff", [128, NFC, 256], BF16)
                fo = sb(ph, "d_fo", [128, 8, 256], F32)
                sq = sb(ph, "d_sq", [128, 8, 256], BF16)
                ROT = [1, 2, 3, 4, 5, 6]
                for kc in range(8):
                    S.dma("sp", wg[:, kc, :], wg_b[l, kc * 128:(kc + 1) * 128, :], reads=[("wg_b", l)], writes=["wg"])
                    S.dma("sp", wu[:, kc, :], wu_b[l, kc * 128:(kc + 1) * 128, :], reads=[("wu_b", l)], writes=["wu"])
                S.dma("sp", wd[:], wd_b[l].rearrange("(c p) n -> p c n", p=128), reads=[("wd_b", l)], writes=["wd"])

                def stats(src_sq, W, lnv, rstd, sqtok, sfx):
                    def f(e):
                        ins = None
                        for c in range(8):
                            ins = e.matmul(ps[0][:, 0:W], onesb[:, :], src_sq[:, c, 0:W], start=(c == 0), stop=(c == 7))
                        return ins
                    S.op("pe", f, reads=["onesb", sqtok], writes=[PS[0]])
                    S.op("act", lambda e: e.activation(lnv[:, 0:W], ps[0][:, 0:W], AF.Ln, bias=cst[:, 0:1]),
                         reads=[PS[0], "cst0"], writes=["lnv" + sfx])
                    S.op("act", lambda e: e.activation(rstd[:, 0:W], lnv[:, 0:W], AF.Exp, scale=-0.5),
                         reads=["lnv" + sfx], writes=["rstd" + sfx])

                def prologue(t2):
                    t0, W = tiles2[t2]
                    b = t2 % 2
                    S.dma("sp", xt[b][:, :, 0:W], res[:, :, t0:t0 + W].rearrange("c p t -> p c t"),
                          reads=rtok(t0, W), writes=["xt4%d" % b])
                    S.op("dve", lambda e: e.tensor_tensor(hn[b][:, :, 0:W], xt[b][:, :, 0:W], xt[b][:, :, 0:W], ALU.mult),
                         reads=["xt4%d" % b], writes=["hn4%d" % b])
                    stats(hn[b], W, lnvA, rstdA, "hn4%d" % b, "A")
                    for c in range(8):
                        S.op("dve", lambda e, c=c: e.scalar_tensor_tensor(
                            hn[b][:, c, 0:W], xt[b][:, c, 0:W], gv[:, gcol(2, l, c):gcol(2, l, c) + 1], rstdA[:, 0:W],
                            ALU.mult, ALU.mult), reads=["xt4%d" % b, "rstdA", "gv", PS[0]], writes=["hn4%d" % b])

                def gate_up(t2, fc0, fc1):
                    t0, W = tiles2[t2]
                    b = t2 % 2
                    for fc in range(fc0, fc1):
                        pg = next_ps(ROT)
                        pu = next_ps(ROT)

                        def fg(e, fc=fc, pg=pg):
                            ins = None
                            for kc in range(8):
                                ins = e.matmul(ps[pg][:, 0:W], wg[:, kc, fc * 128:(fc + 1) * 128], hn[b][:, kc, 0:W],
                                               start=(kc == 0), stop=(kc == 7))
                            return ins

                        def fu(e, fc=fc, pu=pu):
                            ins = None
                            for kc in range(8):
                                ins = e.matmul(ps[pu][:, 0:W], wu[:, kc, fc * 128:(fc + 1) * 128], hn[b][:, kc, 0:W],
                                               start=(kc == 0), stop=(kc == 7))
                            return ins
                        S.op("pe", fg, reads=["wg", "hn4%d" % b], writes=[PS[pg]])
                        S.op("pe", fu, reads=["wu", "hn4%d" % b], writes=[PS[pu]])
                        sgi = fc % 2
                        S.op("act", lambda e, sgi=sgi, pg=pg: e.activation(sg[sgi][:, 0:W], ps[pg][:, 0:W], AF.Silu),
                             reads=[PS[pg]], writes=["sg%d" % sgi])
                        S.op("dve", lambda e, fc=fc, sgi=sgi, pu=pu: e.tensor_tensor(
                            ff[:, fc, 0:W], sg[sgi][:, 0:W], ps[pu][:, 0:W], ALU.mult),
                            reads=["sg%d" % sgi, PS[pu]], writes=["ff"])

                def down(t2):
                    t0, W = tiles2[t2]
                    for oc in range(8):
                        pi = next_ps(ROT)

                        def fd(e, oc=oc, pi=pi):
                            ins = None
                            for fc in range(NFC):
                                ins = e.matmul(ps[pi][:, 0:W], wd[:, fc, oc * 128:(oc + 1) * 128], ff[:, fc, 0:W],
                                               start=(fc == 0), stop=(fc == NFC - 1))
                            return ins
                        S.op("pe", fd, reads=["wd", "ff"], writes=[PS[pi]])
                        S.op("act", lambda e, oc=oc, pi=pi: e.activation(fo[:, oc, 0:W], ps[pi][:, 0:W], AF.Copy),
                             reads=[PS[pi]], writes=["fo"])

                EPI = "pool"

                def epiA(t2):
                    t0, W = tiles2[t2]
                    S.op(EPI, lambda e: e.tensor_tensor(sq[:, :, 0:W], fo[:, :, 0:W], fo[:, :, 0:W], ALU.mult),
                         reads=["fo"], writes=["sq4"])

                def epiB(t2):
                    t0, W = tiles2[t2]
                    stats(sq, W, lnvB, rstdB, "sq4", "B")

                def epiC(t2):
                    t0, W = tiles2[t2]
                    b = t2 % 2
                    for c in range(8):
                        S.op("dve", lambda e, c=c: e.scalar_tensor_tensor(
                            fo[:, c, 0:W], fo[:, c, 0:W], gv[:, gcol(3, l, c):gcol(3, l, c) + 1], rstdB[:, 0:W],
                            ALU.mult, ALU.mult), reads=["fo", "rstdB", "gv"], writes=["fo"])
                    S.op(EPI, lambda e: e.tensor_tensor(fo[:, :, 0:W], fo[:, :, 0:W], xt[b][:, :, 0:W], ALU.add),
                         reads=["fo", "xt4%d" % b], writes=["fo"])
                    if not last:
                        S.dma("pool", res[:, :, t0:t0 + W].rearrange("c p t -> p c t"), fo[:, :, 0:W],
                              reads=["fo"], writes=rtok(t0, W))
                    else:
                        a = max(t0, META)
                        bb = min(t0 + W, T)
                        if bb > a:
                            S.dma("pool", outT[:, :, a - META:bb - META].rearrange("c p t -> p c t"),
                                  fo[:, :, a - t0:bb - t0], reads=["fo"], writes=[("out", t2)])

                n2 = len(tiles2)
                prologue(0)
                for t2 in range(n2):
                    if t2 > 0:
                        epiA(t2 - 1)
                    gate_up(t2, 0, 4)
                    if t2 > 0:
                        epiB(t2 - 1)
                        epiC(t2 - 1)
                    gate_up(t2, 4, NFC)
                    if t2 + 1 < n2:
                        prologue(t2 + 1)
                    down(t2)
                epiA(n2 - 1)
                epiB(n2 - 1)
                epiC(n2 - 1)
            S.barrier()

        S.barrier()
        for l in range(DEPTH):
            phase1(l)
            if stop_after in ("p1", "%d:p1" % l):
                break
            if l + 1 < DEPTH:
                convert_layer(l + 1)
            phaseF(l)
            if stop_after in ("pf", "%d:pf" % l):
                break
            phase2(l)
            if stop_after in ("p2", "%d:p2" % l):
                break
            phase3a(l)
            if stop_after in ("p3a", "%d:p3a" % l):
                break
            phase3b(l, last=(l == DEPTH - 1))
        S.emit(st)
    return nc


def _host_consts(DEPTH):
    negm = np.where(np.arange(128)[:, None] > np.arange(128)[None, :], -30000.0, 0.0).astype(ml_dtypes.bfloat16)
    identb = np.eye(128, dtype=np.float32).astype(ml_dtypes.bfloat16)
    umat = (np.arange(128)[:, None] <= np.arange(128)[None, :]).astype(np.float32)
    sel = np.zeros((16, 16, 128), np.float32)
    for h in range(16):
        sel[h, h, :] = 1.0
    sel = sel.reshape(16, 16 * 128)
    invc = np.zeros((128, 4, 16), np.float32)
    for g in range(4):
        w = 2 << g
        invc[:, g, :] = 1.0 / np.minimum(np.arange(16) + 1, w).astype(np.float32)[None, :]
    return negm, identb, umat, sel, invc.reshape(128, 64)


def make_in_maps(inputs, SEQ, DEPTH, n_cores):
    T = SEQ + META
    L = ((T + 127) // 128) * 128
    x = np.asarray(inputs["x"], np.float32)
    B = x.shape[0]
    meta = np.asarray(inputs["meta_tokens"], np.float32)
    negm, identb, umat, sel, invc = _host_consts(DEPTH)
    kinds = ["norm_mix_pre", "norm_mix_post", "norm_ffn_pre", "norm_ffn_post", "pool_scale"]
    gv = np.stack([np.asarray(inputs[k], np.float32) for k in kinds], 0)
    gv = gv.reshape(5, DEPTH, 8, 128).transpose(3, 0, 1, 2).reshape(128, 5 * DEPTH * 8)
    bf = np.asarray(inputs["b_forget"], np.float32)
    bfb = np.broadcast_to(bf[None, :, None, :], (128, DEPTH, 4, 16)).reshape(128, DEPTH * 64)
    common = {
        "gv": np.ascontiguousarray(gv), "bfb": np.ascontiguousarray(bfb),
        "negm": negm, "identb": identb, "umat": umat, "sel": sel, "invc": invc,
        "cneg": np.full((3, L), -1.0, ml_dtypes.bfloat16), "cpos": np.full((3, L), 1.0, ml_dtypes.bfloat16),
        "cv": np.full((128, L // 128, 64), 1.0, ml_dtypes.bfloat16),
        "w_in": np.asarray(inputs["w_in"], np.float32), "w_pool": np.asarray(inputs["w_pool"], np.float32),
        "w_out": np.asarray(inputs["w_out"], np.float32), "w_ffn_gate": np.asarray(inputs["w_ffn_gate"], np.float32),
        "w_ffn_up": np.asarray(inputs["w_ffn_up"], np.float32), "w_ffn_down": np.asarray(inputs["w_ffn_down"], np.float32),
    }
    maps = []
    for c in range(n_cores):
        b = c % B
        r = np.zeros((L, D), np.float32)
        r[0:META] = meta
        r[META:T] = x[b]
        r0 = np.ascontiguousarray(r.T.reshape(8, 128, L))
        m = dict(common)
        m["res0"] = r0
        maps.append(m)
    return maps


_NC_CACHE = {}


def kernel(x, meta_tokens, norm_mix_pre, norm_mix_post, norm_ffn_pre, norm_ffn_post,
           w_in, b_forget, w_pool, pool_scale, w_out, w_ffn_gate, w_ffn_up, w_ffn_down):
    inputs = dict(x=x, meta_tokens=meta_tokens, norm_mix_pre=norm_mix_pre, norm_mix_post=norm_mix_post,
                  norm_ffn_pre=norm_ffn_pre, norm_ffn_post=norm_ffn_post, w_in=w_in, b_forget=b_forget,
                  w_pool=w_pool, pool_scale=pool_scale, w_out=w_out, w_ffn_gate=w_ffn_gate,
                  w_ffn_up=w_ffn_up, w_ffn_down=w_ffn_down)
    x = np.asarray(x)
    B, SEQ, _ = x.shape
    DEPTH = np.asarray(w_in).shape[0]
    key = (SEQ, DEPTH)
    if key not in _NC_CACHE:
        _NC_CACHE[key] = build_nc(SEQ, DEPTH)
    nc = _NC_CACHE[key]
    maps = make_in_maps(inputs, SEQ, DEPTH, N_CORES)
    res = run_bass_kernel_spmd(nc, maps, core_ids=list(range(N_CORES)))
    out = np.empty((B, SEQ, D), np.float32)
    for b in range(B):
        o = np.asarray(res.results[b]["outT"], np.float32)
        out[b] = o.reshape(D, SEQ).T
    return out
```

```python
import numpy as np
import ml_dtypes
from contextlib import ExitStack
import concourse.bass as bass
import concourse.mybir as mybir
from concourse.bass_utils import run_bass_kernel_spmd

F32 = mybir.dt.float32
BF16 = mybir.dt.bfloat16
ALU = mybir.AluOpType
AF = mybir.ActivationFunctionType

D = 1024
NH = 16
HD = 64
DFF = 2816
NFC = DFF // 128
NIN = 6160
META = 16
KA = 70
EPS = 1e-6
C_POOL, C_Q, C_K, C_V, C_F, C_GP, C_GA = 0, 1024, 2048, 3072, 4096, 4112, 5136
N_CORES = 8


class _Op:
    __slots__ = ("eng", "fn", "deps", "signal", "sigval", "is_dma", "lane", "laneval", "id")


class Sched:
    ENGS = ("pe", "act", "dve", "pool", "sp")

    def __init__(self, nc, n_lanes=8):
        self.nc = nc
        self.ops = []
        self.by_eng = {e: [] for e in self.ENGS}
        self.last_write = {}
        self.readers = {}
        self.n_lanes = n_lanes
        self.lane_next = {}
        self.lane_count = {}
        self.lane_last = {}
        self.pending_barrier = {}

    def _deps(self, eng, reads, writes):
        deps = set()
        for r in reads:
            w = self.last_write.get(r)
            if w is not None:
                deps.add(w)
        for t in writes:
            w = self.last_write.get(t)
            if w is not None:
                deps.add(w)
            rs = self.readers.get(t)
            if rs:
                deps.update(rs)
        pb = self.pending_barrier.pop(eng, None)
        if pb:
            deps.update(pb)
        return deps

    def _commit(self, oid, reads, writes):
        for r in reads:
            self.readers.setdefault(r, []).append(oid)
        for t in writes:
            self.last_write[t] = oid
            self.readers[t] = []

    def op(self, eng, fn, reads=(), writes=()):
        o = _Op()
        o.eng = eng
        o.fn = fn
        o.is_dma = False
        o.signal = False
        o.sigval = None
        o.id = len(self.ops)
        o.deps = self._deps(eng, reads, writes)
        self.ops.append(o)
        self.by_eng[eng].append(o)
        self._commit(o.id, reads, writes)
        return o.id

    def dma(self, queue, out, in_, reads=(), writes=()):
        o = _Op()
        o.eng = queue
        o.is_dma = True
        o.signal = True
        o.sigval = None
        o.id = len(self.ops)
        o.deps = self._deps(queue, reads, writes)
        lane = self.lane_next.get(queue, 0)
        self.lane_next[queue] = (lane + 1) % self.n_lanes
        key = (queue, lane)
        prev = self.lane_last.get(key)
        if prev is not None:
            o.deps.add(prev)
        self.lane_count[key] = self.lane_count.get(key, 0) + 1
        o.lane = key
        o.laneval = 16 * self.lane_count[key]
        self.lane_last[key] = o.id
        o.fn = lambda e, out=out, in_=in_: e.dma_start(out=out, in_=in_)
        self.ops.append(o)
        self.by_eng[queue].append(o)
        self._commit(o.id, reads, writes)
        return o.id

    def dma_custom(self, queue, fn, reads=(), writes=()):
        oid = self.dma(queue, None, None, reads=reads, writes=writes)
        self.ops[oid].fn = fn
        return oid

    def barrier(self):
        pend = set()
        for e in self.ENGS:
            if self.by_eng[e]:
                pend.add(self.by_eng[e][-1].id)
        for oid in self.lane_last.values():
            pend.add(oid)
        self.pending_barrier = {e: set(pend) for e in self.ENGS}

    def emit(self, stack):
        nc = self.nc
        ops = self.ops
        for o in ops:
            for d in o.deps:
                p = ops[d]
                if p.is_dma:
                    continue
                if p.eng != o.eng or p.eng != "pe":
                    p.signal = True
        for e in self.ENGS:
            lst = [o for o in self.by_eng[e] if not o.is_dma]
            if lst:
                lst[-1].signal = True
        for e in self.ENGS:
            c = 0
            for o in self.by_eng[e]:
                if o.is_dma:
                    continue
                if o.signal:
                    c += 1
                    o.sigval = c
        esem = {e: stack.enter_context(nc.semaphore("s_" + e)) for e in self.ENGS}
        lsem = {key: stack.enter_context(nc.semaphore("l_%s%d" % key)) for key in self.lane_count}
        final_waits = [(lsem[key], 16 * cnt) for key, cnt in self.lane_count.items()]
        for e in self.ENGS:
            lst = [o for o in self.by_eng[e] if not o.is_dma and o.signal]
            if lst:
                final_waits.append((esem[e], lst[-1].sigval))

        def run(e, engine):
            waited = {}
            for o in self.by_eng[e]:
                need = {}
                for d in o.deps:
                    p = ops[d]
                    if p.is_dma:
                        s, v = lsem[p.lane], p.laneval
                    else:
                        if p.eng == e and e == "pe":
                            continue
                        s, v = esem[p.eng], p.sigval
                    if v > need.get(s, 0):
                        need[s] = v
                for s, v in need.items():
                    if waited.get(s, 0) < v:
                        engine.wait_ge(s, v)
                        waited[s] = v
                ins = o.fn(engine)
                if o.is_dma:
                    ins.then_inc(lsem[o.lane], 16)
                elif o.signal:
                    ins.then_inc(esem[e], 1)
            if e == "sp":
                for s, v in final_waits:
                    engine.wait_ge(s, v)

        block = stack.enter_context(nc.Block())
        if self.by_eng["pe"]:
            @block.tensor
            def _(eng):
                run("pe", eng)
        if self.by_eng["act"]:
            @block.scalar
            def _(eng):
                run("act", eng)
        if self.by_eng["dve"]:
            @block.vector
            def _(eng):
                run("dve", eng)
        if self.by_eng["pool"]:
            @block.gpsimd
            def _(eng):
                run("pool", eng)

        @block.sync
        def _(eng):
            run("sp", eng)


def _tiles(L, w):
    out = []
    t = 0
    while t < L:
        ww = min(w, L - t)
        out.append((t, ww))
        t += ww
    return out


def build_nc(SEQ, DEPTH, stop_after=None):
    T = SEQ + META
    L = ((T + 127) // 128) * 128
    NB = L // 128
    tiles = _tiles(L, 512)
    NT = len(tiles)
    tiles2 = _tiles(L, 256)

    nc = bass.Bass("TRN2", target_bir_lowering=False)

    def din(name, shape, dt=F32):
        return nc.dram_tensor(name, list(shape), dt, kind="ExternalInput").ap()

    def dscr(name, shape, dt):
        return nc.dram_tensor(name, list(shape), dt, kind="Internal").ap()

    res0 = din("res0", [8, 128, L])
    gv_d = din("gv", [128, 5 * DEPTH * 8])
    bfb_d = din("bfb", [128, DEPTH * 64])
    negm_d = din("negm", [128, 128], BF16)
    identb_d = din("identb", [128, 128], BF16)
    U_d = din("umat", [128, 128])
    sel_d = din("sel", [16, 16 * 128])
    invc_d = din("invc", [128, 64])
    cneg_d = din("cneg", [3, L], BF16)
    cpos_d = din("cpos", [3, L], BF16)
    cv_d = din("cv", [128, NB, 64], BF16)
    w_in_d = din("w_in", [DEPTH, D, NIN])
    w_pool_d = din("w_pool", [DEPTH, 4, 256, 256])
    w_out_d = din("w_out", [DEPTH, D, D])
    w_g_d = din("w_ffn_gate", [DEPTH, D, DFF])
    w_u_d = din("w_ffn_up", [DEPTH, D, DFF])
    w_d_d = din("w_ffn_down", [DEPTH, DFF, D])
    outT = nc.dram_tensor("outT", [8, 128, SEQ], F32, kind="ExternalOutput").ap()

    res = dscr("res", [8, 128, L], F32)
    qT = dscr("qT", [NH, KA, L], BF16)
    kT = dscr("kT", [NH, KA, L], BF16)
    vA = dscr("vA", [NH, 128, NB, 128], BF16)
    sga = dscr("sga", [NH, 64, L], BF16)
    pmT = dscr("pmT", [8, 128, L], BF16)
    amT = dscr("amT", [NH, 64, L], BF16)
    win_b = dscr("win_b", [DEPTH, D, NIN], BF16)
    wpool_b = dscr("wpool_b", [DEPTH, 4, 256, 256], BF16)
    wout_b = dscr("wout_b", [DEPTH, D, D], BF16)
    wg_b = dscr("wg_b", [DEPTH, D, DFF], BF16)
    wu_b = dscr("wu_b", [DEPTH, D, DFF], BF16)
    wd_b = dscr("wd_b", [DEPTH, DFF, D], BF16)

    def gcol(kind, l, c):
        return (kind * DEPTH + l) * 8 + c

    with ExitStack() as st:
        S = Sched(nc)

        uniq = [0]

        def sb(ctx, name, shape, dt):
            uniq[0] += 1
            return ctx.enter_context(nc.sbuf_tensor("%s_%d" % (name, uniq[0]), list(shape), dt))

        gv = sb(st, "gv_sb", [128, 5 * DEPTH * 8], F32)
        bfb = sb(st, "bfb_sb", [128, DEPTH * 64], F32)
        negm = sb(st, "negm_sb", [128, 128], BF16)
        identb = sb(st, "identb_sb", [128, 128], BF16)
        umat = sb(st, "umat_sb", [128, 128], F32)
        invc = sb(st, "invc_sb", [128, 64], F32)
        cst = sb(st, "cst_sb", [128, 4], F32)
        onesb = sb(st, "onesb_sb", [128, 128], BF16)
        onesf = sb(st, "onesf_sb", [128, 1], F32)
        LF = sb(st, "LF_sb", [128, NB, 16], F32)
        RB = sb(st, "RB_sb", [128, NH, NT + NB], F32)
        uh = sb(st, "uh_sb", [128, 8, 16], F32)
        ps = [st.enter_context(nc.psum_tensor("ps%d" % i, [128, 512], F32)) for i in range(8)]
        PS = ["ps%d" % i for i in range(8)]

        S.dma("sp", gv[:], gv_d, writes=["gv"])
        S.dma("sp", bfb[:], bfb_d, writes=["bfb"])
        S.dma("sp", negm[:], negm_d, writes=["negm"])
        S.dma("sp", identb[:], identb_d, writes=["identb"])
        S.dma("sp", umat[:], U_d, writes=["umat"])
        S.dma("sp", invc[:], invc_d, writes=["invc"])
        S.op("dve", lambda e: e.memset(cst[:, 0:1], EPS), writes=["cst0"])
        S.op("dve", lambda e: e.memset(cst[:, 1:2], 1.0), writes=["cst1"])
        S.op("dve", lambda e: e.memset(onesb[:], 1.0 / 1024.0), writes=["onesb"])
        S.op("dve", lambda e: e.memset(onesf[:], 1.0), writes=["onesf"])

        def const_rows():
            for h in range(NH):
                S.dma("pool", qT[h, 67:70, :], cneg_d, writes=[("qc", h)])
                S.dma("pool", kT[h, 64:67, :], cpos_d, writes=[("kc", h)])
                S.dma("pool", vA[h, :, :, 64:128], cv_d, writes=[("vc", h)])
        const_rows()

        def convert_layer(l):
            for r0 in range(0, D, 128):
                S.dma("pool", win_b[l, r0:r0 + 128, :], w_in_d[l, r0:r0 + 128, :], writes=[("win_b", l)])
            S.dma("pool", wpool_b[l].rearrange("g a b -> (g a) b"),
                  w_pool_d[l].rearrange("g a b -> (g a) b"), writes=[("wpool_b", l)])
            for r0 in range(0, D, 256):
                S.dma("pool", wout_b[l, r0:r0 + 256, :], w_out_d[l, r0:r0 + 256, :], writes=[("wout_b", l)])
            for r0 in range(0, D, 128):
                S.dma("pool", wg_b[l, r0:r0 + 128, :], w_g_d[l, r0:r0 + 128, :], writes=[("wg_b", l)])
                S.dma("pool", wu_b[l, r0:r0 + 128, :], w_u_d[l, r0:r0 + 128, :], writes=[("wu_b", l)])
            for r0 in range(0, DFF, 256):
                S.dma("pool", wd_b[l, r0:r0 + 256, :], w_d_d[l, r0:r0 + 256, :], writes=[("wd_b", l)])

        convert_layer(0)

        def rtok(t0, W):
            return [("res", k) for k in range(t0 // 256, (t0 + W + 255) // 256)]

        rot = {"i": 0}

        def next_ps(choices):
            i = choices[rot["i"] % len(choices)]
            rot["i"] += 1
            return i

        def rms_stats(src_sq, W, ps_i, lnv, rstd, sqtok):
            def f(e):
                ins = None
                for c in range(8):
                    ins = e.matmul(ps[ps_i][:, 0:W], onesb[:, :], src_sq[:, c, 0:W], start=(c == 0), stop=(c == 7))
                return ins
            S.op("pe", f, reads=["onesb", sqtok], writes=[PS[ps_i]])
            S.op("act", lambda e: e.activation(lnv[:, 0:W], ps[ps_i][:, 0:W], AF.Ln, bias=cst[:, 0:1]),
                 reads=[PS[ps_i], "cst0"], writes=["lnv"])
            S.op("act", lambda e: e.activation(rstd[:, 0:W], lnv[:, 0:W], AF.Exp, scale=-0.5),
                 reads=["lnv"], writes=["rstd"])

        def phase1(l):
            src = res0 if l == 0 else res
            with ExitStack() as ph:
                win = sb(ph, "win", [128, 8, NIN], BF16)
                wpl = sb(ph, "wpl", [128, 4, 2, 256], BF16)
                xt = sb(ph, "p1_xt", [128, 8, 512], F32)
                hn = sb(ph, "p1_hn", [128, 8, 512], BF16)
                lnv = sb(ph, "p1_lnv", [128, 512], F32)
                ug = [sb(ph, "p1_ug%d" % i, [128, 2, 528], F32) for i in range(4)]
                wa = sb(ph, "p1_wa", [128, 2, 528], F32)
                wb = sb(ph, "p1_wb", [128, 2, 528], F32)
                dsb = [sb(ph, "p1_dsb%d" % i, [128, 2, 512], BF16) for i in range(4)]
                sgp = [sb(ph, "p1_sgp%d" % i, [128, 2, 512], BF16) for i in range(4)]
                pm = sb(ph, "p1_pm", [128, 8, 512], BF16)
                stg = [sb(ph, "p1_stg%d" % i, [128, 4, 512], BF16) for i in range(2)]
                vsb = [sb(ph, "p1_vsb%d" % i, [128, 16, 64], BF16) for i in range(2)]
                tf = sb(ph, "p1_tf", [128, 64], F32)
                rstd = lnv
                ROT = [1, 2, 3, 4, 5, 6]

                for kc in range(8):
                    S.dma("sp", win[:, kc, :], win_b[l, kc * 128:(kc + 1) * 128, :],
                          reads=[("win_b", l)], writes=["win"])
                S.dma("sp", wpl[:], wpool_b[l].rearrange("g (k p) n -> p g k n", p=128),
                      reads=[("wpool_b", l)], writes=["wpl"])
                S.op("dve", lambda e: e.memset(uh[:], 0.0), writes=["uh"])
                stg_i = [0]
                vcnt = [0]

                for ti, (t0, W) in enumerate(tiles):
                    nsb = W // 128
                    S.dma("sp", xt[:, :, 0:W], src[:, :, t0:t0 + W].rearrange("c p t -> p c t"),
                          reads=rtok(t0, W), writes=["xt"])
                    S.op("dve", lambda e, W=W: e.tensor_tensor(hn[:, :, 0:W], xt[:, :, 0:W], xt[:, :, 0:W], ALU.mult),
                         reads=["xt"], writes=["hn"])

                    def fst(e, W=W):
                        ins = None
                        for c in range(8):
                            ins = e.matmul(ps[0][:, 0:W], onesb[:, :], hn[:, c, 0:W], start=(c == 0), stop=(c == 7))
                        return ins
                    S.op("pe", fst, reads=["onesb", "hn"], writes=[PS[0]])
                    S.op("act", lambda e, W=W: e.activation(lnv[:, 0:W], ps[0][:, 0:W], AF.Ln, bias=cst[:, 0:1]),
                         reads=[PS[0], "cst0"], writes=["lnv"])
                    S.op("act", lambda e, W=W: e.activation(lnv[:, 0:W], lnv[:, 0:W], AF.Exp, scale=-0.5),
                         reads=["lnv"], writes=["lnv"])
                    for c in range(8):
                        S.op("dve", lambda e, c=c, W=W: e.scalar_tensor_tensor(
                            hn[:, c, 0:W], xt[:, c, 0:W], gv[:, gcol(0, l, c):gcol(0, l, c) + 1], rstd[:, 0:W],
                            ALU.mult, ALU.mult), reads=["xt", "lnv", "gv", PS[0]], writes=["hn"])

                    def proj(col0, ps_i, W=W):
                        def f(e):
                            ins = None
                            for kc in range(8):
                                ins = e.matmul(ps[ps_i][:, 0:W], win[:, kc, col0:col0 + 128], hn[:, kc, 0:W],
                                               start=(kc == 0), stop=(kc == 7))
                            return ins
                        S.op("pe", f, reads=["win", "hn"], writes=[PS[ps_i]])

                    for g in range(4):
                        u = ug[g]
                        ut = "ug%d" % g
                        S.op("dve", lambda e, u=u, g=g: e.tensor_copy(u[:, :, 0:16], uh[:, 2 * g:2 * g + 2, :]),
                             reads=["uh"], writes=[ut])
                        for k in range(2):
                            pi = next_ps(ROT)
                            proj(C_POOL + (2 * g + k) * 128, pi)
                            S.op("act", lambda e, u=u, k=k, pi=pi, W=W: e.activation(
                                u[:, k, 16:16 + W], ps[pi][:, 0:W], AF.Copy), reads=[PS[pi]], writes=[ut])
                        S.op("dve", lambda e, u=u, g=g, W=W: e.tensor_copy(uh[:, 2 * g:2 * g + 2, :], u[:, :, W:W + 16]),
                             reads=[ut], writes=["uh"])
                    for g in range(4):
                        for k in range(2):
                            pi = next_ps(ROT)
                            proj(C_GP + (2 * g + k) * 128, pi)
                            S.op("act", lambda e, g=g, k=k, pi=pi, W=W: e.activation(
                                sgp[g][:, k, 0:W], ps[pi][:, 0:W], AF.Sigmoid), reads=[PS[pi]], writes=["sgp%d" % g])

                    for g in range(4):
                        w = 2 << g
                        u = ug[g]
                        ut = "ug%d" % g
                        cur, curt = u, ut
                        lo = 0
                        sh = 1
                        bufs = [(wa, "wa"), (wb, "wb")]
                        bi = 0
                        while sh < w:
                            dst, dstt = bufs[bi]
                            bi ^= 1
                            lo2 = lo + sh
                            S.op("dve", lambda e, dst=dst, cur=cur, lo2=lo2, sh=sh, W=W: e.tensor_tensor(
                                dst[:, :, lo2:16 + W], cur[:, :, lo2:16 + W], cur[:, :, lo2 - sh:16 + W - sh], ALU.add),
                                reads=[curt], writes=[dstt])
                            cur, curt = dst, dstt
                            lo = lo2
                            sh *= 2
                        d_ = dsb[g]
                        dt_ = "dsb%d" % g
                        S.op("dve", lambda e, d_=d_, cur=cur, u=u, w=w, W=W: e.scalar_tensor_tensor(
                            d_[:, :, 0:W], cur[:, :, 16:16 + W], 1.0 / w, u[:, :, 16:16 + W], ALU.mult, ALU.subtract),
                            reads=[curt, ut], writes=[dt_])
                        if ti == 0:
                            for k in range(2):
                                S.op("dve", lambda e, cur=cur, g=g, k=k: e.tensor_tensor(
                                    cur[:, k, 16:32], cur[:, k, 16:32], invc[:, g * 16:(g + 1) * 16], ALU.mult),
                                    reads=[curt, "invc", dt_], writes=[curt])
                            S.op("dve", lambda e, d_=d_, cur=cur, u=u: e.tensor_tensor(
                                d_[:, :, 0:16], cur[:, :, 16:32], u[:, :, 16:32], ALU.subtract),
                                reads=[curt, ut], writes=[dt_])

                    def fm_group(col0, dst, kind, tokname, W=W, t0=t0, ti=ti):
                        for half in range(2):
                            sgb = stg[stg_i[0] % 2]
                            sgtok = "stg%d" % (stg_i[0] % 2)
                            stg_i[0] += 1
                            for k in range(4):
                                c = half * 4 + k
                                pi = next_ps(ROT)
                                proj(col0 + c * 128, pi)
                                if kind == "q":
                                    S.op("act", lambda e, sgb=sgb, k=k, pi=pi: e.activation(
                                        sgb[:, k, 0:W], ps[pi][:, 0:W], AF.Copy, scale=0.125),
                                        reads=[PS[pi]], writes=[sgtok])
                                elif kind == "k":
                                    S.op("act", lambda e, sgb=sgb, k=k, pi=pi: e.activation(
                                        sgb[:, k, 0:W], ps[pi][:, 0:W], AF.Copy), reads=[PS[pi]], writes=[sgtok])
                                else:
                                    S.op("act", lambda e, sgb=sgb, k=k, pi=pi: e.activation(
                                        sgb[:, k, 0:W], ps[pi][:, 0:W], AF.Sigmoid), reads=[PS[pi]], writes=[sgtok])
                            h0 = half * 8
                            for par in range(2):
                                dview = dst[h0 + par:h0 + 8:2, 0:64, t0:t0 + W].rearrange("c r t -> r c t")
                                S.dma("pool", dview, sgb[par * 64:(par + 1) * 64, :, 0:W],
                                      reads=[sgtok], writes=[(tokname, ti)])
                    fm_group(C_GA, sga, "g", "sga")
                    fm_group(C_Q, qT, "q", "qT")
                    fm_group(C_K, kT, "k", "kT")

                    b0 = t0 // 128
                    for s_ in range(nsb):
                        vb = vsb[vcnt[0] % 2]
                        vtok = "vsb%d" % (vcnt[0] % 2)
                        vcnt[0] += 1
                        for half in range(2):
                            pi = next_ps(ROT)

                            def f(e, s_=s_, half=half, pi=pi):
                                ins = None
                                for kc in range(8):
                                    ins = e.matmul(ps[pi][:, :], hn[:, kc, s_ * 128:(s_ + 1) * 128],
                                                   win[:, kc, C_V + half * 512:C_V + (half + 1) * 512],
                                                   start=(kc == 0), stop=(kc == 7))
                                return ins
                            S.op("pe", f, reads=["win", "hn"], writes=[PS[pi]])
                            S.op("act", lambda e, vb=vb, half=half, pi=pi: e.activation(
                                vb[:, half * 8:(half + 1) * 8, :],
                                ps[pi][:, :].rearrange("p (h d) -> p h d", d=64), AF.Copy),
                                reads=[PS[pi]], writes=[vtok])

                        def ff_(e, s_=s_):
                            ins = None
                            for kc in range(8):
                                ins = e.matmul(ps[7][:, s_ * 16:(s_ + 1) * 16], hn[:, kc, s_ * 128:(s_ + 1) * 128],
                                               win[:, kc, C_F:C_F + 16], start=(kc == 0), stop=(kc == 7))
                            return ins
                        S.op("pe", ff_, reads=["win", "hn"], writes=[PS[7]])
                        S.dma("pool", vA[:, :, b0 + s_, 0:64].rearrange("h p d -> p h d"),
                              vb[:, :, :], reads=[vtok], writes=[("vA", ti)])

                    for g in range(4):
                        d_ = dsb[g]
                        for o2 in range(2):
                            pi = next_ps(ROT)

                            def f(e, g=g, o2=o2, pi=pi, d_=d_, W=W):
                                ins = None
                                for k2 in range(2):
                                    ins = e.matmul(ps[pi][:, 0:W], wpl[:, g, k2, o2 * 128:(o2 + 1) * 128],
                                                   d_[:, k2, 0:W], start=(k2 == 0), stop=(k2 == 1))
                                return ins
                            S.op("pe", f, reads=["wpl", "dsb%d" % g], writes=[PS[pi]])
                            c = 2 * g + o2
                            S.op("dve", lambda e, c=c, pi=pi, g=g, o2=o2, W=W: e.scalar_tensor_tensor(
                                pm[:, c, 0:W], ps[pi][:, 0:W], gv[:, gcol(4, l, c):gcol(4, l, c) + 1],
                                sgp[g][:, o2, 0:W], ALU.mult, ALU.mult),
                                reads=[PS[pi], "sgp%d" % g, "gv"], writes=["pm"])
                    S.dma("pool", pmT[:, :, t0:t0 + W].rearrange("c p t -> p c t"), pm[:, :, 0:W],
                          reads=["pm"], writes=[("pmT", ti)])

                    S.op("dve", lambda e, nsb=nsb: e.tensor_tensor(
                        tf[:, 0:nsb * 16], ps[7][:, 0:nsb * 16], bfb[:, l * 64:l * 64 + nsb * 16], ALU.add),
                        reads=[PS[7], "bfb"], writes=["tf"])
                    S.op("act", lambda e, nsb=nsb: e.activation(tf[:, 0:nsb * 16], tf[:, 0:nsb * 16], AF.Exp, scale=-1.0),
                         reads=["tf"], writes=["tf"])
                    S.op("act", lambda e, nsb=nsb: e.activation(tf[:, 0:nsb * 16], tf[:, 0:nsb * 16], AF.Ln, bias=cst[:, 1:2]),
                         reads=["tf", "cst1"], writes=["tf"])
                    S.op("dve", lambda e, nsb=nsb, b0=b0: e.tensor_scalar(
                        LF[:, b0:b0 + nsb, :], tf[:, 0:nsb * 16].rearrange("p (b h) -> p b h", h=16), -1.0, None, ALU.mult),
                        reads=["tf"], writes=["LF"])
            S.barrier()

        def phaseF(l):
            with ExitStack() as ph:
                FT = sb(ph, "f_FT", [16, L], F32)
                X = sb(ph, "f_X", [16, L], F32)
                R1 = sb(ph, "f_R1", [16, L], F32)
                A = sb(ph, "f_A", [16, 3, L], BF16)
                tot = sb(ph, "f_tot", [16, NB + 1], F32)
                car = sb(ph, "f_car", [16, NB + 1], F32)
                RS = sb(ph, "f_RS", [16, NT + NB], F32)
                sel = sb(ph, "f_sel", [16, 16 * 128], F32)
                S.dma("sp", sel[:], sel_d, writes=["sel"])
                for b in range(NB):
                    S.op("pe", lambda e, b=b: e.matmul(ps[6][0:16, b:b + 1], LF[:, b, :], onesf[:, 0:1],
                                                       start=True, stop=True),
                         reads=["LF", "onesf"], writes=[PS[6]])
                S.op("dve", lambda e: e.tensor_copy(tot[:, 0:NB], ps[6][0:16, 0:NB]), reads=[PS[6]], writes=["tot"])
                S.op("dve", lambda e: e.memset(car[:, 0:1], 0.0), writes=["car"])
                for b in range(1, NB):
                    S.op("dve", lambda e, b=b: e.tensor_tensor(car[:, b:b + 1], car[:, b - 1:b], tot[:, b - 1:b], ALU.add),
                         reads=["car", "tot"], writes=["car"])
                for b in range(NB):
                    pi = b % 4
                    S.op("pe", lambda e, b=b, pi=pi: e.matmul(ps[pi][0:16, 0:128], LF[:, b, :], umat[:, :],
                                                              start=True, stop=True),
                         reads=["LF", "umat"], writes=[PS[pi]])
                    S.op("dve", lambda e, b=b, pi=pi: e.tensor_scalar(
                        FT[:, b * 128:(b + 1) * 128], ps[pi][0:16, 0:128], car[:, b:b + 1], None, ALU.add),
                        reads=[PS[pi], "car"], writes=["FT"])
                S.op("dve", lambda e: e.tensor_copy(RS[:, 0:NT], FT[:, 0:L:512]), reads=["FT"], writes=["RS"])
                S.op("dve", lambda e: e.tensor_copy(RS[:, NT:NT + NB], FT[:, 0:L:128]), reads=["FT"], writes=["RS"])
                for h in range(NH):
                    pi = 4 + (h % 2)
                    S.op("pe", lambda e, h=h, pi=pi: e.matmul(ps[pi][:, 0:NT + NB], sel[:, h * 128:(h + 1) * 128],
                                                              RS[:, :], start=True, stop=True),
                         reads=["sel", "RS"], writes=[PS[pi]])
                    S.op("dve", lambda e, h=h, pi=pi: e.tensor_copy(RB[:, h, 0:NT], ps[pi][:, 0:NT]),
                         reads=[PS[pi]], writes=["RB"])
                    S.op("dve", lambda e, h=h, pi=pi: e.tensor_scalar(RB[:, h, NT:NT + NB], ps[pi][:, NT:NT + NB],
                                                                      -1.0, None, ALU.mult),
                         reads=[PS[pi]], writes=["RB"])

                def split3(dst_dram, row0, tokname):
                    S.op("dve", lambda e: e.tensor_copy(A[:, 0, :], X[:, :]), reads=["X"], writes=["A"])
                    S.op("dve", lambda e: e.tensor_tensor(R1[:, :], X[:, :], A[:, 0, :], ALU.subtract),
                         reads=["X", "A"], writes=["R1"])
                    S.op("dve", lambda e: e.tensor_copy(A[:, 1, :], R1[:, :]), reads=["R1"], writes=["A"])
                    S.op("dve", lambda e: e.tensor_tensor(R1[:, :], R1[:, :], A[:, 1, :], ALU.subtract),
                         reads=["R1", "A"], writes=["R1"])
                    S.op("dve", lambda e: e.tensor_copy(A[:, 2, :], R1[:, :]), reads=["R1"], writes=["A"])
                    S.dma("pool", dst_dram[:, row0:row0 + 3, :], A[:, :, :], reads=["A"], writes=[tokname])

                for ti, (t0, W) in enumerate(tiles):
                    S.op("dve", lambda e, ti=ti, t0=t0, W=W: e.tensor_scalar(
                        X[:, t0:t0 + W], FT[:, t0:t0 + W], RS[:, ti:ti + 1], None, ALU.subtract),
                        reads=["FT", "RS", "A"], writes=["X"])
                split3(qT, 64, "qTa")
                for b in range(NB):
                    S.op("dve", lambda e, b=b: e.tensor_scalar(
                        X[:, b * 128:(b + 1) * 128], FT[:, b * 128:(b + 1) * 128], RS[:, NT + b:NT + b + 1], None,
                        ALU.subtract), reads=["FT", "RS", "A"], writes=["X"])
                split3(kT, 67, "kTa")
            S.barrier()

        def phase2(l):
            with ExitStack() as ph:
                Kh = [sb(ph, "a_K%d" % i, [KA, L], BF16) for i in range(2)]
                Qh = [sb(ph, "a_Q%d" % i, [KA, L], BF16) for i in range(2)]
                Vh = [sb(ph, "a_V%d" % i, [128, NB, 128], BF16) for i in range(2)]
                Gh = [sb(ph, "a_G%d" % i, [64, L], BF16) for i in range(2)]
                Bc = [sb(ph, "a_B%d" % i, [128, NT, NB], F32) for i in range(2)]
                NPB = 6
                PT = [sb(ph, "a_P%d" % i, [128, 512], BF16) for i in range(NPB)]
                rb = sb(ph, "a_rb", [64, 512], F32)
                tt = sb(ph, "a_tt", [64, 512], F32)
                amo = [sb(ph, "a_am%d" % i, [64, 512], BF16) for i in range(2)]
                alltok = lambda name: [(name, ti) for ti in range(NT)]

                def load_head(h):
                    i = h % 2
                    S.dma("sp", Kh[i][:, :], kT[h, :, :], reads=alltok("kT") + ["kTa", ("kc", h)], writes=["Kh%d" % i])
                    S.dma("sp", Qh[i][:, :], qT[h, :, :], reads=alltok("qT") + ["qTa", ("qc", h)], writes=["Qh%d" % i])
                    S.dma("sp", Vh[i][:, :, :], vA[h, :, :, :], reads=alltok("vA") + [("vc", h)], writes=["Vh%d" % i])
                    S.dma("sp", Gh[i][:, :], sga[h, :, :], reads=alltok("sga"), writes=["Gh%d" % i])
                    for ti in range(NT):
                        S.op("dve", lambda e, i=i, h=h, ti=ti: e.tensor_scalar(
                            Bc[i][:, ti, :], RB[:, h, NT:NT + NB], RB[:, h, ti:ti + 1], None, ALU.add),
                            reads=["RB"], writes=["Bc%d" % i])

                LA = 3
                items = []
                for h in range(NH):
                    for ti, (t0, W) in enumerate(tiles):
                        jmax = (t0 + W) // 128 - 1
                        for j in range(jmax + 1):
                            items.append((h, ti, t0, W, j, jmax))

                def emit_front(n):
                    h, ti, t0, W, j, jmax = items[n]
                    i = h % 2
                    k0 = j * 128
                    q0 = max(t0, k0)
                    Wj = t0 + W - q0
                    si = n % 4
                    pb = n % NPB
                    diag = k0 >= t0

                    def fS(e, i=i, si=si, k0=k0, q0=q0, Wj=Wj, diag=diag):
                        if not diag:
                            return e.matmul(ps[si][:, 0:Wj], Kh[i][:, k0:k0 + 128], Qh[i][:, q0:q0 + Wj],
                                            start=True, stop=True)
                        e.matmul(ps[si][:, 0:128], identb[:, :], negm[:, :], start=True, stop=False)
                        ins = e.matmul(ps[si][:, 0:128], Kh[i][:, k0:k0 + 128], Qh[i][:, q0:q0 + 128],
                                       start=False, stop=True)
                        if Wj > 128:
                            ins = e.matmul(ps[si][:, 128:Wj], Kh[i][:, k0:k0 + 128], Qh[i][:, q0 + 128:q0 + Wj],
                                           start=True, stop=True)
                        return ins
                    S.op("pe", fS, reads=["Kh%d" % i, "Qh%d" % i, "identb", "negm"], writes=[PS[si]])
                    S.op("act", lambda e, i=i, si=si, pb=pb, ti=ti, j=j, Wj=Wj: e.activation(
                        PT[pb][:, 0:Wj], ps[si][:, 0:Wj], AF.Exp, bias=Bc[i][:, ti, j:j + 1]),
                        reads=[PS[si], "Bc%d" % i], writes=["PT%d" % pb])

                def emit_back(n):
                    h, ti, t0, W, j, jmax = items[n]
                    i = h % 2
                    if ti == 0 and j == 0 and h + 1 < NH:
                        load_head(h + 1)
                    k0 = j * 128
                    q0 = max(t0, k0)
                    c0 = q0 - t0
                    Wj = W - c0
                    pb = n % NPB
                    tcount = h * NT + ti
                    oi = 4 + (tcount % 2)
                    S.op("pe", lambda e, i=i, oi=oi, pb=pb, j=j, c0=c0, Wj=Wj, jmax=jmax: e.matmul(
                        ps[oi][:, c0:c0 + Wj], Vh[i][:, j, :], PT[pb][:, 0:Wj], start=(j == 0), stop=(j == jmax)),
                        reads=["Vh%d" % i, "PT%d" % pb], writes=[PS[oi]])
                    if j != jmax:
                        return
                    S.op("act", lambda e, oi=oi, W=W: e.activation(rb[:, 0:W], ps[oi][64:128, 0:W], AF.Copy),
                         reads=[PS[oi]], writes=["rb"])
                    S.op("dve", lambda e, W=W: e.reciprocal(rb[:, 0:W], rb[:, 0:W]), reads=["rb"], writes=["rb"])
                    S.op("dve", lambda e, oi=oi, W=W: e.tensor_tensor(tt[:, 0:W], ps[oi][0:64, 0:W], rb[:, 0:W], ALU.mult),
                         reads=[PS[oi], "rb"], writes=["tt"])
                    ai = tcount % 2
                    S.op("dve", lambda e, ai=ai, i=i, t0=t0, W=W: e.tensor_tensor(
                        amo[ai][:, 0:W], tt[:, 0:W], Gh[i][:, t0:t0 + W], ALU.mult),
                        reads=["tt", "Gh%d" % i], writes=["amo%d" % ai])
                    S.dma("pool", amT[h, :, t0:t0 + W], amo[ai][:, 0:W], reads=["amo%d" % ai], writes=[("amT", ti)])

                load_head(0)
                for n in range(len(items) + LA):
                    if n < len(items):
                        emit_front(n)
                    if n - LA >= 0:
                        emit_back(n - LA)
            S.barrier()

        def phase3a(l):
            with ExitStack() as ph:
                wop = sb(ph, "c_wop", [128, 8, D], BF16)
                woa = sb(ph, "c_woa", [64, NH, D], BF16)
                amB = [sb(ph, "c_am%d" % i, [64, NH, 512], BF16) for i in range(2)]
                pmB = [sb(ph, "c_pm%d" % i, [128, 8, 512], BF16) for i in range(2)]
                xtB = [sb(ph, "c_xt%d" % i, [128, 8, 512], F32) for i in range(2)]
                mo = sb(ph, "c_mo", [128, 8, 512], F32)
                sq = sb(ph, "c_sq", [128, 8, 512], BF16)
                lnv = sb(ph, "c_lnv", [128, 512], F32)
                rstd = sb(ph, "c_rstd", [128, 512], F32)
                xo = sb(ph, "c_xo", [128, 8, 512], F32)
                ROT = [1, 2, 3, 4]
                S.dma("sp", wop[:], wout_b[l].rearrange("(c p) n -> p c n", p=128), reads=[("wout_b", l)], writes=["wop"])
                S.dma("sp", woa[:], wout_b[l].rearrange("(h p) n -> p h n", p=64), reads=[("wout_b", l)], writes=["woa"])
                src = res0 if l == 0 else res
                for ti, (t0, W) in enumerate(tiles):
                    bsel = ti % 2
                    am, pm, xt = amB[bsel], pmB[bsel], xtB[bsel]
                    amt, pmt, xtt = "am%d" % bsel, "pm3%d" % bsel, "xt3%d" % bsel
                    S.dma("sp", am[:, :, 0:W], amT[:, :, t0:t0 + W].rearrange("h r t -> r h t"),
                          reads=[("amT", ti)], writes=[amt])
                    S.dma("sp", pm[:, :, 0:W], pmT[:, :, t0:t0 + W].rearrange("c p t -> p c t"),
                          reads=[("pmT", ti)], writes=[pmt])
                    S.dma("sp", xt[:, :, 0:W], src[:, :, t0:t0 + W].rearrange("c p t -> p c t"),
                          reads=rtok(t0, W), writes=[xtt])
                    for oc in range(8):
                        pi = next_ps(ROT)

                        def f(e, oc=oc, pi=pi, W=W, pm=pm, am=am):
                            ins = None
                            for c in range(8):
                                ins = e.matmul(ps[pi][:, 0:W], wop[:, c, oc * 128:(oc + 1) * 128], pm[:, c, 0:W],
                                               start=(c == 0), stop=False)
                            for h in range(NH):
                                ins = e.matmul(ps[pi][:, 0:W], woa[:, h, oc * 128:(oc + 1) * 128], am[:, h, 0:W],
                                               start=False, stop=(h == NH - 1))
                            return ins
                        S.op("pe", f, reads=["wop", "woa", pmt, amt], writes=[PS[pi]])
                        S.op("act", lambda e, oc=oc, pi=pi, W=W: e.activation(mo[:, oc, 0:W], ps[pi][:, 0:W], AF.Copy),
                             reads=[PS[pi]], writes=["mo"])
                    S.op("dve", lambda e, W=W: e.tensor_tensor(sq[:, :, 0:W], mo[:, :, 0:W], mo[:, :, 0:W], ALU.mult),
                         reads=["mo"], writes=["sq3"])
                    rms_stats(sq, W, 0, lnv, rstd, "sq3")
                    for c in range(8):
                        S.op("dve", lambda e, c=c, W=W: e.scalar_tensor_tensor(
                            mo[:, c, 0:W], mo[:, c, 0:W], gv[:, gcol(1, l, c):gcol(1, l, c) + 1], rstd[:, 0:W],
                            ALU.mult, ALU.mult), reads=["mo", "rstd", "gv"], writes=["mo"])
                    S.op("dve", lambda e, W=W, xt=xt: e.tensor_tensor(xo[:, :, 0:W], mo[:, :, 0:W], xt[:, :, 0:W], ALU.add),
                         reads=["mo", xtt], writes=["xo3"])
                    S.dma("pool", res[:, :, t0:t0 + W].rearrange("c p t -> p c t"), xo[:, :, 0:W],
                          reads=["xo3"], writes=rtok(t0, W))
            S.barrier()

        def phase3b(l, last):
            with ExitStack() as ph:
                wg = sb(ph, "d_wg", [128, 8, DFF], BF16)
                wu = sb(ph, "d_wu", [128, 8, DFF], BF16)
                wd = sb(ph, "d_wd", [128, NFC, D], BF16)
                xt = [sb(ph, "d_xt%d" % i, [128, 8, 256], F32) for i in range(2)]
                hn = [sb(ph, "d_hn%d" % i, [128, 8, 256], BF16) for i in range(2)]
                lnvA = sb(ph, "d_lnvA", [128, 256], F32)
                rstdA = sb(ph, "d_rstdA", [128, 256], F32)
                lnvB = sb(ph, "d_lnvB", [128, 256], F32)
                rstdB = sb(ph, "d_rstdB", [128, 256], F32)
                sg = [sb(ph, "d_sg%d" % i, [128, 256], F32) for i in range(2)]
                ff = sb(ph, "d_ff", [128, NFC, 256], BF16)
                fo = sb(ph, "d_fo", [128, 8, 256], F32)
                sq = sb(ph, "d_sq", [128, 8, 256], BF16)
                ROT = [1, 2, 3, 4, 5, 6]
                for kc in range(8):
                    S.dma("sp", wg[:, kc, :], wg_b[l, kc * 128:(kc + 1) * 128, :], reads=[("wg_b", l)], writes=["wg"])
                    S.dma("sp", wu[:, kc, :], wu_b[l, kc * 128:(kc + 1) * 128, :], reads=[("wu_b", l)], writes=["wu"])
                S.dma("sp", wd[:], wd_b[l].rearrange("(c p) n -> p c n", p=128), reads=[("wd_b", l)], writes=["wd"])

                def stats(src_sq, W, lnv, rstd, sqtok, sfx):
                    def f(e):
                        ins = None
                        for c in range(8):
                            ins = e.matmul(ps[0][:, 0:W], onesb[:, :], src_sq[:, c, 0:W], start=(c == 0), stop=(c == 7))
                        return ins
                    S.op("pe", f, reads=["onesb", sqtok], writes=[PS[0]])
                    S.op("act", lambda e: e.activation(lnv[:, 0:W], ps[0][:, 0:W], AF.Ln, bias=cst[:, 0:1]),
                         reads=[PS[0], "cst0"], writes=["lnv" + sfx])
                    S.op("act", lambda e: e.activation(rstd[:, 0:W], lnv[:, 0:W], AF.Exp, scale=-0.5),
                         reads=["lnv" + sfx], writes=["rstd" + sfx])

                def prologue(t2):
                    t0, W = tiles2[t2]
                    b = t2 % 2
                    S.dma("sp", xt[b][:, :, 0:W], res[:, :, t0:t0 + W].rearrange("c p t -> p c t"),
                          reads=rtok(t0, W), writes=["xt4%d" % b])
                    S.op("dve", lambda e: e.tensor_tensor(hn[b][:, :, 0:W], xt[b][:, :, 0:W], xt[b][:, :, 0:W], ALU.mult),
                         reads=["xt4%d" % b], writes=["hn4%d" % b])
                    stats(hn[b], W, lnvA, rstdA, "hn4%d" % b, "A")
                    for c in range(8):
                        S.op("dve", lambda e, c=c: e.scalar_tensor_tensor(
                            hn[b][:, c, 0:W], xt[b][:, c, 0:W], gv[:, gcol(2, l, c):gcol(2, l, c) + 1], rstdA[:, 0:W],
                            ALU.mult, ALU.mult), reads=["xt4%d" % b, "rstdA", "gv", PS[0]], writes=["hn4%d" % b])

                def gate_up(t2, fc0, fc1):
                    t0, W = tiles2[t2]
                    b = t2 % 2
                    for fc in range(fc0, fc1):
                        pg = next_ps(ROT)
                        pu = next_ps(ROT)

                        def fg(e, fc=fc, pg=pg):
                            ins = None
                            for kc in range(8):
                                ins = e.matmul(ps[pg][:, 0:W], wg[:, kc, fc * 128:(fc + 1) * 128], hn[b][:, kc, 0:W],
                                               start=(kc == 0), stop=(kc == 7))
                            return ins

                        def fu(e, fc=fc, pu=pu):
                            ins = None
                            for kc in range(8):
                                ins = e.matmul(ps[pu][:, 0:W], wu[:, kc, fc * 128:(fc + 1) * 128], hn[b][:, kc, 0:W],
                                               start=(kc == 0), stop=(kc == 7))
                            return ins
                        S.op("pe", fg, reads=["wg", "hn4%d" % b], writes=[PS[pg]])
                        S.op("pe", fu, reads=["wu", "hn4%d" % b], writes=[PS[pu]])
                        sgi = fc % 2
                        S.op("act", lambda e, sgi=sgi, pg=pg: e.activation(sg[sgi][:, 0:W], ps[pg][:, 0:W], AF.Silu),
                             reads=[PS[pg]], writes=["sg%d" % sgi])
                        S.op("dve", lambda e, fc=fc, sgi=sgi, pu=pu: e.tensor_tensor(
                            ff[:, fc, 0:W], sg[sgi][:, 0:W], ps[pu][:, 0:W], ALU.mult),
                            reads=["sg%d" % sgi, PS[pu]], writes=["ff"])

                def down(t2):
                    t0, W = tiles2[t2]
                    for oc in range(8):
                        pi = next_ps(ROT)

                        def fd(e, oc=oc, pi=pi):
                            ins = None
                            for fc in range(NFC):
                                ins = e.matmul(ps[pi][:, 0:W], wd[:, fc, oc * 128:(oc + 1) * 128], ff[:, fc, 0:W],
                                               start=(fc == 0), stop=(fc == NFC - 1))
                            return ins
                        S.op("pe", fd, reads=["wd", "ff"], writes=[PS[pi]])
                        S.op("act", lambda e, oc=oc, pi=pi: e.activation(fo[:, oc, 0:W], ps[pi][:, 0:W], AF.Copy),
                             reads=[PS[pi]], writes=["fo"])

                EPI = "pool"

                def epiA(t2):
                    t0, W = tiles2[t2]
                    S.op(EPI, lambda e: e.tensor_tensor(sq[:, :, 0:W], fo[:, :, 0:W], fo[:, :, 0:W], ALU.mult),
                         reads=["fo"], writes=["sq4"])

                def epiB(t2):
                    t0, W = tiles2[t2]
                    stats(sq, W, lnvB, rstdB, "sq4", "B")

                def epiC(t2):
                    t0, W = tiles2[t2]
                    b = t2 % 2
                    for c in range(8):
                        S.op("dve", lambda e, c=c: e.scalar_tensor_tensor(
                            fo[:, c, 0:W], fo[:, c, 0:W], gv[:, gcol(3, l, c):gcol(3, l, c) + 1], rstdB[:, 0:W],
                            ALU.mult, ALU.mult), reads=["fo", "rstdB", "gv"], writes=["fo"])
                    S.op(EPI, lambda e: e.tensor_tensor(fo[:, :, 0:W], fo[:, :, 0:W], xt[b][:, :, 0:W], ALU.add),
                         reads=["fo", "xt4%d" % b], writes=["fo"])
                    if not last:
                        S.dma("pool", res[:, :, t0:t0 + W].rearrange("c p t -> p c t"), fo[:, :, 0:W],
                              reads=["fo"], writes=rtok(t0, W))
                    else:
                        a = max(t0, META)
                        bb = min(t0 + W, T)
                        if bb > a:
                            S.dma("pool", outT[:, :, a - META:bb - META].rearrange("c p t -> p c t"),
                                  fo[:, :, a - t0:bb - t0], reads=["fo"], writes=[("out", t2)])

                n2 = len(tiles2)
                prologue(0)
                for t2 in range(n2):
                    if t2 > 0:
                        epiA(t2 - 1)
                    gate_up(t2, 0, 4)
                    if t2 > 0:
                        epiB(t2 - 1)
                        epiC(t2 - 1)
                    gate_up(t2, 4, NFC)
                    if t2 + 1 < n2:
                        prologue(t2 + 1)
                    down(t2)
                epiA(n2 - 1)
                epiB(n2 - 1)
                epiC(n2 - 1)
            S.barrier()

        S.barrier()
        for l in range(DEPTH):
            phase1(l)
            if stop_after in ("p1", "%d:p1" % l):
                break
            if l + 1 < DEPTH:
                convert_layer(l + 1)
            phaseF(l)
            if stop_after in ("pf", "%d:pf" % l):
                break
            phase2(l)
            if stop_after in ("p2", "%d:p2" % l):
                break
            phase3a(l)
            if stop_after in ("p3a", "%d:p3a" % l):
                break
            phase3b(l, last=(l == DEPTH - 1))
        S.emit(st)
    return nc


def _host_consts(DEPTH):
    negm = np.where(np.arange(128)[:, None] > np.arange(128)[None, :], -30000.0, 0.0).astype(ml_dtypes.bfloat16)
    identb = np.eye(128, dtype=np.float32).astype(ml_dtypes.bfloat16)
    umat = (np.arange(128)[:, None] <= np.arange(128)[None, :]).astype(np.float32)
    sel = np.zeros((16, 16, 128), np.float32)
    for h in range(16):
        sel[h, h, :] = 1.0
    sel = sel.reshape(16, 16 * 128)
    invc = np.zeros((128, 4, 16), np.float32)
    for g in range(4):
        w = 2 << g
        invc[:, g, :] = 1.0 / np.minimum(np.arange(16) + 1, w).astype(np.float32)[None, :]
    return negm, identb, umat, sel, invc.reshape(128, 64)


def make_in_maps(inputs, SEQ, DEPTH, n_cores):
    T = SEQ + META
    L = ((T + 127) // 128) * 128
    x = np.asarray(inputs["x"], np.float32)
    B = x.shape[0]
    meta = np.asarray(inputs["meta_tokens"], np.float32)
    negm, identb, umat, sel, invc = _host_consts(DEPTH)
    kinds = ["norm_mix_pre", "norm_mix_post", "norm_ffn_pre", "norm_ffn_post", "pool_scale"]
    gv = np.stack([np.asarray(inputs[k], np.float32) for k in kinds], 0)
    gv = gv.reshape(5, DEPTH, 8, 128).transpose(3, 0, 1, 2).reshape(128, 5 * DEPTH * 8)
    bf = np.asarray(inputs["b_forget"], np.float32)
    bfb = np.broadcast_to(bf[None, :, None, :], (128, DEPTH, 4, 16)).reshape(128, DEPTH * 64)
    common = {
        "gv": np.ascontiguousarray(gv), "bfb": np.ascontiguousarray(bfb),
        "negm": negm, "identb": identb, "umat": umat, "sel": sel, "invc": invc,
        "cneg": np.full((3, L), -1.0, ml_dtypes.bfloat16), "cpos": np.full((3, L), 1.0, ml_dtypes.bfloat16),
        "cv": np.full((128, L // 128, 64), 1.0, ml_dtypes.bfloat16),
        "w_in": np.asarray(inputs["w_in"], np.float32), "w_pool": np.asarray(inputs["w_pool"], np.float32),
        "w_out": np.asarray(inputs["w_out"], np.float32), "w_ffn_gate": np.asarray(inputs["w_ffn_gate"], np.float32),
        "w_ffn_up": np.asarray(inputs["w_ffn_up"], np.float32), "w_ffn_down": np.asarray(inputs["w_ffn_down"], np.float32),
    }
    maps = []
    for c in range(n_cores):
        b = c % B
        r = np.zeros((L, D), np.float32)
        r[0:META] = meta
        r[META:T] = x[b]
        r0 = np.ascontiguousarray(r.T.reshape(8, 128, L))
        m = dict(common)
        m["res0"] = r0
        maps.append(m)
    return maps


_NC_CACHE = {}


def kernel(x, meta_tokens, norm_mix_pre, norm_mix_post, norm_ffn_pre, norm_ffn_post,
           w_in, b_forget, w_pool, pool_scale, w_out, w_ffn_gate, w_ffn_up, w_ffn_down):
    inputs = dict(x=x, meta_tokens=meta_tokens, norm_mix_pre=norm_mix_pre, norm_mix_post=norm_mix_post,
                  norm_ffn_pre=norm_ffn_pre, norm_ffn_post=norm_ffn_post, w_in=w_in, b_forget=b_forget,
                  w_pool=w_pool, pool_scale=pool_scale, w_out=w_out, w_ffn_gate=w_ffn_gate,
                  w_ffn_up=w_ffn_up, w_ffn_down=w_ffn_down)
    x = np.asarray(x)
    B, SEQ, _ = x.shape
    DEPTH = np.asarray(w_in).shape[0]
    key = (SEQ, DEPTH)
    if key not in _NC_CACHE:
        _NC_CACHE[key] = build_nc(SEQ, DEPTH)
    nc = _NC_CACHE[key]
    maps = make_in_maps(inputs, SEQ, DEPTH, N_CORES)
    res = run_bass_kernel_spmd(nc, maps, core_ids=list(range(N_CORES)))
    out = np.empty((B, SEQ, D), np.float32)
    for b in range(B):
        o = np.asarray(res.results[b]["outT"], np.float32)
        out[b] = o.reshape(D, SEQ).T
    return out
```

```python
import numpy as np
import ml_dtypes
from contextlib import ExitStack
import concourse.bass as bass
import concourse.mybir as mybir
from concourse.bass_utils import run_bass_kernel_spmd

F32 = mybir.dt.float32
BF16 = mybir.dt.bfloat16
ALU = mybir.AluOpType
AF = mybir.ActivationFunctionType

D = 1024
NH = 16
HD = 64
DFF = 2816
NFC = DFF // 128
NIN = 6160
META = 16
KA = 70
EPS = 1e-6
C_POOL, C_Q, C_K, C_V, C_F, C_GP, C_GA = 0, 1024, 2048, 3072, 4096, 4112, 5136
N_CORES = 8


class _Op:
    __slots__ = ("eng", "fn", "deps", "signal", "sigval", "is_dma", "lane", "laneval", "id")


class Sched:
    ENGS = ("pe", "act", "dve", "pool", "sp")

    def __init__(self, nc, n_lanes=8):
        self.nc = nc
        self.ops = []
        self.by_eng = {e: [] for e in self.ENGS}
        self.last_write = {}
        self.readers = {}
        self.n_lanes = n_lanes
        self.lane_next = {}
        self.lane_count = {}
        self.lane_last = {}
        self.pending_barrier = {}

    def _deps(self, eng, reads, writes):
        deps = set()
        for r in reads:
            w = self.last_write.get(r)
            if w is not None:
                deps.add(w)
        for t in writes:
            w = self.last_write.get(t)
            if w is not None:
                deps.add(w)
            rs = self.readers.get(t)
            if rs:
                deps.update(rs)
        pb = self.pending_barrier.pop(eng, None)
        if pb:
            deps.update(pb)
        return deps

    def _commit(self, oid, reads, writes):
        for r in reads:
            self.readers.setdefault(r, []).append(oid)
        for t in writes:
            self.last_write[t] = oid
            self.readers[t] = []

    def op(self, eng, fn, reads=(), writes=()):
        o = _Op()
        o.eng = eng
        o.fn = fn
        o.is_dma = False
        o.signal = False
        o.sigval = None
        o.id = len(self.ops)
        o.deps = self._deps(eng, reads, writes)
        self.ops.append(o)
        self.by_eng[eng].append(o)
        self._commit(o.id, reads, writes)
        return o.id

    def dma(self, queue, out, in_, reads=(), writes=()):
        o = _Op()
        o.eng = queue
        o.is_dma = True
        o.signal = True
        o.sigval = None
        o.id = len(self.ops)
        o.deps = self._deps(queue, reads, writes)
        lane = self.lane_next.get(queue, 0)
        self.lane_next[queue] = (lane + 1) % self.n_lanes
        key = (queue, lane)
        prev = self.lane_last.get(key)
        if prev is not None:
            o.deps.add(prev)
        self.lane_count[key] = self.lane_count.get(key, 0) + 1
        o.lane = key
        o.laneval = 16 * self.lane_count[key]
        self.lane_last[key] = o.id
        o.fn = lambda e, out=out, in_=in_: e.dma_start(out=out, in_=in_)
        self.ops.append(o)
        self.by_eng[queue].append(o)
        self._commit(o.id, reads, writes)
        return o.id

    def dma_custom(self, queue, fn, reads=(), writes=()):
        oid = self.dma(queue, None, None, reads=reads, writes=writes)
        self.ops[oid].fn = fn
        return oid

    def barrier(self):
        pend = set()
        for e in self.ENGS:
            if self.by_eng[e]:
                pend.add(self.by_eng[e][-1].id)
        for oid in self.lane_last.values():
            pend.add(oid)
        self.pending_barrier = {e: set(pend) for e in self.ENGS}

    def emit(self, stack):
        nc = self.nc
        ops = self.ops
        for o in ops:
            for d in o.deps:
                p = ops[d]
                if p.is_dma:
                    continue
                if p.eng != o.eng or p.eng != "pe":
                    p.signal = True
        for e in self.ENGS:
            lst = [o for o in self.by_eng[e] if not o.is_dma]
            if lst:
                lst[-1].signal = True
        for e in self.ENGS:
            c = 0
            for o in self.by_eng[e]:
                if o.is_dma:
                    continue
                if o.signal:
                    c += 1
                    o.sigval = c
        esem = {e: stack.enter_context(nc.semaphore("s_" + e)) for e in self.ENGS}
        lsem = {key: stack.enter_context(nc.semaphore("l_%s%d" % key)) for key in self.lane_count}
        final_waits = [(lsem[key], 16 * cnt) for key, cnt in self.lane_count.items()]
        for e in self.ENGS:
            lst = [o for o in self.by_eng[e] if not o.is_dma and o.signal]
            if lst:
                final_waits.append((esem[e], lst[-1].sigval))

        def run(e, engine):
            waited = {}
            for o in self.by_eng[e]:
                need = {}
                for d in o.deps:
                    p = ops[d]
                    if p.is_dma:
                        s, v = lsem[p.lane], p.laneval
                    else:
                        if p.eng == e and e == "pe":
                            continue
                        s, v = esem[p.eng], p.sigval
                    if v > need.get(s, 0):
                        need[s] = v
                for s, v in need.items():
                    if waited.get(s, 0) < v:
                        engine.wait_ge(s, v)
                        waited[s] = v
                ins = o.fn(engine)
                if o.is_dma:
                    ins.then_inc(lsem[o.lane], 16)
                elif o.signal:
                    ins.then_inc(esem[e], 1)
            if e == "sp":
                for s, v in final_waits:
                    engine.wait_ge(s, v)

        block = stack.enter_context(nc.Block())
        if self.by_eng["pe"]:
            @block.tensor
            def _(eng):
                run("pe", eng)
        if self.by_eng["act"]:
            @block.scalar
            def _(eng):
                run("act", eng)
        if self.by_eng["dve"]:
            @block.vector
            def _(eng):
                run("dve", eng)
        if self.by_eng["pool"]:
            @block.gpsimd
            def _(eng):
                run("pool", eng)

        @block.sync
        def _(eng):
            run("sp", eng)


def _tiles(L, w):
    out = []
    t = 0
    while t < L:
        ww = min(w, L - t)
        out.append((t, ww))
        t += ww
    return out


def build_nc(SEQ, DEPTH, stop_after=None):
    T = SEQ + META
    L = ((T + 127) // 128) * 128
    NB = L // 128
    tiles = _tiles(L, 512)
    NT = len(tiles)
    tiles2 = _tiles(L, 256)
    tilesA = [(0, 128)] + [(128 + t0, w) for (t0, w) in _tiles(L - 128, 1024)]
    NTA = len(tilesA)

    nc = bass.Bass("TRN2", target_bir_lowering=False)

    def din(name, shape, dt=F32):
        return nc.dram_tensor(name, list(shape), dt, kind="ExternalInput").ap()

    def dscr(name, shape, dt):
        return nc.dram_tensor(name, list(shape), dt, kind="Internal").ap()

    res0 = din("res0", [8, 128, L])
    gv_d = din("gv", [128, 5 * DEPTH * 8])
    bfb_d = din("bfb", [128, DEPTH * 64])
    negm_d = din("negm", [128, 128], BF16)
    identb_d = din("identb", [128, 128], BF16)
    U_d = din("umat", [128, 128])
    sel_d = din("sel", [16, 16 * 128])
    invc_d = din("invc", [128, 64])
    cneg_d = din("cneg", [3, L], BF16)
    cpos_d = din("cpos", [3, L], BF16)
    cv_d = din("cv", [128, NB, 64], BF16)
    w_in_d = din("w_in", [DEPTH, D, NIN])
    w_pool_d = din("w_pool", [DEPTH, 4, 256, 256])
    w_out_d = din("w_out", [DEPTH, D, D])
    w_g_d = din("w_ffn_gate", [DEPTH, D, DFF])
    w_u_d = din("w_ffn_up", [DEPTH, D, DFF])
    w_d_d = din("w_ffn_down", [DEPTH, DFF, D])
    outT = nc.dram_tensor("outT", [8, 128, SEQ], F32, kind="ExternalOutput").ap()

    res = dscr("res", [8, 128, L], F32)
    qT = dscr("qT", [NH, KA, L], BF16)
    kT = dscr("kT", [NH, KA, L], BF16)
    vA = dscr("vA", [NH, 128, NB, 128], BF16)
    sga = dscr("sga", [NH, 64, L], BF16)
    pmT = dscr("pmT", [8, 128, L], BF16)
    amT = dscr("amT", [NH, 64, L], BF16)
    win_b = dscr("win_b", [DEPTH, D, NIN], BF16)
    wpool_b = dscr("wpool_b", [DEPTH, 4, 256, 256], BF16)
    wout_b = dscr("wout_b", [DEPTH, D, D], BF16)
    wg_b = dscr("wg_b", [DEPTH, D, DFF], BF16)
    wu_b = dscr("wu_b", [DEPTH, D, DFF], BF16)
    wd_b = dscr("wd_b", [DEPTH, DFF, D], BF16)

    def gcol(kind, l, c):
        return (kind * DEPTH + l) * 8 + c

    with ExitStack() as st:
        S = Sched(nc)

        uniq = [0]

        def sb(ctx, name, shape, dt):
            uniq[0] += 1
            return ctx.enter_context(nc.sbuf_tensor("%s_%d" % (name, uniq[0]), list(shape), dt))

        gv = sb(st, "gv_sb", [128, 5 * DEPTH * 8], F32)
        bfb = sb(st, "bfb_sb", [128, DEPTH * 64], F32)
        negm = sb(st, "negm_sb", [128, 128], BF16)
        identb = sb(st, "identb_sb", [128, 128], BF16)
        umat = sb(st, "umat_sb", [128, 128], F32)
        invc = sb(st, "invc_sb", [128, 64], F32)
        cst = sb(st, "cst_sb", [128, 4], F32)
        onesb = sb(st, "onesb_sb", [128, 128], BF16)
        onesf = sb(st, "onesf_sb", [128, 1], F32)
        LF = sb(st, "LF_sb", [128, NB, 16], F32)
        RB = sb(st, "RB_sb", [128, NH, NTA + NB], F32)
        uh = sb(st, "uh_sb", [128, 8, 16], F32)
        psw = [st.enter_context(nc.psum_tensor("psw%d" % i, [128, 1024], F32)) for i in range(4)]
        ps = [psw[i // 2][:, (i % 2) * 512:(i % 2 + 1) * 512] for i in range(8)]
        PS = ["ps%d" % i for i in range(8)]

        S.dma("sp", gv[:], gv_d, writes=["gv"])
        S.dma("sp", bfb[:], bfb_d, writes=["bfb"])
        S.dma("sp", negm[:], negm_d, writes=["negm"])
        S.dma("sp", identb[:], identb_d, writes=["identb"])
        S.dma("sp", umat[:], U_d, writes=["umat"])
        S.dma("sp", invc[:], invc_d, writes=["invc"])
        S.op("dve", lambda e: e.memset(cst[:, 0:1], EPS), writes=["cst0"])
        S.op("dve", lambda e: e.memset(cst[:, 1:2], 1.0), writes=["cst1"])
        S.op("dve", lambda e: e.memset(onesb[:], 1.0 / 1024.0), writes=["onesb"])
        S.op("dve", lambda e: e.memset(onesf[:], 1.0), writes=["onesf"])

        def const_rows():
            for h in range(NH):
                S.dma("pool", qT[h, 67:70, :], cneg_d, writes=[("qc", h)])
                S.dma("pool", kT[h, 64:67, :], cpos_d, writes=[("kc", h)])
                S.dma("pool", vA[h, :, :, 64:128], cv_d, writes=[("vc", h)])
        const_rows()

        def convert_layer(l):
            for r0 in range(0, D, 128):
                S.dma("pool", win_b[l, r0:r0 + 128, :], w_in_d[l, r0:r0 + 128, :], writes=[("win_b", l)])
            S.dma("pool", wpool_b[l].rearrange("g a b -> (g a) b"),
                  w_pool_d[l].rearrange("g a b -> (g a) b"), writes=[("wpool_b", l)])
            for r0 in range(0, D, 256):
                S.dma("pool", wout_b[l, r0:r0 + 256, :], w_out_d[l, r0:r0 + 256, :], writes=[("wout_b", l)])
            for r0 in range(0, D, 128):
                S.dma("pool", wg_b[l, r0:r0 + 128, :], w_g_d[l, r0:r0 + 128, :], writes=[("wg_b", l)])
                S.dma("pool", wu_b[l, r0:r0 + 128, :], w_u_d[l, r0:r0 + 128, :], writes=[("wu_b", l)])
            for r0 in range(0, DFF, 256):
                S.dma("pool", wd_b[l, r0:r0 + 256, :], w_d_d[l, r0:r0 + 256, :], writes=[("wd_b", l)])

        convert_layer(0)

        def rtok(t0, W):
            return [("res", k) for k in range(t0 // 256, (t0 + W + 255) // 256)]

        rot = {"i": 0}

        def next_ps(choices):
            i = choices[rot["i"] % len(choices)]
            rot["i"] += 1
            return i

        def rms_stats(src_sq, W, ps_i, lnv, rstd, sqtok):
            def f(e):
                ins = None
                for c in range(8):
                    ins = e.matmul(ps[ps_i][:, 0:W], onesb[:, :], src_sq[:, c, 0:W], start=(c == 0), stop=(c == 7))
                return ins
            S.op("pe", f, reads=["onesb", sqtok], writes=[PS[ps_i]])
            S.op("act", lambda e: e.activation(lnv[:, 0:W], ps[ps_i][:, 0:W], AF.Ln, bias=cst[:, 0:1]),
                 reads=[PS[ps_i], "cst0"], writes=["lnv"])
            S.op("act", lambda e: e.activation(rstd[:, 0:W], lnv[:, 0:W], AF.Exp, scale=-0.5),
                 reads=["lnv"], writes=["rstd"])

        def phase1(l):
            src = res0 if l == 0 else res
            with ExitStack() as ph:
                win = sb(ph, "win", [128, 8, NIN], BF16)
                wpl = sb(ph, "wpl", [128, 4, 2, 256], BF16)
                xt = sb(ph, "p1_xt", [128, 8, 512], F32)
                hn = sb(ph, "p1_hn", [128, 8, 512], BF16)
                lnv = sb(ph, "p1_lnv", [128, 512], F32)
                ug = [sb(ph, "p1_ug%d" % i, [128, 2, 528], F32) for i in range(4)]
                wa = sb(ph, "p1_wa", [128, 2, 528], F32)
                wb = sb(ph, "p1_wb", [128, 2, 528], F32)
                dsb = [sb(ph, "p1_dsb%d" % i, [128, 2, 512], BF16) for i in range(4)]
                sgp = [sb(ph, "p1_sgp%d" % i, [128, 2, 512], BF16) for i in range(4)]
                pm = sb(ph, "p1_pm", [128, 8, 512], BF16)
                stg = [sb(ph, "p1_stg%d" % i, [128, 4, 512], BF16) for i in range(2)]
                vsb = [sb(ph, "p1_vsb%d" % i, [128, 16, 64], BF16) for i in range(2)]
                tf = sb(ph, "p1_tf", [128, 64], F32)
                rstd = lnv
                ROT = [1, 2, 3, 4, 5, 6]

                for kc in range(8):
                    S.dma("sp", win[:, kc, :], win_b[l, kc * 128:(kc + 1) * 128, :],
                          reads=[("win_b", l)], writes=["win"])
                S.dma("sp", wpl[:], wpool_b[l].rearrange("g (k p) n -> p g k n", p=128),
                      reads=[("wpool_b", l)], writes=["wpl"])
                S.op("dve", lambda e: e.memset(uh[:], 0.0), writes=["uh"])
                stg_i = [0]
                vcnt = [0]

                for ti, (t0, W) in enumerate(tiles):
                    nsb = W // 128
                    S.dma("sp", xt[:, :, 0:W], src[:, :, t0:t0 + W].rearrange("c p t -> p c t"),
                          reads=rtok(t0, W), writes=["xt"])
                    S.op("dve", lambda e, W=W: e.tensor_tensor(hn[:, :, 0:W], xt[:, :, 0:W], xt[:, :, 0:W], ALU.mult),
                         reads=["xt"], writes=["hn"])

                    def fst(e, W=W):
                        ins = None
                        for c in range(8):
                            ins = e.matmul(ps[0][:, 0:W], onesb[:, :], hn[:, c, 0:W], start=(c == 0), stop=(c == 7))
                        return ins
                    S.op("pe", fst, reads=["onesb", "hn"], writes=[PS[0]])
                    S.op("act", lambda e, W=W: e.activation(lnv[:, 0:W], ps[0][:, 0:W], AF.Ln, bias=cst[:, 0:1]),
                         reads=[PS[0], "cst0"], writes=["lnv"])
                    S.op("act", lambda e, W=W: e.activation(lnv[:, 0:W], lnv[:, 0:W], AF.Exp, scale=-0.5),
                         reads=["lnv"], writes=["lnv"])
                    for c in range(8):
                        S.op("dve", lambda e, c=c, W=W: e.scalar_tensor_tensor(
                            hn[:, c, 0:W], xt[:, c, 0:W], gv[:, gcol(0, l, c):gcol(0, l, c) + 1], rstd[:, 0:W],
                            ALU.mult, ALU.mult), reads=["xt", "lnv", "gv", PS[0]], writes=["hn"])

                    def proj(col0, ps_i, W=W):
                        def f(e):
                            ins = None
                            for kc in range(8):
                                ins = e.matmul(ps[ps_i][:, 0:W], win[:, kc, col0:col0 + 128], hn[:, kc, 0:W],
                                               start=(kc == 0), stop=(kc == 7))
                            return ins
                        S.op("pe", f, reads=["win", "hn"], writes=[PS[ps_i]])

                    for g in range(4):
                        u = ug[g]
                        ut = "ug%d" % g
                        S.op("dve", lambda e, u=u, g=g: e.tensor_copy(u[:, :, 0:16], uh[:, 2 * g:2 * g + 2, :]),
                             reads=["uh"], writes=[ut])
                        for k in range(2):
                            pi = next_ps(ROT)
                            proj(C_POOL + (2 * g + k) * 128, pi)
                            S.op("act", lambda e, u=u, k=k, pi=pi, W=W: e.activation(
                                u[:, k, 16:16 + W], ps[pi][:, 0:W], AF.Copy), reads=[PS[pi]], writes=[ut])
                        S.op("dve", lambda e, u=u, g=g, W=W: e.tensor_copy(uh[:, 2 * g:2 * g + 2, :], u[:, :, W:W + 16]),
                             reads=[ut], writes=["uh"])
                    for g in range(4):
                        for k in range(2):
                            pi = next_ps(ROT)
                            proj(C_GP + (2 * g + k) * 128, pi)
                            S.op("act", lambda e, g=g, k=k, pi=pi, W=W: e.activation(
                                sgp[g][:, k, 0:W], ps[pi][:, 0:W], AF.Sigmoid), reads=[PS[pi]], writes=["sgp%d" % g])

                    for g in range(4):
                        w = 2 << g
                        u = ug[g]
                        ut = "ug%d" % g
                        cur, curt = u, ut
                        lo = 0
                        sh = 1
                        bufs = [(wa, "wa"), (wb, "wb")]
                        bi = 0
                        while sh < w:
                            dst, dstt = bufs[bi]
                            bi ^= 1
                            lo2 = lo + sh
                            S.op("dve", lambda e, dst=dst, cur=cur, lo2=lo2, sh=sh, W=W: e.tensor_tensor(
                                dst[:, :, lo2:16 + W], cur[:, :, lo2:16 + W], cur[:, :, lo2 - sh:16 + W - sh], ALU.add),
                                reads=[curt], writes=[dstt])
                            cur, curt = dst, dstt
                            lo = lo2
                            sh *= 2
                        d_ = dsb[g]
                        dt_ = "dsb%d" % g
                        S.op("dve", lambda e, d_=d_, cur=cur, u=u, w=w, W=W: e.scalar_tensor_tensor(
                            d_[:, :, 0:W], cur[:, :, 16:16 + W], 1.0 / w, u[:, :, 16:16 + W], ALU.mult, ALU.subtract),
                            reads=[curt, ut], writes=[dt_])
                        if ti == 0:
                            for k in range(2):
                                S.op("dve", lambda e, cur=cur, g=g, k=k: e.tensor_tensor(
                                    cur[:, k, 16:32], cur[:, k, 16:32], invc[:, g * 16:(g + 1) * 16], ALU.mult),
                                    reads=[curt, "invc", dt_], writes=[curt])
                            S.op("dve", lambda e, d_=d_, cur=cur, u=u: e.tensor_tensor(
                                d_[:, :, 0:16], cur[:, :, 16:32], u[:, :, 16:32], ALU.subtract),
                                reads=[curt, ut], writes=[dt_])

                    def fm_group(col0, dst, kind, tokname, W=W, t0=t0, ti=ti):
                        for half in range(2):
                            sgb = stg[stg_i[0] % 2]
                            sgtok = "stg%d" % (stg_i[0] % 2)
                            stg_i[0] += 1
                            for k in range(4):
                                c = half * 4 + k
                                pi = next_ps(ROT)
                                proj(col0 + c * 128, pi)
                                if kind == "q":
                                    S.op("act", lambda e, sgb=sgb, k=k, pi=pi: e.activation(
                                        sgb[:, k, 0:W], ps[pi][:, 0:W], AF.Copy, scale=0.125),
                                        reads=[PS[pi]], writes=[sgtok])
                                elif kind == "k":
                                    S.op("act", lambda e, sgb=sgb, k=k, pi=pi: e.activation(
                                        sgb[:, k, 0:W], ps[pi][:, 0:W], AF.Copy), reads=[PS[pi]], writes=[sgtok])
                                else:
                                    S.op("act", lambda e, sgb=sgb, k=k, pi=pi: e.activation(
                                        sgb[:, k, 0:W], ps[pi][:, 0:W], AF.Sigmoid), reads=[PS[pi]], writes=[sgtok])
                            h0 = half * 8
                            for par in range(2):
                                dview = dst[h0 + par:h0 + 8:2, 0:64, t0:t0 + W].rearrange("c r t -> r c t")
                                S.dma("pool", dview, sgb[par * 64:(par + 1) * 64, :, 0:W],
                                      reads=[sgtok], writes=[(tokname, ti)])
                    fm_group(C_GA, sga, "g", "sga")
                    fm_group(C_Q, qT, "q", "qT")
                    fm_group(C_K, kT, "k", "kT")

                    b0 = t0 // 128
                    for s_ in range(nsb):
                        vb = vsb[vcnt[0] % 2]
                        vtok = "vsb%d" % (vcnt[0] % 2)
                        vcnt[0] += 1
                        for half in range(2):
                            pi = next_ps(ROT)

                            def f(e, s_=s_, half=half, pi=pi):
                                ins = None
                                for kc in range(8):
                                    ins = e.matmul(ps[pi][:, :], hn[:, kc, s_ * 128:(s_ + 1) * 128],
                                                   win[:, kc, C_V + half * 512:C_V + (half + 1) * 512],
                                                   start=(kc == 0), stop=(kc == 7))
                                return ins
                            S.op("pe", f, reads=["win", "hn"], writes=[PS[pi]])
                            S.op("act", lambda e, vb=vb, half=half, pi=pi: e.activation(
                                vb[:, half * 8:(half + 1) * 8, :],
                                ps[pi][:, :].rearrange("p (h d) -> p h d", d=64), AF.Copy),
                                reads=[PS[pi]], writes=[vtok])

                        def ff_(e, s_=s_):
                            ins = None
                            for kc in range(8):
                                ins = e.matmul(ps[7][:, s_ * 16:(s_ + 1) * 16], hn[:, kc, s_ * 128:(s_ + 1) * 128],
                                               win[:, kc, C_F:C_F + 16], start=(kc == 0), stop=(kc == 7))
                            return ins
                        S.op("pe", ff_, reads=["win", "hn"], writes=[PS[7]])
                        S.dma("pool", vA[:, :, b0 + s_, 0:64].rearrange("h p d -> p h d"),
                              vb[:, :, :], reads=[vtok], writes=[("vA", ti)])

                    for g in range(4):
                        d_ = dsb[g]
                        for o2 in range(2):
                            pi = next_ps(ROT)

                            def f(e, g=g, o2=o2, pi=pi, d_=d_, W=W):
                                ins = None
                                for k2 in range(2):
                                    ins = e.matmul(ps[pi][:, 0:W], wpl[:, g, k2, o2 * 128:(o2 + 1) * 128],
                                                   d_[:, k2, 0:W], start=(k2 == 0), stop=(k2 == 1))
                                return ins
                            S.op("pe", f, reads=["wpl", "dsb%d" % g], writes=[PS[pi]])
                            c = 2 * g + o2
                            S.op("dve", lambda e, c=c, pi=pi, g=g, o2=o2, W=W: e.scalar_tensor_tensor(
                                pm[:, c, 0:W], ps[pi][:, 0:W], gv[:, gcol(4, l, c):gcol(4, l, c) + 1],
                                sgp[g][:, o2, 0:W], ALU.mult, ALU.mult),
                                reads=[PS[pi], "sgp%d" % g, "gv"], writes=["pm"])
                    S.dma("pool", pmT[:, :, t0:t0 + W].rearrange("c p t -> p c t"), pm[:, :, 0:W],
                          reads=["pm"], writes=[("pmT", ti)])

                    S.op("dve", lambda e, nsb=nsb: e.tensor_tensor(
                        tf[:, 0:nsb * 16], ps[7][:, 0:nsb * 16], bfb[:, l * 64:l * 64 + nsb * 16], ALU.add),
                        reads=[PS[7], "bfb"], writes=["tf"])
                    S.op("act", lambda e, nsb=nsb: e.activation(tf[:, 0:nsb * 16], tf[:, 0:nsb * 16], AF.Exp, scale=-1.0),
                         reads=["tf"], writes=["tf"])
                    S.op("act", lambda e, nsb=nsb: e.activation(tf[:, 0:nsb * 16], tf[:, 0:nsb * 16], AF.Ln, bias=cst[:, 1:2]),
                         reads=["tf", "cst1"], writes=["tf"])
                    S.op("dve", lambda e, nsb=nsb, b0=b0: e.tensor_scalar(
                        LF[:, b0:b0 + nsb, :], tf[:, 0:nsb * 16].rearrange("p (b h) -> p b h", h=16), -1.0, None, ALU.mult),
                        reads=["tf"], writes=["LF"])
            S.barrier()

        def phaseF(l):
            with ExitStack() as ph:
                FT = sb(ph, "f_FT", [16, L], F32)
                X = sb(ph, "f_X", [16, L], F32)
                R1 = sb(ph, "f_R1", [16, L], F32)
                A = sb(ph, "f_A", [16, 3, L], BF16)
                tot = sb(ph, "f_tot", [16, NB + 1], F32)
                car = sb(ph, "f_car", [16, NB + 1], F32)
                RS = sb(ph, "f_RS", [16, NTA + NB], F32)
                sel = sb(ph, "f_sel", [16, 16 * 128], F32)
                S.dma("sp", sel[:], sel_d, writes=["sel"])
                for b in range(NB):
                    S.op("pe", lambda e, b=b: e.matmul(ps[6][0:16, b:b + 1], LF[:, b, :], onesf[:, 0:1],
                                                       start=True, stop=True),
                         reads=["LF", "onesf"], writes=[PS[6]])
                S.op("dve", lambda e: e.tensor_copy(tot[:, 0:NB], ps[6][0:16, 0:NB]), reads=[PS[6]], writes=["tot"])
                S.op("dve", lambda e: e.memset(car[:, 0:1], 0.0), writes=["car"])
                for b in range(1, NB):
                    S.op("dve", lambda e, b=b: e.tensor_tensor(car[:, b:b + 1], car[:, b - 1:b], tot[:, b - 1:b], ALU.add),
                         reads=["car", "tot"], writes=["car"])
                for b in range(NB):
                    pi = b % 4
                    S.op("pe", lambda e, b=b, pi=pi: e.matmul(ps[pi][0:16, 0:128], LF[:, b, :], umat[:, :],
                                                              start=True, stop=True),
                         reads=["LF", "umat"], writes=[PS[pi]])
                    S.op("dve", lambda e, b=b, pi=pi: e.tensor_scalar(
                        FT[:, b * 128:(b + 1) * 128], ps[pi][0:16, 0:128], car[:, b:b + 1], None, ALU.add),
                        reads=[PS[pi], "car"], writes=["FT"])
                S.op("dve", lambda e: e.tensor_copy(RS[:, 0:1], FT[:, 0:1]), reads=["FT"], writes=["RS"])
                S.op("dve", lambda e: e.tensor_copy(RS[:, 1:NTA], FT[:, 128:L:1024]), reads=["FT"], writes=["RS"])
                S.op("dve", lambda e: e.tensor_copy(RS[:, NTA:NTA + NB], FT[:, 0:L:128]), reads=["FT"], writes=["RS"])
                for h in range(NH):
                    pi = 4 + (h % 2)
                    S.op("pe", lambda e, h=h, pi=pi: e.matmul(ps[pi][:, 0:NTA + NB], sel[:, h * 128:(h + 1) * 128],
                                                              RS[:, :], start=True, stop=True),
                         reads=["sel", "RS"], writes=[PS[pi]])
                    S.op("dve", lambda e, h=h, pi=pi: e.tensor_copy(RB[:, h, 0:NTA], ps[pi][:, 0:NTA]),
                         reads=[PS[pi]], writes=["RB"])
                    S.op("dve", lambda e, h=h, pi=pi: e.tensor_scalar(RB[:, h, NTA:NTA + NB], ps[pi][:, NTA:NTA + NB],
                                                                      -1.0, None, ALU.mult),
                         reads=[PS[pi]], writes=["RB"])

                def split3(dst_dram, row0, tokname):
                    S.op("dve", lambda e: e.tensor_copy(A[:, 0, :], X[:, :]), reads=["X"], writes=["A"])
                    S.op("dve", lambda e: e.tensor_tensor(R1[:, :], X[:, :], A[:, 0, :], ALU.subtract),
                         reads=["X", "A"], writes=["R1"])
                    S.op("dve", lambda e: e.tensor_copy(A[:, 1, :], R1[:, :]), reads=["R1"], writes=["A"])
                    S.op("dve", lambda e: e.tensor_tensor(R1[:, :], R1[:, :], A[:, 1, :], ALU.subtract),
                         reads=["R1", "A"], writes=["R1"])
                    S.op("dve", lambda e: e.tensor_copy(A[:, 2, :], R1[:, :]), reads=["R1"], writes=["A"])
                    S.dma("pool", dst_dram[:, row0:row0 + 3, :], A[:, :, :], reads=["A"], writes=[tokname])

                for ia, (t0, W) in enumerate(tilesA):
                    S.op("dve", lambda e, ia=ia, t0=t0, W=W: e.tensor_scalar(
                        X[:, t0:t0 + W], FT[:, t0:t0 + W], RS[:, ia:ia + 1], None, ALU.subtract),
                        reads=["FT", "RS", "A"], writes=["X"])
                split3(qT, 64, "qTa")
                for b in range(NB):
                    S.op("dve", lambda e, b=b: e.tensor_scalar(
                        X[:, b * 128:(b + 1) * 128], FT[:, b * 128:(b + 1) * 128], RS[:, NTA + b:NTA + b + 1], None,
                        ALU.subtract), reads=["FT", "RS", "A"], writes=["X"])
                split3(kT, 67, "kTa")
            S.barrier()

        def phase2(l):
            with ExitStack() as ph:
                Kh = [sb(ph, "a_K%d" % i, [KA, L], BF16) for i in range(2)]
                Qh = [sb(ph, "a_Q%d" % i, [KA, L], BF16) for i in range(2)]
                Vh = [sb(ph, "a_V%d" % i, [128, NB, 128], BF16) for i in range(2)]
                Gh = [sb(ph, "a_G%d" % i, [64, L], BF16) for i in range(2)]
                Bc = [sb(ph, "a_B%d" % i, [128, NTA, NB], F32) for i in range(2)]
                NPB = 4
                PT = [sb(ph, "a_P%d" % i, [128, 1024], BF16) for i in range(NPB)]
                rb = sb(ph, "a_rb", [64, 1024], F32)
                tt = sb(ph, "a_tt", [64, 1024], F32)
                amo = [sb(ph, "a_am%d" % i, [64, 1024], BF16) for i in range(2)]
                alltok = lambda name: [(name, ti) for ti in range(NT)]

                def load_head(h):
                    i = h % 2
                    S.dma("sp", Kh[i][:, :], kT[h, :, :], reads=alltok("kT") + ["kTa", ("kc", h)], writes=["Kh%d" % i])
                    S.dma("sp", Qh[i][:, :], qT[h, :, :], reads=alltok("qT") + ["qTa", ("qc", h)], writes=["Qh%d" % i])
                    S.dma("sp", Vh[i][:, :, :], vA[h, :, :, :], reads=alltok("vA") + [("vc", h)], writes=["Vh%d" % i])
                    S.dma("sp", Gh[i][:, :], sga[h, :, :], reads=alltok("sga"), writes=["Gh%d" % i])
                    for ia in range(NTA):
                        S.op("dve", lambda e, i=i, h=h, ia=ia: e.tensor_scalar(
                            Bc[i][:, ia, :], RB[:, h, NTA:NTA + NB], RB[:, h, ia:ia + 1], None, ALU.add),
                            reads=["RB"], writes=["Bc%d" % i])

                def pieces(lo, hi):
                    out = []
                    if lo < 512:
                        out.append((lo, min(hi, 512)))
                    if hi > 512:
                        out.append((max(lo, 512), hi))
                    return out

                LA = 2
                items = []
                for h in range(NH):
                    for ia, (t0, W) in enumerate(tilesA):
                        jmax = (t0 + W) // 128 - 1
                        for j in range(jmax + 1):
                            items.append((h, ia, t0, W, j, jmax))

                def emit_front(n):
                    h, ia, t0, W, j, jmax = items[n]
                    i = h % 2
                    k0 = j * 128
                    q0 = max(t0, k0)
                    Wj = t0 + W - q0
                    sw = n % 2
                    pb = n % NPB
                    diag = k0 >= t0

                    def fS(e, i=i, sw=sw, k0=k0, q0=q0, Wj=Wj, diag=diag):
                        lo = 0
                        ins = None
                        if diag:
                            e.matmul(psw[sw][:, 0:128], identb[:, :], negm[:, :], start=True, stop=False)
                            ins = e.matmul(psw[sw][:, 0:128], Kh[i][:, k0:k0 + 128], Qh[i][:, q0:q0 + 128],
                                           start=False, stop=True)
                            lo = 128
                        if Wj > lo:
                            for (a_, b_) in pieces(lo, Wj):
                                ins = e.matmul(psw[sw][:, a_:b_], Kh[i][:, k0:k0 + 128], Qh[i][:, q0 + a_:q0 + b_],
                                               start=True, stop=True)
                        return ins
                    S.op("pe", fS, reads=["Kh%d" % i, "Qh%d" % i, "identb", "negm"], writes=[PS[2 * sw], PS[2 * sw + 1]])
                    S.op("act", lambda e, i=i, sw=sw, pb=pb, ia=ia, j=j, Wj=Wj: e.activation(
                        PT[pb][:, 0:Wj], psw[sw][:, 0:Wj], AF.Exp, bias=Bc[i][:, ia, j:j + 1]),
                        reads=[PS[2 * sw], PS[2 * sw + 1], "Bc%d" % i], writes=["PT%d" % pb])

                def emit_back(n):
                    h, ia, t0, W, j, jmax = items[n]
                    i = h % 2
                    if ia == 0 and j == 0 and h + 1 < NH:
                        load_head(h + 1)
                    k0 = j * 128
                    q0 = max(t0, k0)
                    c0 = q0 - t0
                    Wj = W - c0
                    pb = n % NPB
                    tcount = h * NTA + ia
                    ow = 2 + (tcount % 2)

                    def fO(e, i=i, ow=ow, pb=pb, j=j, c0=c0, Wj=Wj, jmax=jmax):
                        ins = None
                        for (a_, b_) in pieces(c0, c0 + Wj):
                            ins = e.matmul(psw[ow][:, a_:b_], Vh[i][:, j, :], PT[pb][:, a_ - c0:b_ - c0],
                                           start=(j == 0), stop=(j == jmax))
                        return ins
                    S.op("pe", fO, reads=["Vh%d" % i, "PT%d" % pb], writes=[PS[2 * ow], PS[2 * ow + 1]])
                    if j != jmax:
                        return
                    OT = [PS[2 * ow], PS[2 * ow + 1]]
                    S.op("act", lambda e, ow=ow, W=W: e.activation(rb[:, 0:W], psw[ow][64:128, 0:W], AF.Copy),
                         reads=OT, writes=["rb"])
                    S.op("dve", lambda e, W=W: e.reciprocal(rb[:, 0:W], rb[:, 0:W]), reads=["rb"], writes=["rb"])
                    S.op("dve", lambda e, ow=ow, W=W: e.tensor_tensor(tt[:, 0:W], psw[ow][0:64, 0:W], rb[:, 0:W], ALU.mult),
                         reads=OT + ["rb"], writes=["tt"])
                    ai = tcount % 2
                    S.op("dve", lambda e, ai=ai, i=i, t0=t0, W=W: e.tensor_tensor(
                        amo[ai][:, 0:W], tt[:, 0:W], Gh[i][:, t0:t0 + W], ALU.mult),
                        reads=["tt", "Gh%d" % i], writes=["amo%d" % ai])
                    S.dma("pool", amT[h, :, t0:t0 + W], amo[ai][:, 0:W], reads=["amo%d" % ai],
                          writes=[("amT", k) for k in range(t0 // 512, (t0 + W + 511) // 512)])

                load_head(0)
                for n in range(len(items) + LA):
                    if n < len(items):
                        emit_front(n)
                    if n - LA >= 0:
                        emit_back(n - LA)
            S.barrier()

        def phase3a(l):
            with ExitStack() as ph:
                wop = sb(ph, "c_wop", [128, 8, D], BF16)
                woa = sb(ph, "c_woa", [64, NH, D], BF16)
                amB = [sb(ph, "c_am%d" % i, [64, NH, 512], BF16) for i in range(2)]
                pmB = [sb(ph, "c_pm%d" % i, [128, 8, 512], BF16) for i in range(2)]
                xtB = [sb(ph, "c_xt%d" % i, [128, 8, 512], F32) for i in range(2)]
                mo = sb(ph, "c_mo", [128, 8, 512], F32)
                sq = sb(ph, "c_sq", [128, 8, 512], BF16)
                lnv = sb(ph, "c_lnv", [128, 512], F32)
                rstd = sb(ph, "c_rstd", [128, 512], F32)
                xo = sb(ph, "c_xo", [128, 8, 512], F32)
                ROT = [1, 2, 3, 4]
                S.dma("sp", wop[:], wout_b[l].rearrange("(c p) n -> p c n", p=128), reads=[("wout_b", l)], writes=["wop"])
                S.dma("sp", woa[:], wout_b[l].rearrange("(h p) n -> p h n", p=64), reads=[("wout_b", l)], writes=["woa"])
                src = res0 if l == 0 else res
                for ti, (t0, W) in enumerate(tiles):
                    bsel = ti % 2
                    am, pm, xt = amB[bsel], pmB[bsel], xtB[bsel]
                    amt, pmt, xtt = "am%d" % bsel, "pm3%d" % bsel, "xt3%d" % bsel
                    S.dma("sp", am[:, :, 0:W], amT[:, :, t0:t0 + W].rearrange("h r t -> r h t"),
                          reads=[("amT", ti)], writes=[amt])
                    S.dma("sp", pm[:, :, 0:W], pmT[:, :, t0:t0 + W].rearrange("c p t -> p c t"),
                          reads=[("pmT", ti)], writes=[pmt])
                    S.dma("sp", xt[:, :, 0:W], src[:, :, t0:t0 + W].rearrange("c p t -> p c t"),
                          reads=rtok(t0, W), writes=[xtt])
                    for oc in range(8):
                        pi = next_ps(ROT)

                        def f(e, oc=oc, pi=pi, W=W, pm=pm, am=am):
                            ins = None
                            for c in range(8):
                                ins = e.matmul(ps[pi][:, 0:W], wop[:, c, oc * 128:(oc + 1) * 128], pm[:, c, 0:W],
                                               start=(c == 0), stop=False)
                            for h in range(NH):
                                ins = e.matmul(ps[pi][:, 0:W], woa[:, h, oc * 128:(oc + 1) * 128], am[:, h, 0:W],
                                               start=False, stop=(h == NH - 1))
                            return ins
                        S.op("pe", f, reads=["wop", "woa", pmt, amt], writes=[PS[pi]])
                        S.op("act", lambda e, oc=oc, pi=pi, W=W: e.activation(mo[:, oc, 0:W], ps[pi][:, 0:W], AF.Copy),
                             reads=[PS[pi]], writes=["mo"])
                    S.op("dve", lambda e, W=W: e.tensor_tensor(sq[:, :, 0:W], mo[:, :, 0:W], mo[:, :, 0:W], ALU.mult),
                         reads=["mo"], writes=["sq3"])
                    rms_stats(sq, W, 0, lnv, rstd, "sq3")
                    for c in range(8):
                        S.op("dve", lambda e, c=c, W=W: e.scalar_tensor_tensor(
                            mo[:, c, 0:W], mo[:, c, 0:W], gv[:, gcol(1, l, c):gcol(1, l, c) + 1], rstd[:, 0:W],
                            ALU.mult, ALU.mult), reads=["mo", "rstd", "gv"], writes=["mo"])
                    S.op("dve", lambda e, W=W, xt=xt: e.tensor_tensor(xo[:, :, 0:W], mo[:, :, 0:W], xt[:, :, 0:W], ALU.add),
                         reads=["mo", xtt], writes=["xo3"])
                    S.dma("pool", res[:, :, t0:t0 + W].rearrange("c p t -> p c t"), xo[:, :, 0:W],
                          reads=["xo3"], writes=rtok(t0, W))
            S.barrier()

        def phase3b(l, last):
            with ExitStack() as ph:
                wg = sb(ph, "d_wg", [128, 8, DFF], BF16)
                wu = sb(ph, "d_wu", [128, 8, DFF], BF16)
                wd = sb(ph, "d_wd", [128, NFC, D], BF16)
                xt = [sb(ph, "d_xt%d" % i, [128, 8, 256], F32) for i in range(2)]
                hn = [sb(ph, "d_hn%d" % i, [128, 8, 256], BF16) for i in range(2)]
                lnvA = sb(ph, "d_lnvA", [128, 256], F32)
                rstdA = sb(ph, "d_rstdA", [128, 256], F32)
                lnvB = sb(ph, "d_lnvB", [128, 256], F32)
                rstdB = sb(ph, "d_rstdB", [128, 256], F32)
                sg = [sb(ph, "d_sg%d" % i, [128, 256], F32) for i in range(2)]
                ff = sb(ph, "d_ff", [128, NFC, 256], BF16)
                fo = sb(ph, "d_fo", [128, 8, 256], F32)
                sq = sb(ph, "d_sq", [128, 8, 256], BF16)
                ROT = [1, 2, 3, 4, 5, 6]
                for kc in range(8):
                    S.dma("sp", wg[:, kc, :], wg_b[l, kc * 128:(kc + 1) * 128, :], reads=[("wg_b", l)], writes=["wg"])
                    S.dma("sp", wu[:, kc, :], wu_b[l, kc * 128:(kc + 1) * 128, :], reads=[("wu_b", l)], writes=["wu"])
                S.dma("sp", wd[:], wd_b[l].rearrange("(c p) n -> p c n", p=128), reads=[("wd_b", l)], writes=["wd"])

                def stats(src_sq, W, lnv, rstd, sqtok, sfx):
                    def f(e):
                        ins = None
                        for c in range(8):
                            ins = e.matmul(ps[0][:, 0:W], onesb[:, :], src_sq[:, c, 0:W], start=(c == 0), stop=(c == 7))
                        return ins
                    S.op("pe", f, reads=["onesb", sqtok], writes=[PS[0]])
                    S.op("act", lambda e: e.activation(lnv[:, 0:W], ps[0][:, 0:W], AF.Ln, bias=cst[:, 0:1]),
                         reads=[PS[0], "cst0"], writes=["lnv" + sfx])
                    S.op("act", lambda e: e.activation(rstd[:, 0:W], lnv[:, 0:W], AF.Exp, scale=-0.5),
                         reads=["lnv" + sfx], writes=["rstd" + sfx])

                def prologue(t2):
                    t0, W = tiles2[t2]
                    b = t2 % 2
                    S.dma("sp", xt[b][:, :, 0:W], res[:, :, t0:t0 + W].rearrange("c p t -> p c t"),
                          reads=rtok(t0, W), writes=["xt4%d" % b])
                    S.op("dve", lambda e: e.tensor_tensor(hn[b][:, :, 0:W], xt[b][:, :, 0:W], xt[b][:, :, 0:W], ALU.mult),
                         reads=["xt4%d" % b], writes=["hn4%d" % b])
                    stats(hn[b], W, lnvA, rstdA, "hn4%d" % b, "A")
                    for c in range(8):
                        S.op("dve", lambda e, c=c: e.scalar_tensor_tensor(
                            hn[b][:, c, 0:W], xt[b][:, c, 0:W], gv[:, gcol(2, l, c):gcol(2, l, c) + 1], rstdA[:, 0:W],
                            ALU.mult, ALU.mult), reads=["xt4%d" % b, "rstdA", "gv", PS[0]], writes=["hn4%d" % b])

                def gate_up(t2, fc0, fc1):
                    t0, W = tiles2[t2]
                    b = t2 % 2
                    for fc in range(fc0, fc1):
                        pg = next_ps(ROT)
                        pu = next_ps(ROT)

                        def fg(e, fc=fc, pg=pg):
                            ins = None
                            for kc in range(8):
                                ins = e.matmul(ps[pg][:, 0:W], wg[:, kc, fc * 128:(fc + 1) * 128], hn[b][:, kc, 0:W],
                                               start=(kc == 0), stop=(kc == 7))
                            return ins

                        def fu(e, fc=fc, pu=pu):
                            ins = None
                            for kc in range(8):
                                ins = e.matmul(ps[pu][:, 0:W], wu[:, kc, fc * 128:(fc + 1) * 128], hn[b][:, kc, 0:W],
                                               start=(kc == 0), stop=(kc == 7))
                            return ins
                        S.op("pe", fg, reads=["wg", "hn4%d" % b], writes=[PS[pg]])
                        S.op("pe", fu, reads=["wu", "hn4%d" % b], writes=[PS[pu]])
                        sgi = fc % 2
                        S.op("act", lambda e, sgi=sgi, pg=pg: e.activation(sg[sgi][:, 0:W], ps[pg][:, 0:W], AF.Silu),
                             reads=[PS[pg]], writes=["sg%d" % sgi])
                        S.op("dve", lambda e, fc=fc, sgi=sgi, pu=pu: e.tensor_tensor(
                            ff[:, fc, 0:W], sg[sgi][:, 0:W], ps[pu][:, 0:W], ALU.mult),
                            reads=["sg%d" % sgi, PS[pu]], writes=["ff"])

                def down(t2):
                    t0, W = tiles2[t2]
                    for oc in range(8):
                        pi = next_ps(ROT)

                        def fd(e, oc=oc, pi=pi):
                            ins = None
                            for fc in range(NFC):
                                ins = e.matmul(ps[pi][:, 0:W], wd[:, fc, oc * 128:(oc + 1) * 128], ff[:, fc, 0:W],
                                               start=(fc == 0), stop=(fc == NFC - 1))
                            return ins
                        S.op("pe", fd, reads=["wd", "ff"], writes=[PS[pi]])
                        S.op("act", lambda e, oc=oc, pi=pi: e.activation(fo[:, oc, 0:W], ps[pi][:, 0:W], AF.Copy),
                             reads=[PS[pi]], writes=["fo"])

                EPI = "pool"

                def epiA(t2):
                    t0, W = tiles2[t2]
                    S.op(EPI, lambda e: e.tensor_tensor(sq[:, :, 0:W], fo[:, :, 0:W], fo[:, :, 0:W], ALU.mult),
                         reads=["fo"], writes=["sq4"])

                def epiB(t2):
                    t0, W = tiles2[t2]
                    stats(sq, W, lnvB, rstdB, "sq4", "B")

                def epiC(t2):
                    t0, W = tiles2[t2]
                    b = t2 % 2
                    for c in range(8):
                        S.op("dve", lambda e, c=c: e.scalar_tensor_tensor(
                            fo[:, c, 0:W], fo[:, c, 0:W], gv[:, gcol(3, l, c):gcol(3, l, c) + 1], rstdB[:, 0:W],
                            ALU.mult, ALU.mult), reads=["fo", "rstdB", "gv"], writes=["fo"])
                    S.op(EPI, lambda e: e.tensor_tensor(fo[:, :, 0:W], fo[:, :, 0:W], xt[b][:, :, 0:W], ALU.add),
                         reads=["fo", "xt4%d" % b], writes=["fo"])
                    if not last:
                        S.dma("pool", res[:, :, t0:t0 + W].rearrange("c p t -> p c t"), fo[:, :, 0:W],
                              reads=["fo"], writes=rtok(t0, W))
                    else:
                        a = max(t0, META)
                        bb = min(t0 + W, T)
                        if bb > a:
                            S.dma("pool", outT[:, :, a - META:bb - META].rearrange("c p t -> p c t"),
                                  fo[:, :, a - t0:bb - t0], reads=["fo"], writes=[("out", t2)])

                n2 = len(tiles2)
                prologue(0)
                for t2 in range(n2):
                    if t2 > 0:
                        epiA(t2 - 1)
                    gate_up(t2, 0, 4)
                    if t2 > 0:
                        epiB(t2 - 1)
                        epiC(t2 - 1)
                    gate_up(t2, 4, NFC)
                    if t2 + 1 < n2:
                        prologue(t2 + 1)
                    down(t2)
                epiA(n2 - 1)
                epiB(n2 - 1)
                epiC(n2 - 1)
            S.barrier()

        S.barrier()
        for l in range(DEPTH):
            phase1(l)
            if stop_after in ("p1", "%d:p1" % l):
                break
            if l + 1 < DEPTH:
                convert_layer(l + 1)
            phaseF(l)
            if stop_after in ("pf", "%d:pf" % l):
                break
            phase2(l)
            if stop_after in ("p2", "%d:p2" % l):
                break
            phase3a(l)
            if stop_after in ("p3a", "%d:p3a" % l):
                break
            phase3b(l, last=(l == DEPTH - 1))
        S.emit(st)
    return nc


def _host_consts(DEPTH):
    negm = np.where(np.arange(128)[:, None] > np.arange(128)[None, :], -30000.0, 0.0).astype(ml_dtypes.bfloat16)
    identb = np.eye(128, dtype=np.float32).astype(ml_dtypes.bfloat16)
    umat = (np.arange(128)[:, None] <= np.arange(128)[None, :]).astype(np.float32)
    sel = np.zeros((16, 16, 128), np.float32)
    for h in range(16):
        sel[h, h, :] = 1.0
    sel = sel.reshape(16, 16 * 128)
    invc = np.zeros((128, 4, 16), np.float32)
    for g in range(4):
        w = 2 << g
        invc[:, g, :] = 1.0 / np.minimum(np.arange(16) + 1, w).astype(np.float32)[None, :]
    return negm, identb, umat, sel, invc.reshape(128, 64)


def make_in_maps(inputs, SEQ, DEPTH, n_cores):
    T = SEQ + META
    L = ((T + 127) // 128) * 128
    x = np.asarray(inputs["x"], np.float32)
    B = x.shape[0]
    meta = np.asarray(inputs["meta_tokens"], np.float32)
    negm, identb, umat, sel, invc = _host_consts(DEPTH)
    kinds = ["norm_mix_pre", "norm_mix_post", "norm_ffn_pre", "norm_ffn_post", "pool_scale"]
    gv = np.stack([np.asarray(inputs[k], np.float32) for k in kinds], 0)
    gv = gv.reshape(5, DEPTH, 8, 128).transpose(3, 0, 1, 2).reshape(128, 5 * DEPTH * 8)
    bf = np.asarray(inputs["b_forget"], np.float32)
    bfb = np.broadcast_to(bf[None, :, None, :], (128, DEPTH, 4, 16)).reshape(128, DEPTH * 64)
    common = {
        "gv": np.ascontiguousarray(gv), "bfb": np.ascontiguousarray(bfb),
        "negm": negm, "identb": identb, "umat": umat, "sel": sel, "invc": invc,
        "cneg": np.full((3, L), -1.0, ml_dtypes.bfloat16), "cpos": np.full((3, L), 1.0, ml_dtypes.bfloat16),
        "cv": np.full((128, L // 128, 64), 1.0, ml_dtypes.bfloat16),
        "w_in": np.asarray(inputs["w_in"], np.float32), "w_pool": np.asarray(inputs["w_pool"], np.float32),
        "w_out": np.asarray(inputs["w_out"], np.float32), "w_ffn_gate": np.asarray(inputs["w_ffn_gate"], np.float32),
        "w_ffn_up": np.asarray(inputs["w_ffn_up"], np.float32), "w_ffn_down": np.asarray(inputs["w_ffn_down"], np.float32),
    }
    maps = []
    for c in range(n_cores):
        b = c % B
        r = np.zeros((L, D), np.float32)
        r[0:META] = meta
        r[META:T] = x[b]
        r0 = np.ascontiguousarray(r.T.reshape(8, 128, L))
        m = dict(common)
        m["res0"] = r0
        maps.append(m)
    return maps


_NC_CACHE = {}


def kernel(x, meta_tokens, norm_mix_pre, norm_mix_post, norm_ffn_pre, norm_ffn_post,
           w_in, b_forget, w_pool, pool_scale, w_out, w_ffn_gate, w_ffn_up, w_ffn_down):
    inputs = dict(x=x, meta_tokens=meta_tokens, norm_mix_pre=norm_mix_pre, norm_mix_post=norm_mix_post,
                  norm_ffn_pre=norm_ffn_pre, norm_ffn_post=norm_ffn_post, w_in=w_in, b_forget=b_forget,
                  w_pool=w_pool, pool_scale=pool_scale, w_out=w_out, w_ffn_gate=w_ffn_gate,
                  w_ffn_up=w_ffn_up, w_ffn_down=w_ffn_down)
    x = np.asarray(x)
    B, SEQ, _ = x.shape
    DEPTH = np.asarray(w_in).shape[0]
    key = (SEQ, DEPTH)
    if key not in _NC_CACHE:
        _NC_CACHE[key] = build_nc(SEQ, DEPTH)
    nc = _NC_CACHE[key]
    maps = make_in_maps(inputs, SEQ, DEPTH, N_CORES)
    res = run_bass_kernel_spmd(nc, maps, core_ids=list(range(N_CORES)))
    out = np.empty((B, SEQ, D), np.float32)
    for b in range(B):
        o = np.asarray(res.results[b]["outT"], np.float32)
        out[b] = o.reshape(D, SEQ).T
    return out
```

```python
import numpy as np
import ml_dtypes
from contextlib import ExitStack
import concourse.bass as bass
import concourse.mybir as mybir
from concourse.bass_utils import run_bass_kernel_spmd

F32 = mybir.dt.float32
BF16 = mybir.dt.bfloat16
ALU = mybir.AluOpType
AF = mybir.ActivationFunctionType

D = 1024
NH = 16
HD = 64
DFF = 2816
NFC = DFF // 128
NIN = 6160
META = 16
KA = 70
EPS = 1e-6
C_POOL, C_Q, C_K, C_V, C_F, C_GP, C_GA = 0, 1024, 2048, 3072, 4096, 4112, 5136
N_CORES = 8
_DBG = False


class _Op:
    __slots__ = ("eng", "fn", "deps", "signal", "sigval", "is_dma", "lane", "laneval", "id")


class Sched:
    ENGS = ("pe", "act", "dve", "pool", "sp")

    def __init__(self, nc, n_lanes=8):
        self.nc = nc
        self.ops = []
        self.by_eng = {e: [] for e in self.ENGS}
        self.last_write = {}
        self.readers = {}
        self.n_lanes = n_lanes
        self.lane_next = {}
        self.lane_count = {}
        self.lane_last = {}
        self.pending_barrier = {}

    def _deps(self, eng, reads, writes):
        deps = set()
        for r in reads:
            w = self.last_write.get(r)
            if w is not None:
                deps.add(w)
        for t in writes:
            w = self.last_write.get(t)
            if w is not None:
                deps.add(w)
            rs = self.readers.get(t)
            if rs:
                deps.update(rs)
        pb = self.pending_barrier.pop(eng, None)
        if pb:
            deps.update(pb)
        return deps

    def _commit(self, oid, reads, writes):
        for r in reads:
            self.readers.setdefault(r, []).append(oid)
        for t in writes:
            self.last_write[t] = oid
            self.readers[t] = []

    def op(self, eng, fn, reads=(), writes=()):
        o = _Op()
        o.eng = eng
        o.fn = fn
        o.is_dma = False
        o.signal = False
        o.sigval = None
        o.id = len(self.ops)
        o.deps = self._deps(eng, reads, writes)
        self.ops.append(o)
        self.by_eng[eng].append(o)
        self._commit(o.id, reads, writes)
        return o.id

    def dma(self, queue, out, in_, reads=(), writes=()):
        o = _Op()
        o.eng = queue
        o.is_dma = True
        o.signal = True
        o.sigval = None
        o.id = len(self.ops)
        o.deps = self._deps(queue, reads, writes)
        lane = self.lane_next.get(queue, 0)
        self.lane_next[queue] = (lane + 1) % self.n_lanes
        key = (queue, lane)
        prev = self.lane_last.get(key)
        if prev is not None:
            o.deps.add(prev)
        self.lane_count[key] = self.lane_count.get(key, 0) + 1
        o.lane = key
        o.laneval = 16 * self.lane_count[key]
        self.lane_last[key] = o.id
        o.fn = lambda e, out=out, in_=in_: e.dma_start(out=out, in_=in_)
        self.ops.append(o)
        self.by_eng[queue].append(o)
        self._commit(o.id, reads, writes)
        return o.id

    def dma_custom(self, queue, fn, reads=(), writes=()):
        oid = self.dma(queue, None, None, reads=reads, writes=writes)
        self.ops[oid].fn = fn
        return oid

    def barrier(self):
        pend = set()
        for e in self.ENGS:
            if self.by_eng[e]:
                pend.add(self.by_eng[e][-1].id)
        for oid in self.lane_last.values():
            pend.add(oid)
        self.pending_barrier = {e: set(pend) for e in self.ENGS}

    def emit(self, stack):
        nc = self.nc
        ops = self.ops
        for o in ops:
            for d in o.deps:
                p = ops[d]
                if p.is_dma:
                    continue
                if p.eng != o.eng or p.eng != "pe":
                    p.signal = True
        for e in self.ENGS:
            lst = [o for o in self.by_eng[e] if not o.is_dma]
            if lst:
                lst[-1].signal = True
        for e in self.ENGS:
            c = 0
            for o in self.by_eng[e]:
                if o.is_dma:
                    continue
                if o.signal:
                    c += 1
                    o.sigval = c
        esem = {e: stack.enter_context(nc.semaphore("s_" + e)) for e in self.ENGS}
        lsem = {key: stack.enter_context(nc.semaphore("l_%s%d" % key)) for key in self.lane_count}
        final_waits = [(lsem[key], 16 * cnt) for key, cnt in self.lane_count.items()]
        for e in self.ENGS:
            lst = [o for o in self.by_eng[e] if not o.is_dma and o.signal]
            if lst:
                final_waits.append((esem[e], lst[-1].sigval))

        def run(e, engine):
            waited = {}
            for o in self.by_eng[e]:
                need = {}
                for d in o.deps:
                    p = ops[d]
                    if p.is_dma:
                        s, v = lsem[p.lane], p.laneval
                    else:
                        if p.eng == e and e == "pe":
                            continue
                        s, v = esem[p.eng], p.sigval
                    if v > need.get(s, 0):
                        need[s] = v
                for s, v in need.items():
                    if waited.get(s, 0) < v:
                        engine.wait_ge(s, v)
                        waited[s] = v
                ins = o.fn(engine)
                if o.is_dma:
                    ins.then_inc(lsem[o.lane], 16)
                elif o.signal:
                    ins.then_inc(esem[e], 1)
            if e == "sp":
                for s, v in final_waits:
                    engine.wait_ge(s, v)

        block = stack.enter_context(nc.Block())
        if self.by_eng["pe"]:
            @block.tensor
            def _(eng):
                run("pe", eng)
        if self.by_eng["act"]:
            @block.scalar
            def _(eng):
                run("act", eng)
        if self.by_eng["dve"]:
            @block.vector
            def _(eng):
                run("dve", eng)
        if self.by_eng["pool"]:
            @block.gpsimd
            def _(eng):
                run("pool", eng)

        @block.sync
        def _(eng):
            run("sp", eng)


def _tiles(L, w):
    out = []
    t = 0
    while t < L:
        ww = min(w, L - t)
        out.append((t, ww))
        t += ww
    return out


def build_nc(SEQ, DEPTH, stop_after=None):
    T = SEQ + META
    L = ((T + 127) // 128) * 128
    NB = L // 128
    tiles = _tiles(L, 512)
    NT = len(tiles)
    tiles2 = _tiles(L, 256)
    tilesA = [(0, 128)] + [(128 + t0, w) for (t0, w) in _tiles(L - 128, 1024)]
    NTA = len(tilesA)

    nc = bass.Bass("TRN2", target_bir_lowering=False)

    def din(name, shape, dt=F32):
        return nc.dram_tensor(name, list(shape), dt, kind="ExternalInput").ap()

    def dscr(name, shape, dt):
        return nc.dram_tensor(name, list(shape), dt, kind="Internal").ap()

    res0 = din("res0", [8, 128, L])
    gv_d = din("gv", [128, 5 * DEPTH * 8])
    bfb_d = din("bfb", [128, DEPTH * 64])
    negm_d = din("negm", [128, 128], BF16)
    identb_d = din("identb", [128, 128], BF16)
    U_d = din("umat", [128, 128])
    sel_d = din("sel", [16, 16 * 128])
    invc_d = din("invc", [128, 64])
    cneg_d = din("cneg", [3, L], BF16)
    cpos_d = din("cpos", [3, L], BF16)
    cv_d = din("cv", [128, NB, 64], BF16)
    w_in_d = din("w_in", [DEPTH, D, NIN])
    w_pool_d = din("w_pool", [DEPTH, 4, 256, 256])
    w_out_d = din("w_out", [DEPTH, D, D])
    w_g_d = din("w_ffn_gate", [DEPTH, D, DFF])
    w_u_d = din("w_ffn_up", [DEPTH, D, DFF])
    w_d_d = din("w_ffn_down", [DEPTH, DFF, D])
    outT = nc.dram_tensor("outT", [8, 128, SEQ], F32, kind="ExternalOutput").ap()

    res = dscr("res", [8, 128, L], F32)
    qT = dscr("qT", [NH, KA, L], BF16)
    kT = dscr("kT", [NH, KA, L], BF16)
    vA = dscr("vA", [NH, 128, NB, 128], BF16)
    sga = dscr("sga", [NH, 64, L], BF16)
    pmT = dscr("pmT", [8, 128, L], BF16)
    amT = dscr("amT", [NH, 64, L], BF16)
    win_b = dscr("win_b", [DEPTH, D, NIN], BF16)
    wpool_b = dscr("wpool_b", [DEPTH, 4, 256, 256], BF16)
    wout_b = dscr("wout_b", [DEPTH, D, D], BF16)
    wg_b = dscr("wg_b", [DEPTH, D, DFF], BF16)
    wu_b = dscr("wu_b", [DEPTH, D, DFF], BF16)
    wd_b = dscr("wd_b", [DEPTH, DFF, D], BF16)

    def gcol(kind, l, c):
        return (kind * DEPTH + l) * 8 + c

    with ExitStack() as st:
        S = Sched(nc)

        uniq = [0]

        def sb(ctx, name, shape, dt):
            uniq[0] += 1
            return ctx.enter_context(nc.sbuf_tensor("%s_%d" % (name, uniq[0]), list(shape), dt))

        gv = sb(st, "gv_sb", [128, 5 * DEPTH * 8], F32)
        bfb = sb(st, "bfb_sb", [128, DEPTH * 64], F32)
        negm = sb(st, "negm_sb", [128, 128], BF16)
        identb = sb(st, "identb_sb", [128, 128], BF16)
        umat = sb(st, "umat_sb", [128, 128], F32)
        invc = sb(st, "invc_sb", [128, 64], F32)
        cst = sb(st, "cst_sb", [128, 4], F32)
        onesb = sb(st, "onesb_sb", [128, 128], BF16)
        onesf = sb(st, "onesf_sb", [128, 1], F32)
        LF = sb(st, "LF_sb", [128, NB, 16], F32)
        RB = sb(st, "RB_sb", [128, NH, NTA + NB], F32)
        uh = sb(st, "uh_sb", [128, 8, 16], F32)
        psw = [st.enter_context(nc.psum_tensor("psw%d" % i, [128, 1024], F32)) for i in range(4)]
        ps = [psw[i // 2][:, (i % 2) * 512:(i % 2 + 1) * 512] for i in range(8)]
        PS = ["ps%d" % i for i in range(8)]

        S.dma("sp", gv[:], gv_d, writes=["gv"])
        S.dma("sp", bfb[:], bfb_d, writes=["bfb"])
        S.dma("sp", negm[:], negm_d, writes=["negm"])
        S.dma("sp", identb[:], identb_d, writes=["identb"])
        S.dma("sp", umat[:], U_d, writes=["umat"])
        S.dma("sp", invc[:], invc_d, writes=["invc"])
        S.op("dve", lambda e: e.memset(cst[:, 0:1], EPS), writes=["cst0"])
        S.op("dve", lambda e: e.memset(cst[:, 1:2], 1.0), writes=["cst1"])
        S.op("dve", lambda e: e.memset(onesb[:], 1.0 / 1024.0), writes=["onesb"])
        S.op("dve", lambda e: e.memset(onesf[:], 1.0), writes=["onesf"])

        def const_rows():
            for h in range(NH):
                S.dma("pool", qT[h, 67:70, :], cneg_d, writes=[("qc", h)])
                S.dma("pool", kT[h, 64:67, :], cpos_d, writes=[("kc", h)])
                S.dma("pool", vA[h, :, :, 64:128], cv_d, writes=[("vc", h)])
        const_rows()

        conv_queue = []

        def convert_layer(l, now):
            jobs = []
            for r0 in range(0, D, 128):
                jobs.append((win_b[l, r0:r0 + 128, :], w_in_d[l, r0:r0 + 128, :], ("win_b", l), True))
            jobs.append((wpool_b[l].rearrange("g a b -> (g a) b"), w_pool_d[l].rearrange("g a b -> (g a) b"),
                         ("wpool_b", l), True))
            for r0 in range(0, D, 256):
                jobs.append((wout_b[l, r0:r0 + 256, :], w_out_d[l, r0:r0 + 256, :], ("wout_b", l), False))
            for r0 in range(0, D, 128):
                jobs.append((wg_b[l, r0:r0 + 128, :], w_g_d[l, r0:r0 + 128, :], ("wg_b", l), False))
                jobs.append((wu_b[l, r0:r0 + 128, :], w_u_d[l, r0:r0 + 128, :], ("wu_b", l), False))
            for r0 in range(0, DFF, 256):
                jobs.append((wd_b[l, r0:r0 + 256, :], w_d_d[l, r0:r0 + 256, :], ("wd_b", l), False))
            for (o_, i_, tok, first) in jobs:
                if now and first:
                    S.dma("pool", o_, i_, writes=[tok])
                else:
                    conv_queue.append(lambda o_=o_, i_=i_, tok=tok: S.dma("pool", o_, i_, writes=[tok]))

        def conv_trickle(n):
            for _ in range(n):
                if conv_queue:
                    conv_queue.pop(0)()

        convert_layer(0, True)

        def rtok(t0, W):
            return [("res", k) for k in range(t0 // 256, (t0 + W + 255) // 256)]

        rot = {"i": 0}

        def next_ps(choices):
            i = choices[rot["i"] % len(choices)]
            rot["i"] += 1
            return i

        def rms_stats(src_sq, W, ps_i, lnv, rstd, sqtok):
            def f(e):
                ins = None
                for c in range(8):
                    ins = e.matmul(ps[ps_i][:, 0:W], onesb[:, :], src_sq[:, c, 0:W], start=(c == 0), stop=(c == 7))
                return ins
            S.op("pe", f, reads=["onesb", sqtok], writes=[PS[ps_i]])
            S.op("act", lambda e: e.activation(lnv[:, 0:W], ps[ps_i][:, 0:W], AF.Ln, bias=cst[:, 0:1]),
                 reads=[PS[ps_i], "cst0"], writes=["lnv"])
            S.op("act", lambda e: e.activation(rstd[:, 0:W], lnv[:, 0:W], AF.Exp, scale=-0.5),
                 reads=["lnv"], writes=["rstd"])

        def phase1(l):
            src = res0 if l == 0 else res
            with ExitStack() as ph:
                win = sb(ph, "win", [128, 8, NIN], BF16)
                wpl = sb(ph, "wpl", [128, 4, 2, 256], BF16)
                xt = sb(ph, "p1_xt", [128, 8, 512], F32)
                hnB = [sb(ph, "p1_hn%d" % i, [128, 8, 512], BF16) for i in range(2)]
                lnv = sb(ph, "p1_lnv", [128, 512], F32)
                ug = [sb(ph, "p1_ug%d" % i, [128, 2, 528], F32) for i in range(4)]
                wa = sb(ph, "p1_wa", [128, 2, 528], F32)
                wb = sb(ph, "p1_wb", [128, 2, 528], F32)
                dsb = [sb(ph, "p1_dsb%d" % i, [128, 2, 512], BF16) for i in range(4)]
                sgp = [sb(ph, "p1_sgp%d" % i, [128, 2, 512], BF16) for i in range(4)]
                pmg = [sb(ph, "p1_pm%d" % i, [128, 2, 512], BF16) for i in range(2)]
                stg = [sb(ph, "p1_stg%d" % i, [128, 4, 512], BF16) for i in range(2)]
                vsb = [sb(ph, "p1_vsb%d" % i, [128, 16, 64], BF16) for i in range(2)]
                tf = sb(ph, "p1_tf", [128, 64], F32)
                rstd = lnv
                ROT = [1, 2, 3, 4, 5, 6]
                if l == 0 and _DBG:
                    print("P1 sbuf bytes remaining", nc.sbuf_bytes_remaining)

                for kc in range(8):
                    S.dma("sp", win[:, kc, :], win_b[l, kc * 128:(kc + 1) * 128, :],
                          reads=[("win_b", l)], writes=["win"])
                S.dma("sp", wpl[:], wpool_b[l].rearrange("g (k p) n -> p g k n", p=128),
                      reads=[("wpool_b", l)], writes=["wpl"])
                S.op("dve", lambda e: e.memset(uh[:], 0.0), writes=["uh"])
                stg_i = [0]
                vcnt = [0]

                def prologue(ti):
                    t0, W = tiles[ti]
                    hn = hnB[ti % 2]
                    hnt = "hn%d" % (ti % 2)
                    S.dma("sp", xt[:, :, 0:W], src[:, :, t0:t0 + W].rearrange("c p t -> p c t"),
                          reads=rtok(t0, W), writes=["xt"])
                    S.op("dve", lambda e: e.tensor_tensor(hn[:, :, 0:W], xt[:, :, 0:W], xt[:, :, 0:W], ALU.mult),
                         reads=["xt"], writes=[hnt])

                    def fst(e):
                        ins = None
                        for c in range(8):
                            ins = e.matmul(ps[0][:, 0:W], onesb[:, :], hn[:, c, 0:W], start=(c == 0), stop=(c == 7))
                        return ins
                    S.op("pe", fst, reads=["onesb", hnt], writes=[PS[0]])
                    S.op("act", lambda e: e.activation(lnv[:, 0:W], ps[0][:, 0:W], AF.Ln, bias=cst[:, 0:1]),
                         reads=[PS[0], "cst0"], writes=["lnv"])
                    S.op("act", lambda e: e.activation(lnv[:, 0:W], lnv[:, 0:W], AF.Exp, scale=-0.5),
                         reads=["lnv"], writes=["lnv"])
                    for c in range(8):
                        S.op("dve", lambda e, c=c: e.scalar_tensor_tensor(
                            hn[:, c, 0:W], xt[:, c, 0:W], gv[:, gcol(0, l, c):gcol(0, l, c) + 1], rstd[:, 0:W],
                            ALU.mult, ALU.mult), reads=["xt", "lnv", "gv", PS[0]], writes=[hnt])

                prologue(0)
                for ti, (t0, W) in enumerate(tiles):
                    nsb = W // 128
                    hn = hnB[ti % 2]
                    hnt = "hn%d" % (ti % 2)

                    def proj(col0, ps_i, W=W, hn=hn, hnt=hnt):
                        def f(e):
                            ins = None
                            for kc in range(8):
                                ins = e.matmul(ps[ps_i][:, 0:W], win[:, kc, col0:col0 + 128], hn[:, kc, 0:W],
                                               start=(kc == 0), stop=(kc == 7))
                            return ins
                        S.op("pe", f, reads=["win", hnt], writes=[PS[ps_i]])

                    for g in range(4):
                        u = ug[g]
                        ut = "ug%d" % g
                        S.op("dve", lambda e, u=u, g=g: e.tensor_copy(u[:, :, 0:16], uh[:, 2 * g:2 * g + 2, :]),
                             reads=["uh"], writes=[ut])
                        for k in range(2):
                            pi = next_ps(ROT)
                            proj(C_POOL + (2 * g + k) * 128, pi)
                            S.op("act", lambda e, u=u, k=k, pi=pi, W=W: e.activation(
                                u[:, k, 16:16 + W], ps[pi][:, 0:W], AF.Copy), reads=[PS[pi]], writes=[ut])
                        S.op("dve", lambda e, u=u, g=g, W=W: e.tensor_copy(uh[:, 2 * g:2 * g + 2, :], u[:, :, W:W + 16]),
                             reads=[ut], writes=["uh"])
                    for g in range(4):
                        for k in range(2):
                            pi = next_ps(ROT)
                            proj(C_GP + (2 * g + k) * 128, pi)
                            S.op("act", lambda e, g=g, k=k, pi=pi, W=W: e.activation(
                                sgp[g][:, k, 0:W], ps[pi][:, 0:W], AF.Sigmoid), reads=[PS[pi]], writes=["sgp%d" % g])

                    if ti + 1 < NT:
                        prologue(ti + 1)

                    for g in range(4):
                        w = 2 << g
                        u = ug[g]
                        ut = "ug%d" % g
                        cur, curt = u, ut
                        lo = 0
                        sh = 1
                        bufs = [(wa, "wa"), (wb, "wb")]
                        bi = 0
                        while sh < w:
                            dst, dstt = bufs[bi]
                            bi ^= 1
                            lo2 = lo + sh
                            S.op("dve", lambda e, dst=dst, cur=cur, lo2=lo2, sh=sh, W=W: e.tensor_tensor(
                                dst[:, :, lo2:16 + W], cur[:, :, lo2:16 + W], cur[:, :, lo2 - sh:16 + W - sh], ALU.add),
                                reads=[curt], writes=[dstt])
                            cur, curt = dst, dstt
                            lo = lo2
                            sh *= 2
                        d_ = dsb[g]
                        dt_ = "dsb%d" % g
                        S.op("dve", lambda e, d_=d_, cur=cur, u=u, w=w, W=W: e.scalar_tensor_tensor(
                            d_[:, :, 0:W], cur[:, :, 16:16 + W], 1.0 / w, u[:, :, 16:16 + W], ALU.mult, ALU.subtract),
                            reads=[curt, ut], writes=[dt_])
                        if ti == 0:
                            for k in range(2):
                                S.op("dve", lambda e, cur=cur, g=g, k=k: e.tensor_tensor(
                                    cur[:, k, 16:32], cur[:, k, 16:32], invc[:, g * 16:(g + 1) * 16], ALU.mult),
                                    reads=[curt, "invc", dt_], writes=[curt])
                            S.op("dve", lambda e, d_=d_, cur=cur, u=u: e.tensor_tensor(
                                d_[:, :, 0:16], cur[:, :, 16:32], u[:, :, 16:32], ALU.subtract),
                                reads=[curt, ut], writes=[dt_])

                    def fm_group(col0, dst, kind, tokname, W=W, t0=t0, ti=ti):
                        for half in range(2):
                            sgb = stg[stg_i[0] % 2]
                            sgtok = "stg%d" % (stg_i[0] % 2)
                            stg_i[0] += 1
                            for k in range(4):
                                c = half * 4 + k
                                pi = next_ps(ROT)
                                proj(col0 + c * 128, pi)
                                if kind == "q":
                                    S.op("act", lambda e, sgb=sgb, k=k, pi=pi: e.activation(
                                        sgb[:, k, 0:W], ps[pi][:, 0:W], AF.Copy, scale=0.125),
                                        reads=[PS[pi]], writes=[sgtok])
                                elif kind == "k":
                                    S.op("act", lambda e, sgb=sgb, k=k, pi=pi: e.activation(
                                        sgb[:, k, 0:W], ps[pi][:, 0:W], AF.Copy), reads=[PS[pi]], writes=[sgtok])
                                else:
                                    S.op("act", lambda e, sgb=sgb, k=k, pi=pi: e.activation(
                                        sgb[:, k, 0:W], ps[pi][:, 0:W], AF.Sigmoid), reads=[PS[pi]], writes=[sgtok])
                            h0 = half * 8
                            for par in range(2):
                                dview = dst[h0 + par:h0 + 8:2, 0:64, t0:t0 + W].rearrange("c r t -> r c t")
                                S.dma("pool", dview, sgb[par * 64:(par + 1) * 64, :, 0:W],
                                      reads=[sgtok], writes=[(tokname, ti)])
                    fm_group(C_GA, sga, "g", "sga")
                    fm_group(C_Q, qT, "q", "qT")
                    fm_group(C_K, kT, "k", "kT")

                    b0 = t0 // 128
                    for s_ in range(nsb):
                        vb = vsb[vcnt[0] % 2]
                        vtok = "vsb%d" % (vcnt[0] % 2)
                        vcnt[0] += 1
                        for half in range(2):
                            pi = next_ps(ROT)

                            def f(e, s_=s_, half=half, pi=pi, hn=hn):
                                ins = None
                                for kc in range(8):
                                    ins = e.matmul(ps[pi][:, :], hn[:, kc, s_ * 128:(s_ + 1) * 128],
                                                   win[:, kc, C_V + half * 512:C_V + (half + 1) * 512],
                                                   start=(kc == 0), stop=(kc == 7))
                                return ins
                            S.op("pe", f, reads=["win", hnt], writes=[PS[pi]])
                            S.op("act", lambda e, vb=vb, half=half, pi=pi: e.activation(
                                vb[:, half * 8:(half + 1) * 8, :],
                                ps[pi][:, :].rearrange("p (h d) -> p h d", d=64), AF.Copy),
                                reads=[PS[pi]], writes=[vtok])

                        def ff_(e, s_=s_, hn=hn):
                            ins = None
                            for kc in range(8):
                                ins = e.matmul(ps[7][:, s_ * 16:(s_ + 1) * 16], hn[:, kc, s_ * 128:(s_ + 1) * 128],
                                               win[:, kc, C_F:C_F + 16], start=(kc == 0), stop=(kc == 7))
                            return ins
                        S.op("pe", ff_, reads=["win", hnt], writes=[PS[7]])
                        S.dma("pool", vA[:, :, b0 + s_, 0:64].rearrange("h p d -> p h d"),
                              vb[:, :, :], reads=[vtok], writes=[("vA", ti)])

                    for g in range(4):
                        d_ = dsb[g]
                        for o2 in range(2):
                            pi = next_ps(ROT)

                            def f(e, g=g, o2=o2, pi=pi, d_=d_, W=W):
                                ins = None
                                for k2 in range(2):
                                    ins = e.matmul(ps[pi][:, 0:W], wpl[:, g, k2, o2 * 128:(o2 + 1) * 128],
                                                   d_[:, k2, 0:W], start=(k2 == 0), stop=(k2 == 1))
                                return ins
                            S.op("pe", f, reads=["wpl", "dsb%d" % g], writes=[PS[pi]])
                            c = 2 * g + o2
                            S.op("dve", lambda e, c=c, pi=pi, g=g, o2=o2, W=W: e.scalar_tensor_tensor(
                                pmg[g % 2][:, o2, 0:W], ps[pi][:, 0:W], gv[:, gcol(4, l, c):gcol(4, l, c) + 1],
                                sgp[g][:, o2, 0:W], ALU.mult, ALU.mult),
                                reads=[PS[pi], "sgp%d" % g, "gv"], writes=["pm%d" % (g % 2)])
                        S.dma("pool", pmT[2 * g:2 * g + 2, :, t0:t0 + W].rearrange("c p t -> p c t"),
                              pmg[g % 2][:, :, 0:W], reads=["pm%d" % (g % 2)], writes=[("pmT", ti)])

                    S.op("dve", lambda e, nsb=nsb: e.tensor_tensor(
                        tf[:, 0:nsb * 16], ps[7][:, 0:nsb * 16], bfb[:, l * 64:l * 64 + nsb * 16], ALU.add),
                        reads=[PS[7], "bfb"], writes=["tf"])
                    S.op("act", lambda e, nsb=nsb: e.activation(tf[:, 0:nsb * 16], tf[:, 0:nsb * 16], AF.Exp, scale=-1.0),
                         reads=["tf"], writes=["tf"])
                    S.op("act", lambda e, nsb=nsb: e.activation(tf[:, 0:nsb * 16], tf[:, 0:nsb * 16], AF.Ln, bias=cst[:, 1:2]),
                         reads=["tf", "cst1"], writes=["tf"])
                    S.op("dve", lambda e, nsb=nsb, b0=b0: e.tensor_scalar(
                        LF[:, b0:b0 + nsb, :], tf[:, 0:nsb * 16].rearrange("p (b h) -> p b h", h=16), -1.0, None, ALU.mult),
                        reads=["tf"], writes=["LF"])
            S.barrier()

        def phaseF(l):
            with ExitStack() as ph:
                FT = sb(ph, "f_FT", [16, L], F32)
                X = sb(ph, "f_X", [16, L], F32)
                R1 = sb(ph, "f_R1", [16, L], F32)
                A = sb(ph, "f_A", [16, 3, L], BF16)
                tot = sb(ph, "f_tot", [16, NB + 1], F32)
                car = sb(ph, "f_car", [16, NB + 1], F32)
                RS = sb(ph, "f_RS", [16, NTA + NB], F32)
                sel = sb(ph, "f_sel", [16, 16 * 128], F32)
                S.dma("sp", sel[:], sel_d, writes=["sel"])
                for b in range(NB):
                    S.op("pe", lambda e, b=b: e.matmul(ps[6][0:16, b:b + 1], LF[:, b, :], onesf[:, 0:1],
                                                       start=True, stop=True),
                         reads=["LF", "onesf"], writes=[PS[6]])
                S.op("dve", lambda e: e.tensor_copy(tot[:, 0:NB], ps[6][0:16, 0:NB]), reads=[PS[6]], writes=["tot"])
                S.op("dve", lambda e: e.memset(car[:, 0:1], 0.0), writes=["car"])
                for b in range(1, NB):
                    S.op("dve", lambda e, b=b: e.tensor_tensor(car[:, b:b + 1], car[:, b - 1:b], tot[:, b - 1:b], ALU.add),
                         reads=["car", "tot"], writes=["car"])
                for b in range(NB):
                    pi = b % 4
                    S.op("pe", lambda e, b=b, pi=pi: e.matmul(ps[pi][0:16, 0:128], LF[:, b, :], umat[:, :],
                                                              start=True, stop=True),
                         reads=["LF", "umat"], writes=[PS[pi]])
                    S.op("dve", lambda e, b=b, pi=pi: e.tensor_scalar(
                        FT[:, b * 128:(b + 1) * 128], ps[pi][0:16, 0:128], car[:, b:b + 1], None, ALU.add),
                        reads=[PS[pi], "car"], writes=["FT"])
                S.op("dve", lambda e: e.tensor_copy(RS[:, 0:1], FT[:, 0:1]), reads=["FT"], writes=["RS"])
                S.op("dve", lambda e: e.tensor_copy(RS[:, 1:NTA], FT[:, 128:L:1024]), reads=["FT"], writes=["RS"])
                S.op("dve", lambda e: e.tensor_copy(RS[:, NTA:NTA + NB], FT[:, 0:L:128]), reads=["FT"], writes=["RS"])
                for h in range(NH):
                    pi = 4 + (h % 2)
                    S.op("pe", lambda e, h=h, pi=pi: e.matmul(ps[pi][:, 0:NTA + NB], sel[:, h * 128:(h + 1) * 128],
                                                              RS[:, :], start=True, stop=True),
                         reads=["sel", "RS"], writes=[PS[pi]])
                    S.op("dve", lambda e, h=h, pi=pi: e.tensor_copy(RB[:, h, 0:NTA], ps[pi][:, 0:NTA]),
                         reads=[PS[pi]], writes=["RB"])
                    S.op("dve", lambda e, h=h, pi=pi: e.tensor_scalar(RB[:, h, NTA:NTA + NB], ps[pi][:, NTA:NTA + NB],
                                                                      -1.0, None, ALU.mult),
                         reads=[PS[pi]], writes=["RB"])

                def split3(dst_dram, row0, tokname):
                    S.op("dve", lambda e: e.tensor_copy(A[:, 0, :], X[:, :]), reads=["X"], writes=["A"])
                    S.op("dve", lambda e: e.tensor_tensor(R1[:, :], X[:, :], A[:, 0, :], ALU.subtract),
                         reads=["X", "A"], writes=["R1"])
                    S.op("dve", lambda e: e.tensor_copy(A[:, 1, :], R1[:, :]), reads=["R1"], writes=["A"])
                    S.op("dve", lambda e: e.tensor_tensor(R1[:, :], R1[:, :], A[:, 1, :], ALU.subtract),
                         reads=["R1", "A"], writes=["R1"])
                    S.op("dve", lambda e: e.tensor_copy(A[:, 2, :], R1[:, :]), reads=["R1"], writes=["A"])
                    S.dma("pool", dst_dram[:, row0:row0 + 3, :], A[:, :, :], reads=["A"], writes=[tokname])

                for ia, (t0, W) in enumerate(tilesA):
                    S.op("dve", lambda e, ia=ia, t0=t0, W=W: e.tensor_scalar(
                        X[:, t0:t0 + W], FT[:, t0:t0 + W], RS[:, ia:ia + 1], None, ALU.subtract),
                        reads=["FT", "RS", "A"], writes=["X"])
                split3(qT, 64, "qTa")
                for b in range(NB):
                    S.op("dve", lambda e, b=b: e.tensor_scalar(
                        X[:, b * 128:(b + 1) * 128], FT[:, b * 128:(b + 1) * 128], RS[:, NTA + b:NTA + b + 1], None,
                        ALU.subtract), reads=["FT", "RS", "A"], writes=["X"])
                split3(kT, 67, "kTa")
            S.barrier()

        def phase2(l):
            with ExitStack() as ph:
                Kh = [sb(ph, "a_K%d" % i, [KA, L], BF16) for i in range(2)]
                Qh = [sb(ph, "a_Q%d" % i, [KA, L], BF16) for i in range(2)]
                Vh = [sb(ph, "a_V%d" % i, [128, NB, 128], BF16) for i in range(2)]
                Gh = [sb(ph, "a_G%d" % i, [64, L], BF16) for i in range(2)]
                Bc = [sb(ph, "a_B%d" % i, [128, NTA, NB], F32) for i in range(2)]
                NPB = 4
                PT = [sb(ph, "a_P%d" % i, [128, 1024], BF16) for i in range(NPB)]
                rb = sb(ph, "a_rb", [64, 1024], F32)
                tt = sb(ph, "a_tt", [64, 1024], F32)
                amo = [sb(ph, "a_am%d" % i, [64, 1024], BF16) for i in range(2)]
                alltok = lambda name: [(name, ti) for ti in range(NT)]

                def load_head(h):
                    i = h % 2
                    S.dma("sp", Kh[i][:, :], kT[h, :, :], reads=alltok("kT") + ["kTa", ("kc", h)], writes=["Kh%d" % i])
                    S.dma("sp", Qh[i][:, :], qT[h, :, :], reads=alltok("qT") + ["qTa", ("qc", h)], writes=["Qh%d" % i])
                    S.dma("sp", Vh[i][:, :, :], vA[h, :, :, :], reads=alltok("vA") + [("vc", h)], writes=["Vh%d" % i])
                    S.dma("sp", Gh[i][:, :], sga[h, :, :], reads=alltok("sga"), writes=["Gh%d" % i])
                    for ia in range(NTA):
                        S.op("dve", lambda e, i=i, h=h, ia=ia: e.tensor_scalar(
                            Bc[i][:, ia, :], RB[:, h, NTA:NTA + NB], RB[:, h, ia:ia + 1], None, ALU.add),
                            reads=["RB"], writes=["Bc%d" % i])

                def pieces(lo, hi):
                    out = []
                    if lo < 512:
                        out.append((lo, min(hi, 512)))
                    if hi > 512:
                        out.append((max(lo, 512), hi))
                    return out

                LA = 2
                items = []
                for h in range(NH):
                    for ia, (t0, W) in enumerate(tilesA):
                        jmax = (t0 + W) // 128 - 1
                        for j in range(jmax + 1):
                            items.append((h, ia, t0, W, j, jmax))

                def emit_front(n):
                    h, ia, t0, W, j, jmax = items[n]
                    i = h % 2
                    k0 = j * 128
                    q0 = max(t0, k0)
                    Wj = t0 + W - q0
                    sw = n % 2
                    pb = n % NPB
                    diag = k0 >= t0

                    def fS(e, i=i, sw=sw, k0=k0, q0=q0, Wj=Wj, diag=diag):
                        lo = 0
                        ins = None
                        if diag:
                            e.matmul(psw[sw][:, 0:128], identb[:, :], negm[:, :], start=True, stop=False)
                            ins = e.matmul(psw[sw][:, 0:128], Kh[i][:, k0:k0 + 128], Qh[i][:, q0:q0 + 128],
                                           start=False, stop=True)
                            lo = 128
                        if Wj > lo:
                            for (a_, b_) in pieces(lo, Wj):
                                ins = e.matmul(psw[sw][:, a_:b_], Kh[i][:, k0:k0 + 128], Qh[i][:, q0 + a_:q0 + b_],
                                               start=True, stop=True)
                        return ins
                    S.op("pe", fS, reads=["Kh%d" % i, "Qh%d" % i, "identb", "negm"], writes=[PS[2 * sw], PS[2 * sw + 1]])
                    S.op("act", lambda e, i=i, sw=sw, pb=pb, ia=ia, j=j, Wj=Wj: e.activation(
                        PT[pb][:, 0:Wj], psw[sw][:, 0:Wj], AF.Exp, bias=Bc[i][:, ia, j:j + 1]),
                        reads=[PS[2 * sw], PS[2 * sw + 1], "Bc%d" % i], writes=["PT%d" % pb])

                def emit_back(n):
                    h, ia, t0, W, j, jmax = items[n]
                    i = h % 2
                    if ia == 0 and j == 0:
                        conv_trickle(6)
                    if ia == 0 and j == 0 and h + 1 < NH:
                        load_head(h + 1)
                    k0 = j * 128
                    q0 = max(t0, k0)
                    c0 = q0 - t0
                    Wj = W - c0
                    pb = n % NPB
                    tcount = h * NTA + ia
                    ow = 2 + (tcount % 2)

                    def fO(e, i=i, ow=ow, pb=pb, j=j, c0=c0, Wj=Wj, jmax=jmax):
                        ins = None
                        for (a_, b_) in pieces(c0, c0 + Wj):
                            ins = e.matmul(psw[ow][:, a_:b_], Vh[i][:, j, :], PT[pb][:, a_ - c0:b_ - c0],
                                           start=(j == 0), stop=(j == jmax))
                        return ins
                    S.op("pe", fO, reads=["Vh%d" % i, "PT%d" % pb], writes=[PS[2 * ow], PS[2 * ow + 1]])
                    if j != jmax:
                        return
                    OT = [PS[2 * ow], PS[2 * ow + 1]]
                    S.op("dve", lambda e, ow=ow, W=W: e.reciprocal(rb[:, 0:W], psw[ow][64:128, 0:W]),
                         reads=OT, writes=["rb"])
                    S.op("dve", lambda e, ow=ow, W=W: e.tensor_tensor(tt[:, 0:W], psw[ow][0:64, 0:W], rb[:, 0:W], ALU.mult),
                         reads=OT + ["rb"], writes=["tt"])
                    ai = tcount % 2
                    S.op("dve", lambda e, ai=ai, i=i, t0=t0, W=W: e.tensor_tensor(
                        amo[ai][:, 0:W], tt[:, 0:W], Gh[i][:, t0:t0 + W], ALU.mult),
                        reads=["tt", "Gh%d" % i], writes=["amo%d" % ai])
                    S.dma("pool", amT[h, :, t0:t0 + W], amo[ai][:, 0:W], reads=["amo%d" % ai],
                          writes=[("amT", k) for k in range(t0 // 512, (t0 + W + 511) // 512)])

                load_head(0)
                for n in range(len(items) + LA):
                    if n < len(items):
                        emit_front(n)
                    if n - LA >= 0:
                        emit_back(n - LA)
                conv_trickle(len(conv_queue))
            S.barrier()

        def phase3a(l):
            with ExitStack() as ph:
                wop = sb(ph, "c_wop", [128, 8, D], BF16)
                woa = sb(ph, "c_woa", [64, NH, D], BF16)
                amB = [sb(ph, "c_am%d" % i, [64, NH, 512], BF16) for i in range(2)]
                pmB = [sb(ph, "c_pm%d" % i, [128, 8, 512], BF16) for i in range(2)]
                xtB = [sb(ph, "c_xt%d" % i, [128, 8, 512], F32) for i in range(2)]
                mo = sb(ph, "c_mo", [128, 8, 512], F32)
                sq = sb(ph, "c_sq", [128, 8, 512], BF16)
                lnv = sb(ph, "c_lnv", [128, 512], F32)
                rstd = sb(ph, "c_rstd", [128, 512], F32)
                xo = sb(ph, "c_xo", [128, 8, 512], F32)
                ROT = [1, 2, 3, 4]
                S.dma("sp", wop[:], wout_b[l].rearrange("(c p) n -> p c n", p=128), reads=[("wout_b", l)], writes=["wop"])
                S.dma("sp", woa[:], wout_b[l].rearrange("(h p) n -> p h n", p=64), reads=[("wout_b", l)], writes=["woa"])
                src = res0 if l == 0 else res
                for ti, (t0, W) in enumerate(tiles):
                    bsel = ti % 2
                    am, pm, xt = amB[bsel], pmB[bsel], xtB[bsel]
                    amt, pmt, xtt = "am%d" % bsel, "pm3%d" % bsel, "xt3%d" % bsel
                    S.dma("sp", am[:, :, 0:W], amT[:, :, t0:t0 + W].rearrange("h r t -> r h t"),
                          reads=[("amT", ti)], writes=[amt])
                    S.dma("sp", pm[:, :, 0:W], pmT[:, :, t0:t0 + W].rearrange("c p t -> p c t"),
                          reads=[("pmT", ti)], writes=[pmt])
                    S.dma("sp", xt[:, :, 0:W], src[:, :, t0:t0 + W].rearrange("c p t -> p c t"),
                          reads=rtok(t0, W), writes=[xtt])
                    for oc in range(8):
                        pi = next_ps(ROT)

                        def f(e, oc=oc, pi=pi, W=W, pm=pm, am=am):
                            ins = None
                            for c in range(8):
                                ins = e.matmul(ps[pi][:, 0:W], wop[:, c, oc * 128:(oc + 1) * 128], pm[:, c, 0:W],
                                               start=(c == 0), stop=False)
                            for h in range(NH):
                                ins = e.matmul(ps[pi][:, 0:W], woa[:, h, oc * 128:(oc + 1) * 128], am[:, h, 0:W],
                                               start=False, stop=(h == NH - 1))
                            return ins
                        S.op("pe", f, reads=["wop", "woa", pmt, amt], writes=[PS[pi]])
                        S.op("act", lambda e, oc=oc, pi=pi, W=W: e.activation(mo[:, oc, 0:W], ps[pi][:, 0:W], AF.Copy),
                             reads=[PS[pi]], writes=["mo"])
                    S.op("dve", lambda e, W=W: e.tensor_tensor(sq[:, :, 0:W], mo[:, :, 0:W], mo[:, :, 0:W], ALU.mult),
                         reads=["mo"], writes=["sq3"])
                    rms_stats(sq, W, 0, lnv, rstd, "sq3")
                    for c in range(8):
                        S.op("dve", lambda e, c=c, W=W: e.scalar_tensor_tensor(
                            mo[:, c, 0:W], mo[:, c, 0:W], gv[:, gcol(1, l, c):gcol(1, l, c) + 1], rstd[:, 0:W],
                            ALU.mult, ALU.mult), reads=["mo", "rstd", "gv"], writes=["mo"])
                    S.op("dve", lambda e, W=W, xt=xt: e.tensor_tensor(xo[:, :, 0:W], mo[:, :, 0:W], xt[:, :, 0:W], ALU.add),
                         reads=["mo", xtt], writes=["xo3"])
                    S.dma("pool", res[:, :, t0:t0 + W].rearrange("c p t -> p c t"), xo[:, :, 0:W],
                          reads=["xo3"], writes=rtok(t0, W))
            S.barrier()

        def phase3b(l, last):
            with ExitStack() as ph:
                wg = sb(ph, "d_wg", [128, 8, DFF], BF16)
                wu = sb(ph, "d_wu", [128, 8, DFF], BF16)
                wd = sb(ph, "d_wd", [128, NFC, D], BF16)
                xt = [sb(ph, "d_xt%d" % i, [128, 8, 256], F32) for i in range(2)]
                hn = [sb(ph, "d_hn%d" % i, [128, 8, 256], BF16) for i in range(2)]
                lnvA = sb(ph, "d_lnvA", [128, 256], F32)
                rstdA = sb(ph, "d_rstdA", [128, 256], F32)
                lnvB = sb(ph, "d_lnvB", [128, 256], F32)
                rstdB = sb(ph, "d_rstdB", [128, 256], F32)
                sg = [sb(ph, "d_sg%d" % i, [128, 256], F32) for i in range(2)]
                ff = sb(ph, "d_ff", [128, NFC, 256], BF16)
                fo = sb(ph, "d_fo", [128, 8, 256], F32)
                sq = sb(ph, "d_sq", [128, 8, 256], BF16)
                ROT = [1, 2, 3, 4, 5, 6]
                for kc in range(8):
                    S.dma("sp", wg[:, kc, :], wg_b[l, kc * 128:(kc + 1) * 128, :], reads=[("wg_b", l)], writes=["wg"])
                    S.dma("sp", wu[:, kc, :], wu_b[l, kc * 128:(kc + 1) * 128, :], reads=[("wu_b", l)], writes=["wu"])
                S.dma("sp", wd[:], wd_b[l].rearrange("(c p) n -> p c n", p=128), reads=[("wd_b", l)], writes=["wd"])

                def stats(src_sq, W, lnv, rstd, sqtok, sfx):
                    def f(e):
                        ins = None
                        for c in range(8):
                            ins = e.matmul(ps[0][:, 0:W], onesb[:, :], src_sq[:, c, 0:W], start=(c == 0), stop=(c == 7))
                        return ins
                    S.op("pe", f, reads=["onesb", sqtok], writes=[PS[0]])
                    S.op("act", lambda e: e.activation(lnv[:, 0:W], ps[0][:, 0:W], AF.Ln, bias=cst[:, 0:1]),
                         reads=[PS[0], "cst0"], writes=["lnv" + sfx])
                    S.op("act", lambda e: e.activation(rstd[:, 0:W], lnv[:, 0:W], AF.Exp, scale=-0.5),
                         reads=["lnv" + sfx], writes=["rstd" + sfx])

                def prologue(t2):
                    t0, W = tiles2[t2]
                    b = t2 % 2
                    S.dma("sp", xt[b][:, :, 0:W], res[:, :, t0:t0 + W].rearrange("c p t -> p c t"),
                          reads=rtok(t0, W), writes=["xt4%d" % b])
                    S.op("dve", lambda e: e.tensor_tensor(hn[b][:, :, 0:W], xt[b][:, :, 0:W], xt[b][:, :, 0:W], ALU.mult),
                         reads=["xt4%d" % b], writes=["hn4%d" % b])
                    stats(hn[b], W, lnvA, rstdA, "hn4%d" % b, "A")
                    for c in range(8):
                        S.op("dve", lambda e, c=c: e.scalar_tensor_tensor(
                            hn[b][:, c, 0:W], xt[b][:, c, 0:W], gv[:, gcol(2, l, c):gcol(2, l, c) + 1], rstdA[:, 0:W],
                            ALU.mult, ALU.mult), reads=["xt4%d" % b, "rstdA", "gv", PS[0]], writes=["hn4%d" % b])

                def gate_up(t2, fc0, fc1):
                    t0, W = tiles2[t2]
                    b = t2 % 2
                    for fc in range(fc0, fc1):
                        pg = next_ps(ROT)
                        pu = next_ps(ROT)

                        def fg(e, fc=fc, pg=pg):
                            ins = None
                            for kc in range(8):
                                ins = e.matmul(ps[pg][:, 0:W], wg[:, kc, fc * 128:(fc + 1) * 128], hn[b][:, kc, 0:W],
                                               start=(kc == 0), stop=(kc == 7))
                            return ins

                        def fu(e, fc=fc, pu=pu):
                            ins = None
                            for kc in range(8):
                                ins = e.matmul(ps[pu][:, 0:W], wu[:, kc, fc * 128:(fc + 1) * 128], hn[b][:, kc, 0:W],
                                               start=(kc == 0), stop=(kc == 7))
                            return ins
                        S.op("pe", fg, reads=["wg", "hn4%d" % b], writes=[PS[pg]])
                        S.op("pe", fu, reads=["wu", "hn4%d" % b], writes=[PS[pu]])
                        sgi = fc % 2
                        S.op("act", lambda e, sgi=sgi, pg=pg: e.activation(sg[sgi][:, 0:W], ps[pg][:, 0:W], AF.Silu),
                             reads=[PS[pg]], writes=["sg%d" % sgi])
                        S.op("dve", lambda e, fc=fc, sgi=sgi, pu=pu: e.tensor_tensor(
                            ff[:, fc, 0:W], sg[sgi][:, 0:W], ps[pu][:, 0:W], ALU.mult),
                            reads=["sg%d" % sgi, PS[pu]], writes=["ff"])

                def down(t2):
                    t0, W = tiles2[t2]
                    for oc in range(8):
                        pi = next_ps(ROT)

                        def fd(e, oc=oc, pi=pi):
                            ins = None
                            for fc in range(NFC):
                                ins = e.matmul(ps[pi][:, 0:W], wd[:, fc, oc * 128:(oc + 1) * 128], ff[:, fc, 0:W],
                                               start=(fc == 0), stop=(fc == NFC - 1))
                            return ins
                        S.op("pe", fd, reads=["wd", "ff"], writes=[PS[pi]])
                        S.op("act", lambda e, oc=oc, pi=pi: e.activation(fo[:, oc, 0:W], ps[pi][:, 0:W], AF.Copy),
                             reads=[PS[pi]], writes=["fo"])

                EPI = "pool"

                def epiA(t2):
                    t0, W = tiles2[t2]
                    S.op(EPI, lambda e: e.tensor_tensor(sq[:, :, 0:W], fo[:, :, 0:W], fo[:, :, 0:W], ALU.mult),
                         reads=["fo"], writes=["sq4"])

                def epiB(t2):
                    t0, W = tiles2[t2]
                    stats(sq, W, lnvB, rstdB, "sq4", "B")

                def epiC(t2):
                    t0, W = tiles2[t2]
                    b = t2 % 2
                    for c in range(8):
                        S.op("dve", lambda e, c=c: e.scalar_tensor_tensor(
                            fo[:, c, 0:W], fo[:, c, 0:W], gv[:, gcol(3, l, c):gcol(3, l, c) + 1], rstdB[:, 0:W],
                            ALU.mult, ALU.mult), reads=["fo", "rstdB", "gv"], writes=["fo"])
                    S.op(EPI, lambda e: e.tensor_tensor(fo[:, :, 0:W], fo[:, :, 0:W], xt[b][:, :, 0:W], ALU.add),
                         reads=["fo", "xt4%d" % b], writes=["fo"])
                    if not last:
                        S.dma("pool", res[:, :, t0:t0 + W].rearrange("c p t -> p c t"), fo[:, :, 0:W],
                              reads=["fo"], writes=rtok(t0, W))
                    else:
                        a = max(t0, META)
                        bb = min(t0 + W, T)
                        if bb > a:
                            S.dma("pool", outT[:, :, a - META:bb - META].rearrange("c p t -> p c t"),
                                  fo[:, :, a - t0:bb - t0], reads=["fo"], writes=[("out", t2)])

                n2 = len(tiles2)
                prologue(0)
                for t2 in range(n2):
                    if t2 > 0:
                        epiA(t2 - 1)
                    gate_up(t2, 0, 4)
                    if t2 > 0:
                        epiB(t2 - 1)
                        epiC(t2 - 1)
                    gate_up(t2, 4, NFC)
                    if t2 + 1 < n2:
                        prologue(t2 + 1)
                    down(t2)
                epiA(n2 - 1)
                epiB(n2 - 1)
                epiC(n2 - 1)
            S.barrier()

        S.barrier()
        for l in range(DEPTH):
            phase1(l)
            if stop_after in ("p1", "%d:p1" % l):
                break
            if l + 1 < DEPTH:
                convert_layer(l + 1, False)
            phaseF(l)
            if stop_after in ("pf", "%d:pf" % l):
                break
            phase2(l)
            if stop_after in ("p2", "%d:p2" % l):
                break
            phase3a(l)
            if stop_after in ("p3a", "%d:p3a" % l):
                break
            phase3b(l, last=(l == DEPTH - 1))
        S.emit(st)
    return nc


def _host_consts(DEPTH):
    negm = np.where(np.arange(128)[:, None] > np.arange(128)[None, :], -30000.0, 0.0).astype(ml_dtypes.bfloat16)
    identb = np.eye(128, dtype=np.float32).astype(ml_dtypes.bfloat16)
    umat = (np.arange(128)[:, None] <= np.arange(128)[None, :]).astype(np.float32)
    sel = np.zeros((16, 16, 128), np.float32)
    for h in range(16):
        sel[h, h, :] = 1.0
    sel = sel.reshape(16, 16 * 128)
    invc = np.zeros((128, 4, 16), np.float32)
    for g in range(4):
        w = 2 << g
        invc[:, g, :] = 1.0 / np.minimum(np.arange(16) + 1, w).astype(np.float32)[None, :]
    return negm, identb, umat, sel, invc.reshape(128, 64)


def make_in_maps(inputs, SEQ, DEPTH, n_cores):
    T = SEQ + META
    L = ((T + 127) // 128) * 128
    x = np.asarray(inputs["x"], np.float32)
    B = x.shape[0]
    meta = np.asarray(inputs["meta_tokens"], np.float32)
    negm, identb, umat, sel, invc = _host_consts(DEPTH)
    kinds = ["norm_mix_pre", "norm_mix_post", "norm_ffn_pre", "norm_ffn_post", "pool_scale"]
    gv = np.stack([np.asarray(inputs[k], np.float32) for k in kinds], 0)
    gv = gv.reshape(5, DEPTH, 8, 128).transpose(3, 0, 1, 2).reshape(128, 5 * DEPTH * 8)
    bf = np.asarray(inputs["b_forget"], np.float32)
    bfb = np.broadcast_to(bf[None, :, None, :], (128, DEPTH, 4, 16)).reshape(128, DEPTH * 64)
    common = {
        "gv": np.ascontiguousarray(gv), "bfb": np.ascontiguousarray(bfb),
        "negm": negm, "identb": identb, "umat": umat, "sel": sel, "invc": invc,
        "cneg": np.full((3, L), -1.0, ml_dtypes.bfloat16), "cpos": np.full((3, L), 1.0, ml_dtypes.bfloat16),
        "cv": np.full((128, L // 128, 64), 1.0, ml_dtypes.bfloat16),
        "w_in": np.asarray(inputs["w_in"], np.float32), "w_pool": np.asarray(inputs["w_pool"], np.float32),
        "w_out": np.asarray(inputs["w_out"], np.float32), "w_ffn_gate": np.asarray(inputs["w_ffn_gate"], np.float32),
        "w_ffn_up": np.asarray(inputs["w_ffn_up"], np.float32), "w_ffn_down": np.asarray(inputs["w_ffn_down"], np.float32),
    }
    maps = []
    for c in range(n_cores):
        b = c % B
        r = np.zeros((L, D), np.float32)
        r[0:META] = meta
        r[META:T] = x[b]
        r0 = np.ascontiguousarray(r.T.reshape(8, 128, L))
        m = dict(common)
        m["res0"] = r0
        maps.append(m)
    return maps


_NC_CACHE = {}


def kernel(x, meta_tokens, norm_mix_pre, norm_mix_post, norm_ffn_pre, norm_ffn_post,
           w_in, b_forget, w_pool, pool_scale, w_out, w_ffn_gate, w_ffn_up, w_ffn_down):
    inputs = dict(x=x, meta_tokens=meta_tokens, norm_mix_pre=norm_mix_pre, norm_mix_post=norm_mix_post,
                  norm_ffn_pre=norm_ffn_pre, norm_ffn_post=norm_ffn_post, w_in=w_in, b_forget=b_forget,
                  w_pool=w_pool, pool_scale=pool_scale, w_out=w_out, w_ffn_gate=w_ffn_gate,
                  w_ffn_up=w_ffn_up, w_ffn_down=w_ffn_down)
    x = np.asarray(x)
    B, SEQ, _ = x.shape
    DEPTH = np.asarray(w_in).shape[0]
    key = (SEQ, DEPTH)
    if key not in _NC_CACHE:
        _NC_CACHE[key] = build_nc(SEQ, DEPTH)
    nc = _NC_CACHE[key]
    maps = make_in_maps(inputs, SEQ, DEPTH, N_CORES)
    res = run_bass_kernel_spmd(nc, maps, core_ids=list(range(N_CORES)))
    out = np.empty((B, SEQ, D), np.float32)
    for b in range(B):
        o = np.asarray(res.results[b]["outT"], np.float32)
        out[b] = o.reshape(D, SEQ).T
    return out
```

```python
import numpy as np
import ml_dtypes
from contextlib import ExitStack
import concourse.bass as bass
import concourse.mybir as mybir
from concourse.bass_utils import run_bass_kernel_spmd

F32 = mybir.dt.float32
BF16 = mybir.dt.bfloat16
ALU = mybir.AluOpType
AF = mybir.ActivationFunctionType

D = 1024
NH = 16
HD = 64
DFF = 2816
NFC = DFF // 128
NIN = 6160
META = 16
KA = 70
EPS = 1e-6
C_POOL, C_Q, C_K, C_V, C_F, C_GP, C_GA = 0, 1024, 2048, 3072, 4096, 4112, 5136
N_CORES = 8
_DBG = False


class _Op:
    __slots__ = ("eng", "fn", "deps", "signal", "sigval", "is_dma", "lane", "laneval", "id")


class Sched:
    ENGS = ("pe", "act", "dve", "pool", "sp")

    def __init__(self, nc, n_lanes=8):
        self.nc = nc
        self.ops = []
        self.by_eng = {e: [] for e in self.ENGS}
        self.last_write = {}
        self.readers = {}
        self.n_lanes = n_lanes
        self.lane_next = {}
        self.lane_count = {}
        self.lane_last = {}
        self.pending_barrier = {}

    def _deps(self, eng, reads, writes):
        deps = set()
        for r in reads:
            w = self.last_write.get(r)
            if w is not None:
                deps.add(w)
        for t in writes:
            w = self.last_write.get(t)
            if w is not None:
                deps.add(w)
            rs = self.readers.get(t)
            if rs:
                deps.update(rs)
        pb = self.pending_barrier.pop(eng, None)
        if pb:
            deps.update(pb)
        return deps

    def _commit(self, oid, reads, writes):
        for r in reads:
            self.readers.setdefault(r, []).append(oid)
        for t in writes:
            self.last_write[t] = oid
            self.readers[t] = []

    def op(self, eng, fn, reads=(), writes=()):
        o = _Op()
        o.eng = eng
        o.fn = fn
        o.is_dma = False
        o.signal = False
        o.sigval = None
        o.id = len(self.ops)
        o.deps = self._deps(eng, reads, writes)
        self.ops.append(o)
        self.by_eng[eng].append(o)
        self._commit(o.id, reads, writes)
        return o.id

    def dma(self, queue, out, in_, reads=(), writes=()):
        o = _Op()
        o.eng = queue
        o.is_dma = True
        o.signal = True
        o.sigval = None
        o.id = len(self.ops)
        o.deps = self._deps(queue, reads, writes)
        lane = self.lane_next.get(queue, 0)
        self.lane_next[queue] = (lane + 1) % self.n_lanes
        key = (queue, lane)
        prev = self.lane_last.get(key)
        if prev is not None:
            o.deps.add(prev)
        self.lane_count[key] = self.lane_count.get(key, 0) + 1
        o.lane = key
        o.laneval = 16 * self.lane_count[key]
        self.lane_last[key] = o.id
        o.fn = lambda e, out=out, in_=in_: e.dma_start(out=out, in_=in_)
        self.ops.append(o)
        self.by_eng[queue].append(o)
        self._commit(o.id, reads, writes)
        return o.id

    def dma_custom(self, queue, fn, reads=(), writes=()):
        oid = self.dma(queue, None, None, reads=reads, writes=writes)
        self.ops[oid].fn = fn
        return oid

    def barrier(self):
        pend = set()
        for e in self.ENGS:
            if self.by_eng[e]:
                pend.add(self.by_eng[e][-1].id)
        for oid in self.lane_last.values():
            pend.add(oid)
        self.pending_barrier = {e: set(pend) for e in self.ENGS}

    def emit(self, stack):
        nc = self.nc
        ops = self.ops
        for o in ops:
            for d in o.deps:
                p = ops[d]
                if p.is_dma:
                    continue
                if p.eng != o.eng or p.eng != "pe":
                    p.signal = True
        for e in self.ENGS:
            lst = [o for o in self.by_eng[e] if not o.is_dma]
            if lst:
                lst[-1].signal = True
        for e in self.ENGS:
            c = 0
            for o in self.by_eng[e]:
                if o.is_dma:
                    continue
                if o.signal:
                    c += 1
                    o.sigval = c
        esem = {e: stack.enter_context(nc.semaphore("s_" + e)) for e in self.ENGS}
        lsem = {key: stack.enter_context(nc.semaphore("l_%s%d" % key)) for key in self.lane_count}
        final_waits = [(lsem[key], 16 * cnt) for key, cnt in self.lane_count.items()]
        for e in self.ENGS:
            lst = [o for o in self.by_eng[e] if not o.is_dma and o.signal]
            if lst:
                final_waits.append((esem[e], lst[-1].sigval))

        def run(e, engine):
            waited = {}
            for o in self.by_eng[e]:
                need = {}
                for d in o.deps:
                    p = ops[d]
                    if p.is_dma:
                        s, v = lsem[p.lane], p.laneval
                    else:
                        if p.eng == e and e == "pe":
                            continue
                        s, v = esem[p.eng], p.sigval
                    if v > need.get(s, 0):
                        need[s] = v
                for s, v in need.items():
                    if waited.get(s, 0) < v:
                        engine.wait_ge(s, v)
                        waited[s] = v
                ins = o.fn(engine)
                if o.is_dma:
                    ins.then_inc(lsem[o.lane], 16)
                elif o.signal:
                    ins.then_inc(esem[e], 1)
            if e == "sp":
                for s, v in final_waits:
                    engine.wait_ge(s, v)

        block = stack.enter_context(nc.Block())
        if self.by_eng["pe"]:
            @block.tensor
            def _(eng):
                run("pe", eng)
        if self.by_eng["act"]:
            @block.scalar
            def _(eng):
                run("act", eng)
        if self.by_eng["dve"]:
            @block.vector
            def _(eng):
                run("dve", eng)
        if self.by_eng["pool"]:
            @block.gpsimd
            def _(eng):
                run("pool", eng)

        @block.sync
        def _(eng):
            run("sp", eng)


def _tiles(L, w):
    out = []
    t = 0
    while t < L:
        ww = min(w, L - t)
        out.append((t, ww))
        t += ww
    return out


def build_nc(SEQ, DEPTH, stop_after=None):
    T = SEQ + META
    L = ((T + 127) // 128) * 128
    NB = L // 128
    tiles = _tiles(L, 512)
    NT = len(tiles)
    tiles2 = _tiles(L, 256)
    tilesA = [(0, 128)] + [(128 + t0, w) for (t0, w) in _tiles(T - 128, 1024)]
    NTA = len(tilesA)

    nc = bass.Bass("TRN2", target_bir_lowering=False)

    def din(name, shape, dt=F32):
        return nc.dram_tensor(name, list(shape), dt, kind="ExternalInput").ap()

    def dscr(name, shape, dt):
        return nc.dram_tensor(name, list(shape), dt, kind="Internal").ap()

    res0 = din("res0", [8, 128, L])
    gv_d = din("gv", [128, 5 * DEPTH * 8])
    bfb_d = din("bfb", [128, DEPTH * 64])
    negm_d = din("negm", [128, 128], BF16)
    identb_d = din("identb", [128, 128], BF16)
    U_d = din("umat", [128, 128])
    sel_d = din("sel", [16, 16 * 128])
    invc_d = din("invc", [128, 64])
    cneg_d = din("cneg", [3, L], BF16)
    cpos_d = din("cpos", [3, L], BF16)
    cv_d = din("cv", [128, NB, 64], BF16)
    w_in_d = din("w_in", [DEPTH, D, NIN])
    w_pool_d = din("w_pool", [DEPTH, 4, 256, 256])
    w_out_d = din("w_out", [DEPTH, D, D])
    w_g_d = din("w_ffn_gate", [DEPTH, D, DFF])
    w_u_d = din("w_ffn_up", [DEPTH, D, DFF])
    w_d_d = din("w_ffn_down", [DEPTH, DFF, D])
    outT = nc.dram_tensor("outT", [8, 128, SEQ], F32, kind="ExternalOutput").ap()

    res = dscr("res", [8, 128, L], F32)
    qT = dscr("qT", [NH, KA, L], BF16)
    kT = dscr("kT", [NH, KA, L], BF16)
    vA = dscr("vA", [NH, 128, NB, 128], BF16)
    sga = dscr("sga", [NH, 64, L], BF16)
    pmT = dscr("pmT", [8, 128, L], BF16)
    amT = dscr("amT", [NH, 64, L], BF16)
    win_b = dscr("win_b", [DEPTH, D, NIN], BF16)
    wpool_b = dscr("wpool_b", [DEPTH, 4, 256, 256], BF16)
    wout_b = dscr("wout_b", [DEPTH, D, D], BF16)
    wg_b = dscr("wg_b", [DEPTH, D, DFF], BF16)
    wu_b = dscr("wu_b", [DEPTH, D, DFF], BF16)
    wd_b = dscr("wd_b", [DEPTH, DFF, D], BF16)

    def gcol(kind, l, c):
        return (kind * DEPTH + l) * 8 + c

    with ExitStack() as st:
        S = Sched(nc)

        uniq = [0]

        def sb(ctx, name, shape, dt):
            uniq[0] += 1
            return ctx.enter_context(nc.sbuf_tensor("%s_%d" % (name, uniq[0]), list(shape), dt))

        gv = sb(st, "gv_sb", [128, 5 * DEPTH * 8], F32)
        bfb = sb(st, "bfb_sb", [128, DEPTH * 64], F32)
        negm = sb(st, "negm_sb", [128, 128], BF16)
        identb = sb(st, "identb_sb", [128, 128], BF16)
        umat = sb(st, "umat_sb", [128, 128], F32)
        invc = sb(st, "invc_sb", [128, 64], F32)
        cst = sb(st, "cst_sb", [128, 4], F32)
        onesb = sb(st, "onesb_sb", [128, 128], BF16)
        onesf = sb(st, "onesf_sb", [128, 1], F32)
        LF = sb(st, "LF_sb", [128, NB, 16], F32)
        RB = sb(st, "RB_sb", [128, NH, NTA + NB], F32)
        uh = sb(st, "uh_sb", [128, 8, 16], F32)
        psw = [st.enter_context(nc.psum_tensor("psw%d" % i, [128, 1024], F32)) for i in range(4)]
        ps = [psw[i // 2][:, (i % 2) * 512:(i % 2 + 1) * 512] for i in range(8)]
        PS = ["ps%d" % i for i in range(8)]

        S.dma("sp", gv[:], gv_d, writes=["gv"])
        S.dma("sp", bfb[:], bfb_d, writes=["bfb"])
        S.dma("sp", negm[:], negm_d, writes=["negm"])
        S.dma("sp", identb[:], identb_d, writes=["identb"])
        S.dma("sp", umat[:], U_d, writes=["umat"])
        S.dma("sp", invc[:], invc_d, writes=["invc"])
        S.op("dve", lambda e: e.memset(cst[:, 0:1], EPS), writes=["cst0"])
        S.op("dve", lambda e: e.memset(cst[:, 1:2], 1.0), writes=["cst1"])
        S.op("dve", lambda e: e.memset(onesb[:], 1.0 / 1024.0), writes=["onesb"])
        S.op("dve", lambda e: e.memset(onesf[:], 1.0), writes=["onesf"])

        def const_rows():
            for h in range(NH):
                S.dma("pool", qT[h, 67:70, :], cneg_d, writes=[("qc", h)])
                S.dma("pool", kT[h, 64:67, :], cpos_d, writes=[("kc", h)])
                S.dma("pool", vA[h, :, :, 64:128], cv_d, writes=[("vc", h)])
        const_rows()
        if T < L:
            with nc.sbuf_tensor("zpad", [64, NH, L - T], BF16) as zpad:
                S.op("dve", lambda e: e.memset(zpad[:], 0.0), writes=["zpad"])
                S.dma("pool", amT[:, :, T:L].rearrange("h r t -> r h t"), zpad[:], reads=["zpad"],
                      writes=[("amT", k) for k in range(T // 512, (L + 511) // 512)])

        conv_queue = []

        def convert_layer(l, now):
            jobs = []
            for r0 in range(0, D, 128):
                jobs.append((win_b[l, r0:r0 + 128, :], w_in_d[l, r0:r0 + 128, :], ("win_b", l), True))
            jobs.append((wpool_b[l].rearrange("g a b -> (g a) b"), w_pool_d[l].rearrange("g a b -> (g a) b"),
                         ("wpool_b", l), True))
            for r0 in range(0, D, 256):
                jobs.append((wout_b[l, r0:r0 + 256, :], w_out_d[l, r0:r0 + 256, :], ("wout_b", l), False))
            for r0 in range(0, D, 128):
                jobs.append((wg_b[l, r0:r0 + 128, :], w_g_d[l, r0:r0 + 128, :], ("wg_b", l), False))
                jobs.append((wu_b[l, r0:r0 + 128, :], w_u_d[l, r0:r0 + 128, :], ("wu_b", l), False))
            for r0 in range(0, DFF, 256):
                jobs.append((wd_b[l, r0:r0 + 256, :], w_d_d[l, r0:r0 + 256, :], ("wd_b", l), False))
            for (o_, i_, tok, first) in jobs:
                if now and first:
                    S.dma("pool", o_, i_, writes=[tok])
                else:
                    conv_queue.append(lambda o_=o_, i_=i_, tok=tok: S.dma("pool", o_, i_, writes=[tok]))

        def conv_trickle(n):
            for _ in range(n):
                if conv_queue:
                    conv_queue.pop(0)()

        convert_layer(0, True)

        def rtok(t0, W):
            return [("res", k) for k in range(t0 // 256, (t0 + W + 255) // 256)]

        rot = {"i": 0}

        def next_ps(choices):
            i = choices[rot["i"] % len(choices)]
            rot["i"] += 1
            return i

        def rms_stats(src_sq, W, ps_i, lnv, rstd, sqtok):
            def f(e):
                ins = None
                for c in range(8):
                    ins = e.matmul(ps[ps_i][:, 0:W], onesb[:, :], src_sq[:, c, 0:W], start=(c == 0), stop=(c == 7))
                return ins
            S.op("pe", f, reads=["onesb", sqtok], writes=[PS[ps_i]])
            S.op("act", lambda e: e.activation(lnv[:, 0:W], ps[ps_i][:, 0:W], AF.Ln, bias=cst[:, 0:1]),
                 reads=[PS[ps_i], "cst0"], writes=["lnv"])
            S.op("act", lambda e: e.activation(rstd[:, 0:W], lnv[:, 0:W], AF.Exp, scale=-0.5),
                 reads=["lnv"], writes=["rstd"])

        def phase1(l):
            src = res0 if l == 0 else res
            with ExitStack() as ph:
                win = sb(ph, "win", [128, 8, NIN], BF16)
                wpl = sb(ph, "wpl", [128, 4, 2, 256], BF16)
                xt = sb(ph, "p1_xt", [128, 8, 512], F32)
                hnB = [sb(ph, "p1_hn%d" % i, [128, 8, 512], BF16) for i in range(2)]
                lnv = sb(ph, "p1_lnv", [128, 512], F32)
                ug = [sb(ph, "p1_ug%d" % i, [128, 2, 528], F32) for i in range(4)]
                wa = sb(ph, "p1_wa", [128, 2, 528], F32)
                wb = sb(ph, "p1_wb", [128, 2, 528], F32)
                dsb = [sb(ph, "p1_dsb%d" % i, [128, 2, 512], BF16) for i in range(4)]
                sgp = [sb(ph, "p1_sgp%d" % i, [128, 2, 512], BF16) for i in range(4)]
                pmg = [sb(ph, "p1_pm%d" % i, [128, 2, 512], BF16) for i in range(2)]
                stg = [sb(ph, "p1_stg%d" % i, [128, 4, 512], BF16) for i in range(2)]
                vsb = [sb(ph, "p1_vsb%d" % i, [128, 16, 64], BF16) for i in range(2)]
                tf = sb(ph, "p1_tf", [128, 64], F32)
                rstd = lnv
                ROT = [1, 2, 3, 4, 5, 6]
                if l == 0 and _DBG:
                    print("P1 sbuf bytes remaining", nc.sbuf_bytes_remaining)

                for kc in range(8):
                    S.dma("sp", win[:, kc, :], win_b[l, kc * 128:(kc + 1) * 128, :],
                          reads=[("win_b", l)], writes=["win"])
                S.dma("sp", wpl[:], wpool_b[l].rearrange("g (k p) n -> p g k n", p=128),
                      reads=[("wpool_b", l)], writes=["wpl"])
                S.op("dve", lambda e: e.memset(uh[:], 0.0), writes=["uh"])
                stg_i = [0]
                vcnt = [0]

                def prologue(ti):
                    t0, W = tiles[ti]
                    hn = hnB[ti % 2]
                    hnt = "hn%d" % (ti % 2)
                    S.dma("sp", xt[:, :, 0:W], src[:, :, t0:t0 + W].rearrange("c p t -> p c t"),
                          reads=rtok(t0, W), writes=["xt"])
                    S.op("dve", lambda e: e.tensor_tensor(hn[:, :, 0:W], xt[:, :, 0:W], xt[:, :, 0:W], ALU.mult),
                         reads=["xt"], writes=[hnt])

                    def fst(e):
                        ins = None
                        for c in range(8):
                            ins = e.matmul(ps[0][:, 0:W], onesb[:, :], hn[:, c, 0:W], start=(c == 0), stop=(c == 7))
                        return ins
                    S.op("pe", fst, reads=["onesb", hnt], writes=[PS[0]])
                    S.op("act", lambda e: e.activation(lnv[:, 0:W], ps[0][:, 0:W], AF.Ln, bias=cst[:, 0:1]),
                         reads=[PS[0], "cst0"], writes=["lnv"])
                    S.op("act", lambda e: e.activation(lnv[:, 0:W], lnv[:, 0:W], AF.Exp, scale=-0.5),
                         reads=["lnv"], writes=["lnv"])
                    for c in range(8):
                        S.op("dve", lambda e, c=c: e.scalar_tensor_tensor(
                            hn[:, c, 0:W], xt[:, c, 0:W], gv[:, gcol(0, l, c):gcol(0, l, c) + 1], rstd[:, 0:W],
                            ALU.mult, ALU.mult), reads=["xt", "lnv", "gv", PS[0]], writes=[hnt])

                prologue(0)
                for ti, (t0, W) in enumerate(tiles):
                    nsb = W // 128
                    hn = hnB[ti % 2]
                    hnt = "hn%d" % (ti % 2)

                    def proj(col0, ps_i, W=W, hn=hn, hnt=hnt):
                        def f(e):
                            ins = None
                            for kc in range(8):
                                ins = e.matmul(ps[ps_i][:, 0:W], win[:, kc, col0:col0 + 128], hn[:, kc, 0:W],
                                               start=(kc == 0), stop=(kc == 7))
                            return ins
                        S.op("pe", f, reads=["win", hnt], writes=[PS[ps_i]])

                    for g in range(4):
                        u = ug[g]
                        ut = "ug%d" % g
                        S.op("dve", lambda e, u=u, g=g: e.tensor_copy(u[:, :, 0:16], uh[:, 2 * g:2 * g + 2, :]),
                             reads=["uh"], writes=[ut])
                        for k in range(2):
                            pi = next_ps(ROT)
                            proj(C_POOL + (2 * g + k) * 128, pi)
                            S.op("act", lambda e, u=u, k=k, pi=pi, W=W: e.activation(
                                u[:, k, 16:16 + W], ps[pi][:, 0:W], AF.Copy), reads=[PS[pi]], writes=[ut])
                        S.op("dve", lambda e, u=u, g=g, W=W: e.tensor_copy(uh[:, 2 * g:2 * g + 2, :], u[:, :, W:W + 16]),
                             reads=[ut], writes=["uh"])
                    for g in range(4):
                        for k in range(2):
                            pi = next_ps(ROT)
                            proj(C_GP + (2 * g + k) * 128, pi)
                            S.op("act", lambda e, g=g, k=k, pi=pi, W=W: e.activation(
                                sgp[g][:, k, 0:W], ps[pi][:, 0:W], AF.Sigmoid), reads=[PS[pi]], writes=["sgp%d" % g])

                    if ti + 1 < NT:
                        prologue(ti + 1)

                    for g in range(4):
                        w = 2 << g
                        u = ug[g]
                        ut = "ug%d" % g
                        cur, curt = u, ut
                        lo = 0
                        sh = 1
                        bufs = [(wa, "wa"), (wb, "wb")]
                        bi = 0
                        while sh < w:
                            dst, dstt = bufs[bi]
                            bi ^= 1
                            lo2 = lo + sh
                            S.op("dve", lambda e, dst=dst, cur=cur, lo2=lo2, sh=sh, W=W: e.tensor_tensor(
                                dst[:, :, lo2:16 + W], cur[:, :, lo2:16 + W], cur[:, :, lo2 - sh:16 + W - sh], ALU.add),
                                reads=[curt], writes=[dstt])
                            cur, curt = dst, dstt
                            lo = lo2
                            sh *= 2
                        d_ = dsb[g]
                        dt_ = "dsb%d" % g
                        S.op("dve", lambda e, d_=d_, cur=cur, u=u, w=w, W=W: e.scalar_tensor_tensor(
                            d_[:, :, 0:W], cur[:, :, 16:16 + W], 1.0 / w, u[:, :, 16:16 + W], ALU.mult, ALU.subtract),
                            reads=[curt, ut], writes=[dt_])
                        if ti == 0:
                            for k in range(2):
                                S.op("dve", lambda e, cur=cur, g=g, k=k: e.tensor_tensor(
                                    cur[:, k, 16:32], cur[:, k, 16:32], invc[:, g * 16:(g + 1) * 16], ALU.mult),
                                    reads=[curt, "invc", dt_], writes=[curt])
                            S.op("dve", lambda e, d_=d_, cur=cur, u=u: e.tensor_tensor(
                                d_[:, :, 0:16], cur[:, :, 16:32], u[:, :, 16:32], ALU.subtract),
                                reads=[curt, ut], writes=[dt_])

                    def fm_group(col0, dst, kind, tokname, W=W, t0=t0, ti=ti):
                        for half in range(2):
                            sgb = stg[stg_i[0] % 2]
                            sgtok = "stg%d" % (stg_i[0] % 2)
                            stg_i[0] += 1
                            for k in range(4):
                                c = half * 4 + k
                                pi = next_ps(ROT)
                                proj(col0 + c * 128, pi)
                                if kind == "q":
                                    S.op("act", lambda e, sgb=sgb, k=k, pi=pi: e.activation(
                                        sgb[:, k, 0:W], ps[pi][:, 0:W], AF.Copy, scale=0.125),
                                        reads=[PS[pi]], writes=[sgtok])
                                elif kind == "k":
                                    S.op("act", lambda e, sgb=sgb, k=k, pi=pi: e.activation(
                                        sgb[:, k, 0:W], ps[pi][:, 0:W], AF.Copy), reads=[PS[pi]], writes=[sgtok])
                                else:
                                    S.op("act", lambda e, sgb=sgb, k=k, pi=pi: e.activation(
                                        sgb[:, k, 0:W], ps[pi][:, 0:W], AF.Sigmoid), reads=[PS[pi]], writes=[sgtok])
                            h0 = half * 8
                            for par in range(2):
                                dview = dst[h0 + par:h0 + 8:2, 0:64, t0:t0 + W].rearrange("c r t -> r c t")
                                S.dma("pool", dview, sgb[par * 64:(par + 1) * 64, :, 0:W],
                                      reads=[sgtok], writes=[(tokname, ti)])
                    fm_group(C_GA, sga, "g", "sga")
                    fm_group(C_Q, qT, "q", "qT")
                    fm_group(C_K, kT, "k", "kT")

                    b0 = t0 // 128
                    for s_ in range(nsb):
                        vb = vsb[vcnt[0] % 2]
                        vtok = "vsb%d" % (vcnt[0] % 2)
                        vcnt[0] += 1
                        for half in range(2):
                            pi = next_ps(ROT)

                            def f(e, s_=s_, half=half, pi=pi, hn=hn):
                                ins = None
                                for kc in range(8):
                                    ins = e.matmul(ps[pi][:, :], hn[:, kc, s_ * 128:(s_ + 1) * 128],
                                                   win[:, kc, C_V + half * 512:C_V + (half + 1) * 512],
                                                   start=(kc == 0), stop=(kc == 7))
                                return ins
                            S.op("pe", f, reads=["win", hnt], writes=[PS[pi]])
                            S.op("act", lambda e, vb=vb, half=half, pi=pi: e.activation(
                                vb[:, half * 8:(half + 1) * 8, :],
                                ps[pi][:, :].rearrange("p (h d) -> p h d", d=64), AF.Copy),
                                reads=[PS[pi]], writes=[vtok])

                        def ff_(e, s_=s_, hn=hn):
                            ins = None
                            for kc in range(8):
                                ins = e.matmul(ps[7][:, s_ * 16:(s_ + 1) * 16], hn[:, kc, s_ * 128:(s_ + 1) * 128],
                                               win[:, kc, C_F:C_F + 16], start=(kc == 0), stop=(kc == 7))
                            return ins
                        S.op("pe", ff_, reads=["win", hnt], writes=[PS[7]])
                        S.dma("pool", vA[:, :, b0 + s_, 0:64].rearrange("h p d -> p h d"),
                              vb[:, :, :], reads=[vtok], writes=[("vA", ti)])

                    for g in range(4):
                        d_ = dsb[g]
                        for o2 in range(2):
                            pi = next_ps(ROT)

                            def f(e, g=g, o2=o2, pi=pi, d_=d_, W=W):
                                ins = None
                                for k2 in range(2):
                                    ins = e.matmul(ps[pi][:, 0:W], wpl[:, g, k2, o2 * 128:(o2 + 1) * 128],
                                                   d_[:, k2, 0:W], start=(k2 == 0), stop=(k2 == 1))
                                return ins
                            S.op("pe", f, reads=["wpl", "dsb%d" % g], writes=[PS[pi]])
                            c = 2 * g + o2
                            S.op("dve", lambda e, c=c, pi=pi, g=g, o2=o2, W=W: e.scalar_tensor_tensor(
                                pmg[g % 2][:, o2, 0:W], ps[pi][:, 0:W], gv[:, gcol(4, l, c):gcol(4, l, c) + 1],
                                sgp[g][:, o2, 0:W], ALU.mult, ALU.mult),
                                reads=[PS[pi], "sgp%d" % g, "gv"], writes=["pm%d" % (g % 2)])
                        S.dma("pool", pmT[2 * g:2 * g + 2, :, t0:t0 + W].rearrange("c p t -> p c t"),
                              pmg[g % 2][:, :, 0:W], reads=["pm%d" % (g % 2)], writes=[("pmT", ti)])

                    S.op("dve", lambda e, nsb=nsb: e.tensor_tensor(
                        tf[:, 0:nsb * 16], ps[7][:, 0:nsb * 16], bfb[:, l * 64:l * 64 + nsb * 16], ALU.add),
                        reads=[PS[7], "bfb"], writes=["tf"])
                    S.op("act", lambda e, nsb=nsb: e.activation(tf[:, 0:nsb * 16], tf[:, 0:nsb * 16], AF.Exp, scale=-1.0),
                         reads=["tf"], writes=["tf"])
                    S.op("act", lambda e, nsb=nsb: e.activation(tf[:, 0:nsb * 16], tf[:, 0:nsb * 16], AF.Ln, bias=cst[:, 1:2]),
                         reads=["tf", "cst1"], writes=["tf"])
                    S.op("dve", lambda e, nsb=nsb, b0=b0: e.tensor_scalar(
                        LF[:, b0:b0 + nsb, :], tf[:, 0:nsb * 16].rearrange("p (b h) -> p b h", h=16), -1.0, None, ALU.mult),
                        reads=["tf"], writes=["LF"])
            S.barrier()

        def phaseF(l):
            with ExitStack() as ph:
                FT = sb(ph, "f_FT", [16, L], F32)
                X = sb(ph, "f_X", [16, L], F32)
                R1 = sb(ph, "f_R1", [16, L], F32)
                A = sb(ph, "f_A", [16, 3, L], BF16)
                tot = sb(ph, "f_tot", [16, NB + 1], F32)
                car = sb(ph, "f_car", [16, NB + 1], F32)
                RS = sb(ph, "f_RS", [16, NTA + NB], F32)
                sel = sb(ph, "f_sel", [16, 16 * 128], F32)
                S.dma("sp", sel[:], sel_d, writes=["sel"])
                for b in range(NB):
                    S.op("pe", lambda e, b=b: e.matmul(ps[6][0:16, b:b + 1], LF[:, b, :], onesf[:, 0:1],
                                                       start=True, stop=True),
                         reads=["LF", "onesf"], writes=[PS[6]])
                S.op("dve", lambda e: e.tensor_copy(tot[:, 0:NB], ps[6][0:16, 0:NB]), reads=[PS[6]], writes=["tot"])
                S.op("dve", lambda e: e.memset(car[:, 0:1], 0.0), writes=["car"])
                for b in range(1, NB):
                    S.op("dve", lambda e, b=b: e.tensor_tensor(car[:, b:b + 1], car[:, b - 1:b], tot[:, b - 1:b], ALU.add),
                         reads=["car", "tot"], writes=["car"])
                for b in range(NB):
                    pi = b % 4
                    S.op("pe", lambda e, b=b, pi=pi: e.matmul(ps[pi][0:16, 0:128], LF[:, b, :], umat[:, :],
                                                              start=True, stop=True),
                         reads=["LF", "umat"], writes=[PS[pi]])
                    S.op("dve", lambda e, b=b, pi=pi: e.tensor_scalar(
                        FT[:, b * 128:(b + 1) * 128], ps[pi][0:16, 0:128], car[:, b:b + 1], None, ALU.add),
                        reads=[PS[pi], "car"], writes=["FT"])
                S.op("dve", lambda e: e.tensor_copy(RS[:, 0:1], FT[:, 0:1]), reads=["FT"], writes=["RS"])
                S.op("dve", lambda e: e.tensor_copy(RS[:, 1:NTA], FT[:, 128:T:1024]), reads=["FT"], writes=["RS"])
                S.op("dve", lambda e: e.tensor_copy(RS[:, NTA:NTA + NB], FT[:, 0:L:128]), reads=["FT"], writes=["RS"])
                for h in range(NH):
                    pi = 4 + (h % 2)
                    S.op("pe", lambda e, h=h, pi=pi: e.matmul(ps[pi][:, 0:NTA + NB], sel[:, h * 128:(h + 1) * 128],
                                                              RS[:, :], start=True, stop=True),
                         reads=["sel", "RS"], writes=[PS[pi]])
                    S.op("dve", lambda e, h=h, pi=pi: e.tensor_copy(RB[:, h, 0:NTA], ps[pi][:, 0:NTA]),
                         reads=[PS[pi]], writes=["RB"])
                    S.op("dve", lambda e, h=h, pi=pi: e.tensor_scalar(RB[:, h, NTA:NTA + NB], ps[pi][:, NTA:NTA + NB],
                                                                      -1.0, None, ALU.mult),
                         reads=[PS[pi]], writes=["RB"])

                def split3(dst_dram, row0, tokname):
                    S.op("dve", lambda e: e.tensor_copy(A[:, 0, :], X[:, :]), reads=["X"], writes=["A"])
                    S.op("dve", lambda e: e.tensor_tensor(R1[:, :], X[:, :], A[:, 0, :], ALU.subtract),
                         reads=["X", "A"], writes=["R1"])
                    S.op("dve", lambda e: e.tensor_copy(A[:, 1, :], R1[:, :]), reads=["R1"], writes=["A"])
                    S.op("dve", lambda e: e.tensor_tensor(R1[:, :], R1[:, :], A[:, 1, :], ALU.subtract),
                         reads=["R1", "A"], writes=["R1"])
                    S.op("dve", lambda e: e.tensor_copy(A[:, 2, :], R1[:, :]), reads=["R1"], writes=["A"])
                    S.dma("pool", dst_dram[:, row0:row0 + 3, :], A[:, :, :], reads=["A"], writes=[tokname])

                if T < L:
                    S.op("dve", lambda e: e.memset(X[:, T:L], 0.0), reads=["A"], writes=["X"])
                for ia, (t0, W) in enumerate(tilesA):
                    S.op("dve", lambda e, ia=ia, t0=t0, W=W: e.tensor_scalar(
                        X[:, t0:t0 + W], FT[:, t0:t0 + W], RS[:, ia:ia + 1], None, ALU.subtract),
                        reads=["FT", "RS", "A"], writes=["X"])
                split3(qT, 64, "qTa")
                for b in range(NB):
                    S.op("dve", lambda e, b=b: e.tensor_scalar(
                        X[:, b * 128:(b + 1) * 128], FT[:, b * 128:(b + 1) * 128], RS[:, NTA + b:NTA + b + 1], None,
                        ALU.subtract), reads=["FT", "RS", "A"], writes=["X"])
                split3(kT, 67, "kTa")
            S.barrier()

        def phase2(l):
            with ExitStack() as ph:
                Kh = [sb(ph, "a_K%d" % i, [KA, L], BF16) for i in range(2)]
                Qh = [sb(ph, "a_Q%d" % i, [KA, L], BF16) for i in range(2)]
                Vh = [sb(ph, "a_V%d" % i, [128, NB, 128], BF16) for i in range(2)]
                Gh = [sb(ph, "a_G%d" % i, [64, L], BF16) for i in range(2)]
                Bc = [sb(ph, "a_B%d" % i, [128, NTA, NB], F32) for i in range(2)]
                NPB = 4
                PT = [sb(ph, "a_P%d" % i, [128, 1024], BF16) for i in range(NPB)]
                rb = sb(ph, "a_rb", [64, 1024], F32)
                tt = sb(ph, "a_tt", [64, 1024], F32)
                amo = [sb(ph, "a_am%d" % i, [64, 1024], BF16) for i in range(2)]
                alltok = lambda name: [(name, ti) for ti in range(NT)]

                def load_head(h):
                    i = h % 2
                    S.dma("sp", Kh[i][:, :], kT[h, :, :], reads=alltok("kT") + ["kTa", ("kc", h)], writes=["Kh%d" % i])
                    S.dma("sp", Qh[i][:, :], qT[h, :, :], reads=alltok("qT") + ["qTa", ("qc", h)], writes=["Qh%d" % i])
                    S.dma("sp", Vh[i][:, :, :], vA[h, :, :, :], reads=alltok("vA") + [("vc", h)], writes=["Vh%d" % i])
                    S.dma("sp", Gh[i][:, :], sga[h, :, :], reads=alltok("sga"), writes=["Gh%d" % i])
                    for ia in range(NTA):
                        S.op("dve", lambda e, i=i, h=h, ia=ia: e.tensor_scalar(
                            Bc[i][:, ia, :], RB[:, h, NTA:NTA + NB], RB[:, h, ia:ia + 1], None, ALU.add),
                            reads=["RB"], writes=["Bc%d" % i])

                def pieces(lo, hi):
                    out = []
                    if lo < 512:
                        out.append((lo, min(hi, 512)))
                    if hi > 512:
                        out.append((max(lo, 512), hi))
                    return out

                LA = 2
                items = []
                for h in range(NH):
                    for ia, (t0, W) in enumerate(tilesA):
                        jmax = (t0 + W + 127) // 128 - 1
                        for j in range(jmax + 1):
                            items.append((h, ia, t0, W, j, jmax))

                def emit_front(n):
                    h, ia, t0, W, j, jmax = items[n]
                    i = h % 2
                    k0 = j * 128
                    q0 = max(t0, k0)
                    Wj = t0 + W - q0
                    sw = n % 2
                    pb = n % NPB
                    diag = k0 >= t0

                    def fS(e, i=i, sw=sw, k0=k0, q0=q0, Wj=Wj, diag=diag):
                        lo = 0
                        ins = None
                        if diag:
                            mw = min(128, Wj)
                            e.matmul(psw[sw][:, 0:mw], identb[:, :], negm[:, 0:mw], start=True, stop=False)
                            ins = e.matmul(psw[sw][:, 0:mw], Kh[i][:, k0:k0 + 128], Qh[i][:, q0:q0 + mw],
                                           start=False, stop=True)
                            lo = mw
                        if Wj > lo:
                            for (a_, b_) in pieces(lo, Wj):
                                ins = e.matmul(psw[sw][:, a_:b_], Kh[i][:, k0:k0 + 128], Qh[i][:, q0 + a_:q0 + b_],
                                               start=True, stop=True)
                        return ins
                    S.op("pe", fS, reads=["Kh%d" % i, "Qh%d" % i, "identb", "negm"], writes=[PS[2 * sw], PS[2 * sw + 1]])
                    S.op("act", lambda e, i=i, sw=sw, pb=pb, ia=ia, j=j, Wj=Wj: e.activation(
                        PT[pb][:, 0:Wj], psw[sw][:, 0:Wj], AF.Exp, bias=Bc[i][:, ia, j:j + 1]),
                        reads=[PS[2 * sw], PS[2 * sw + 1], "Bc%d" % i], writes=["PT%d" % pb])

                def emit_back(n):
                    h, ia, t0, W, j, jmax = items[n]
                    i = h % 2
                    if ia == 0 and j == 0:
                        conv_trickle(6)
                    if ia == 0 and j == 0 and h + 1 < NH:
                        load_head(h + 1)
                    k0 = j * 128
                    q0 = max(t0, k0)
                    c0 = q0 - t0
                    Wj = W - c0
                    pb = n % NPB
                    tcount = h * NTA + ia
                    ow = 2 + (tcount % 2)

                    def fO(e, i=i, ow=ow, pb=pb, j=j, c0=c0, Wj=Wj, jmax=jmax):
                        ins = None
                        for (a_, b_) in pieces(c0, c0 + Wj):
                            ins = e.matmul(psw[ow][:, a_:b_], Vh[i][:, j, :], PT[pb][:, a_ - c0:b_ - c0],
                                           start=(j == 0), stop=(j == jmax))
                        return ins
                    S.op("pe", fO, reads=["Vh%d" % i, "PT%d" % pb], writes=[PS[2 * ow], PS[2 * ow + 1]])
                    if j != jmax:
                        return
                    OT = [PS[2 * ow], PS[2 * ow + 1]]
                    S.op("dve", lambda e, ow=ow, W=W: e.reciprocal(rb[:, 0:W], psw[ow][64:128, 0:W]),
                         reads=OT, writes=["rb"])
                    S.op("dve", lambda e, ow=ow, W=W: e.tensor_tensor(tt[:, 0:W], psw[ow][0:64, 0:W], rb[:, 0:W], ALU.mult),
                         reads=OT + ["rb"], writes=["tt"])
                    ai = tcount % 2
                    S.op("dve", lambda e, ai=ai, i=i, t0=t0, W=W: e.tensor_tensor(
                        amo[ai][:, 0:W], tt[:, 0:W], Gh[i][:, t0:t0 + W], ALU.mult),
                        reads=["tt", "Gh%d" % i], writes=["amo%d" % ai])
                    S.dma("pool", amT[h, :, t0:t0 + W], amo[ai][:, 0:W], reads=["amo%d" % ai],
                          writes=[("amT", k) for k in range(t0 // 512, (t0 + W + 511) // 512)])

                load_head(0)
                for n in range(len(items) + LA):
                    if n < len(items):
                        emit_front(n)
                    if n - LA >= 0:
                        emit_back(n - LA)
                conv_trickle(len(conv_queue))
            S.barrier()

        def phase3a(l):
            with ExitStack() as ph:
                wop = sb(ph, "c_wop", [128, 8, D], BF16)
                woa = sb(ph, "c_woa", [64, NH, D], BF16)
                amB = [sb(ph, "c_am%d" % i, [64, NH, 512], BF16) for i in range(2)]
                pmB = [sb(ph, "c_pm%d" % i, [128, 8, 512], BF16) for i in range(2)]
                xtB = [sb(ph, "c_xt%d" % i, [128, 8, 512], F32) for i in range(2)]
                moB = [sb(ph, "c_mo%d" % i, [128, 8, 512], F32) for i in range(2)]
                sq = sb(ph, "c_sq", [128, 8, 512], BF16)
                lnv = sb(ph, "c_lnv", [128, 512], F32)
                rstd = sb(ph, "c_rstd", [128, 512], F32)
                ROT = [1, 2, 3, 4]
                S.dma("sp", wop[:], wout_b[l].rearrange("(c p) n -> p c n", p=128), reads=[("wout_b", l)], writes=["wop"])
                S.dma("sp", woa[:], wout_b[l].rearrange("(h p) n -> p h n", p=64), reads=[("wout_b", l)], writes=["woa"])
                src = res0 if l == 0 else res

                def epilogue(ti):
                    t0, W = tiles[ti]
                    bsel = ti % 2
                    mo, xt = moB[bsel], xtB[bsel]
                    mot, xtt = "mo%d" % bsel, "xt3%d" % bsel
                    S.op("dve", lambda e: e.tensor_tensor(sq[:, :, 0:W], mo[:, :, 0:W], mo[:, :, 0:W], ALU.mult),
                         reads=[mot], writes=["sq3"])
                    rms_stats(sq, W, 0, lnv, rstd, "sq3")
                    for c in range(8):
                        S.op("dve", lambda e, c=c: e.scalar_tensor_tensor(
                            mo[:, c, 0:W], mo[:, c, 0:W], gv[:, gcol(1, l, c):gcol(1, l, c) + 1], rstd[:, 0:W],
                            ALU.mult, ALU.mult), reads=[mot, "rstd", "gv"], writes=[mot])
                    S.op("dve", lambda e: e.tensor_tensor(mo[:, :, 0:W], mo[:, :, 0:W], xt[:, :, 0:W], ALU.add),
                         reads=[mot, xtt], writes=[mot])
                    S.dma("pool", res[:, :, t0:t0 + W].rearrange("c p t -> p c t"), mo[:, :, 0:W],
                          reads=[mot], writes=rtok(t0, W))

                for ti, (t0, W) in enumerate(tiles):
                    bsel = ti % 2
                    am, pm, xt, mo = amB[bsel], pmB[bsel], xtB[bsel], moB[bsel]
                    amt, pmt, xtt, mot = "am%d" % bsel, "pm3%d" % bsel, "xt3%d" % bsel, "mo%d" % bsel
                    S.dma("sp", am[:, :, 0:W], amT[:, :, t0:t0 + W].rearrange("h r t -> r h t"),
                          reads=[("amT", ti)], writes=[amt])
                    S.dma("sp", pm[:, :, 0:W], pmT[:, :, t0:t0 + W].rearrange("c p t -> p c t"),
                          reads=[("pmT", ti)], writes=[pmt])
                    S.dma("sp", xt[:, :, 0:W], src[:, :, t0:t0 + W].rearrange("c p t -> p c t"),
                          reads=rtok(t0, W), writes=[xtt])
                    for oc in range(8):
                        pi = next_ps(ROT)

                        def f(e, oc=oc, pi=pi, W=W, pm=pm, am=am):
                            ins = None
                            for c in range(8):
                                ins = e.matmul(ps[pi][:, 0:W], wop[:, c, oc * 128:(oc + 1) * 128], pm[:, c, 0:W],
                                               start=(c == 0), stop=False)
                            for h in range(NH):
                                ins = e.matmul(ps[pi][:, 0:W], woa[:, h, oc * 128:(oc + 1) * 128], am[:, h, 0:W],
                                               start=False, stop=(h == NH - 1))
                            return ins
                        S.op("pe", f, reads=["wop", "woa", pmt, amt], writes=[PS[pi]])
                        S.op("act", lambda e, oc=oc, pi=pi, W=W, mo=mo: e.activation(mo[:, oc, 0:W], ps[pi][:, 0:W], AF.Copy),
                             reads=[PS[pi]], writes=[mot])
                        if oc == 1 and ti > 0:
                            epilogue(ti - 1)
                epilogue(NT - 1)
            S.barrier()

        def phase3b(l, last):
            with ExitStack() as ph:
                wg = sb(ph, "d_wg", [128, 8, DFF], BF16)
                wu = sb(ph, "d_wu", [128, 8, DFF], BF16)
                wd = sb(ph, "d_wd", [128, NFC, D], BF16)
                xt = [sb(ph, "d_xt%d" % i, [128, 8, 256], F32) for i in range(2)]
                hn = [sb(ph, "d_hn%d" % i, [128, 8, 256], BF16) for i in range(2)]
                lnvA = sb(ph, "d_lnvA", [128, 256], F32)
                rstdA = sb(ph, "d_rstdA", [128, 256], F32)
                lnvB = sb(ph, "d_lnvB", [128, 256], F32)
                rstdB = sb(ph, "d_rstdB", [128, 256], F32)
                sg = [sb(ph, "d_sg%d" % i, [128, 256], F32) for i in range(2)]
                ff = sb(ph, "d_ff", [128, NFC, 256], BF16)
                fo = sb(ph, "d_fo", [128, 8, 256], F32)
                sq = sb(ph, "d_sq", [128, 8, 256], BF16)
                ROT = [1, 2, 3, 4, 5, 6]
                for kc in range(8):
                    S.dma("sp", wg[:, kc, :], wg_b[l, kc * 128:(kc + 1) * 128, :], reads=[("wg_b", l)], writes=["wg"])
                    S.dma("sp", wu[:, kc, :], wu_b[l, kc * 128:(kc + 1) * 128, :], reads=[("wu_b", l)], writes=["wu"])
                S.dma("sp", wd[:], wd_b[l].rearrange("(c p) n -> p c n", p=128), reads=[("wd_b", l)], writes=["wd"])

                def stats(src_sq, W, lnv, rstd, sqtok, sfx):
                    def f(e):
                        ins = None
                        for c in range(8):
                            ins = e.matmul(ps[0][:, 0:W], onesb[:, :], src_sq[:, c, 0:W], start=(c == 0), stop=(c == 7))
                        return ins
                    S.op("pe", f, reads=["onesb", sqtok], writes=[PS[0]])
                    S.op("act", lambda e: e.activation(lnv[:, 0:W], ps[0][:, 0:W], AF.Ln, bias=cst[:, 0:1]),
                         reads=[PS[0], "cst0"], writes=["lnv" + sfx])
                    S.op("act", lambda e: e.activation(rstd[:, 0:W], lnv[:, 0:W], AF.Exp, scale=-0.5),
                         reads=["lnv" + sfx], writes=["rstd" + sfx])

                def prologue(t2):
                    t0, W = tiles2[t2]
                    b = t2 % 2
                    S.dma("sp", xt[b][:, :, 0:W], res[:, :, t0:t0 + W].rearrange("c p t -> p c t"),
                          reads=rtok(t0, W), writes=["xt4%d" % b])
                    S.op("dve", lambda e: e.tensor_tensor(hn[b][:, :, 0:W], xt[b][:, :, 0:W], xt[b][:, :, 0:W], ALU.mult),
                         reads=["xt4%d" % b], writes=["hn4%d" % b])
                    stats(hn[b], W, lnvA, rstdA, "hn4%d" % b, "A")
                    for c in range(8):
                        S.op("dve", lambda e, c=c: e.scalar_tensor_tensor(
                            hn[b][:, c, 0:W], xt[b][:, c, 0:W], gv[:, gcol(2, l, c):gcol(2, l, c) + 1], rstdA[:, 0:W],
                            ALU.mult, ALU.mult), reads=["xt4%d" % b, "rstdA", "gv", PS[0]], writes=["hn4%d" % b])

                def gate_up(t2, fc0, fc1):
                    t0, W = tiles2[t2]
                    b = t2 % 2
                    for fc in range(fc0, fc1):
                        pg = next_ps(ROT)
                        pu = next_ps(ROT)

                        def fg(e, fc=fc, pg=pg):
                            ins = None
                            for kc in range(8):
                                ins = e.matmul(ps[pg][:, 0:W], wg[:, kc, fc * 128:(fc + 1) * 128], hn[b][:, kc, 0:W],
                                               start=(kc == 0), stop=(kc == 7))
                            return ins

                        def fu(e, fc=fc, pu=pu):
                            ins = None
                            for kc in range(8):
                                ins = e.matmul(ps[pu][:, 0:W], wu[:, kc, fc * 128:(fc + 1) * 128], hn[b][:, kc, 0:W],
                                               start=(kc == 0), stop=(kc == 7))
                            return ins
                        S.op("pe", fg, reads=["wg", "hn4%d" % b], writes=[PS[pg]])
                        S.op("pe", fu, reads=["wu", "hn4%d" % b], writes=[PS[pu]])
                        sgi = fc % 2
                        S.op("act", lambda e, sgi=sgi, pg=pg: e.activation(sg[sgi][:, 0:W], ps[pg][:, 0:W], AF.Silu),
                             reads=[PS[pg]], writes=["sg%d" % sgi])
                        S.op("dve", lambda e, fc=fc, sgi=sgi, pu=pu: e.tensor_tensor(
                            ff[:, fc, 0:W], sg[sgi][:, 0:W], ps[pu][:, 0:W], ALU.mult),
                            reads=["sg%d" % sgi, PS[pu]], writes=["ff"])

                def down(t2):
                    t0, W = tiles2[t2]
                    for oc in range(8):
                        pi = next_ps(ROT)

                        def fd(e, oc=oc, pi=pi):
                            ins = None
                            for fc in range(NFC):
                                ins = e.matmul(ps[pi][:, 0:W], wd[:, fc, oc * 128:(oc + 1) * 128], ff[:, fc, 0:W],
                                               start=(fc == 0), stop=(fc == NFC - 1))
                            return ins
                        S.op("pe", fd, reads=["wd", "ff"], writes=[PS[pi]])
                        S.op("act", lambda e, oc=oc, pi=pi: e.activation(fo[:, oc, 0:W], ps[pi][:, 0:W], AF.Copy),
                             reads=[PS[pi]], writes=["fo"])

                EPI = "pool"

                def epiA(t2):
                    t0, W = tiles2[t2]
                    S.op(EPI, lambda e: e.tensor_tensor(sq[:, :, 0:W], fo[:, :, 0:W], fo[:, :, 0:W], ALU.mult),
                         reads=["fo"], writes=["sq4"])

                def epiB(t2):
                    t0, W = tiles2[t2]
                    stats(sq, W, lnvB, rstdB, "sq4", "B")

                def epiC(t2):
                    t0, W = tiles2[t2]
                    b = t2 % 2
                    for c in range(8):
                        S.op("dve", lambda e, c=c: e.scalar_tensor_tensor(
                            fo[:, c, 0:W], fo[:, c, 0:W], gv[:, gcol(3, l, c):gcol(3, l, c) + 1], rstdB[:, 0:W],
                            ALU.mult, ALU.mult), reads=["fo", "rstdB", "gv"], writes=["fo"])
                    S.op(EPI, lambda e: e.tensor_tensor(fo[:, :, 0:W], fo[:, :, 0:W], xt[b][:, :, 0:W], ALU.add),
                         reads=["fo", "xt4%d" % b], writes=["fo"])
                    if not last:
                        S.dma("pool", res[:, :, t0:t0 + W].rearrange("c p t -> p c t"), fo[:, :, 0:W],
                              reads=["fo"], writes=rtok(t0, W))
                    else:
                        a = max(t0, META)
                        bb = min(t0 + W, T)
                        if bb > a:
                            S.dma("pool", outT[:, :, a - META:bb - META].rearrange("c p t -> p c t"),
                                  fo[:, :, a - t0:bb - t0], reads=["fo"], writes=[("out", t2)])

                n2 = len(tiles2)
                prologue(0)
                for t2 in range(n2):
                    if t2 > 0:
                        epiA(t2 - 1)
                    gate_up(t2, 0, 4)
                    if t2 > 0:
                        epiB(t2 - 1)
                        epiC(t2 - 1)
                    gate_up(t2, 4, NFC)
                    if t2 + 1 < n2:
                        prologue(t2 + 1)
                    down(t2)
                epiA(n2 - 1)
                epiB(n2 - 1)
                epiC(n2 - 1)
            S.barrier()

        S.barrier()
        for l in range(DEPTH):
            phase1(l)
            if stop_after in ("p1", "%d:p1" % l):
                break
            if l + 1 < DEPTH:
                convert_layer(l + 1, False)
            phaseF(l)
            if stop_after in ("pf", "%d:pf" % l):
                break
            phase2(l)
            if stop_after in ("p2", "%d:p2" % l):
                break
            phase3a(l)
            if stop_after in ("p3a", "%d:p3a" % l):
                break
            phase3b(l, last=(l == DEPTH - 1))
        S.emit(st)
    return nc


def _host_consts(DEPTH):
    negm = np.where(np.arange(128)[:, None] > np.arange(128)[None, :], -30000.0, 0.0).astype(ml_dtypes.bfloat16)
    identb = np.eye(128, dtype=np.float32).astype(ml_dtypes.bfloat16)
    umat = (np.arange(128)[:, None] <= np.arange(128)[None, :]).astype(np.float32)
    sel = np.zeros((16, 16, 128), np.float32)
    for h in range(16):
        sel[h, h, :] = 1.0
    sel = sel.reshape(16, 16 * 128)
    invc = np.zeros((128, 4, 16), np.float32)
    for g in range(4):
        w = 2 << g
        invc[:, g, :] = 1.0 / np.minimum(np.arange(16) + 1, w).astype(np.float32)[None, :]
    return negm, identb, umat, sel, invc.reshape(128, 64)


def make_in_maps(inputs, SEQ, DEPTH, n_cores):
    T = SEQ + META
    L = ((T + 127) // 128) * 128
    x = np.asarray(inputs["x"], np.float32)
    B = x.shape[0]
    meta = np.asarray(inputs["meta_tokens"], np.float32)
    negm, identb, umat, sel, invc = _host_consts(DEPTH)
    kinds = ["norm_mix_pre", "norm_mix_post", "norm_ffn_pre", "norm_ffn_post", "pool_scale"]
    gv = np.stack([np.asarray(inputs[k], np.float32) for k in kinds], 0)
    gv = gv.reshape(5, DEPTH, 8, 128).transpose(3, 0, 1, 2).reshape(128, 5 * DEPTH * 8)
    bf = np.asarray(inputs["b_forget"], np.float32)
    bfb = np.broadcast_to(bf[None, :, None, :], (128, DEPTH, 4, 16)).reshape(128, DEPTH * 64)
    common = {
        "gv": np.ascontiguousarray(gv), "bfb": np.ascontiguousarray(bfb),
        "negm": negm, "identb": identb, "umat": umat, "sel": sel, "invc": invc,
        "cneg": np.full((3, L), -1.0, ml_dtypes.bfloat16), "cpos": np.full((3, L), 1.0, ml_dtypes.bfloat16),
        "cv": np.full((128, L // 128, 64), 1.0, ml_dtypes.bfloat16),
        "w_in": np.asarray(inputs["w_in"], np.float32), "w_pool": np.asarray(inputs["w_pool"], np.float32),
        "w_out": np.asarray(inputs["w_out"], np.float32), "w_ffn_gate": np.asarray(inputs["w_ffn_gate"], np.float32),
        "w_ffn_up": np.asarray(inputs["w_ffn_up"], np.float32), "w_ffn_down": np.asarray(inputs["w_ffn_down"], np.float32),
    }
    maps = []
    for c in range(n_cores):
        b = c % B
        r = np.zeros((L, D), np.float32)
        r[0:META] = meta
        r[META:T] = x[b]
        r0 = np.ascontiguousarray(r.T.reshape(8, 128, L))
        m = dict(common)
        m["res0"] = r0
        maps.append(m)
    return maps


_NC_CACHE = {}


def kernel(x, meta_tokens, norm_mix_pre, norm_mix_post, norm_ffn_pre, norm_ffn_post,
           w_in, b_forget, w_pool, pool_scale, w_out, w_ffn_gate, w_ffn_up, w_ffn_down):
    inputs = dict(x=x, meta_tokens=meta_tokens, norm_mix_pre=norm_mix_pre, norm_mix_post=norm_mix_post,
                  norm_ffn_pre=norm_ffn_pre, norm_ffn_post=norm_ffn_post, w_in=w_in, b_forget=b_forget,
                  w_pool=w_pool, pool_scale=pool_scale, w_out=w_out, w_ffn_gate=w_ffn_gate,
                  w_ffn_up=w_ffn_up, w_ffn_down=w_ffn_down)
    x = np.asarray(x)
    B, SEQ, _ = x.shape
    DEPTH = np.asarray(w_in).shape[0]
    key = (SEQ, DEPTH)
    if key not in _NC_CACHE:
        _NC_CACHE[key] = build_nc(SEQ, DEPTH)
    nc = _NC_CACHE[key]
    maps = make_in_maps(inputs, SEQ, DEPTH, N_CORES)
    res = run_bass_kernel_spmd(nc, maps, core_ids=list(range(N_CORES)))
    out = np.empty((B, SEQ, D), np.float32)
    for b in range(B):
        o = np.asarray(res.results[b]["outT"], np.float32)
        out[b] = o.reshape(D, SEQ).T
    return out
```

```python
import numpy as np
import ml_dtypes
from contextlib import ExitStack
import concourse.bass as bass
import concourse.mybir as mybir
from concourse.bass_utils import run_bass_kernel_spmd

F32 = mybir.dt.float32
BF16 = mybir.dt.bfloat16
ALU = mybir.AluOpType
AF = mybir.ActivationFunctionType

D = 1024
NH = 16
HD = 64
DFF = 2816
NFC = DFF // 128
NIN = 6160
META = 16
KA = 70
EPS = 1e-6
C_POOL, C_Q, C_K, C_V, C_F, C_GP, C_GA = 0, 1024, 2048, 3072, 4096, 4112, 5136
N_CORES = 8
_DBG = False


class _Op:
    __slots__ = ("eng", "fn", "deps", "signal", "sigval", "is_dma", "lane", "laneval", "id")


class Sched:
    ENGS = ("pe", "act", "dve", "pool", "sp")

    def __init__(self, nc, n_lanes=8):
        self.nc = nc
        self.ops = []
        self.by_eng = {e: [] for e in self.ENGS}
        self.last_write = {}
        self.readers = {}
        self.n_lanes = n_lanes
        self.lane_next = {}
        self.lane_count = {}
        self.lane_last = {}
        self.pending_barrier = {}

    def _deps(self, eng, reads, writes):
        deps = set()
        for r in reads:
            w = self.last_write.get(r)
            if w is not None:
                deps.add(w)
        for t in writes:
            w = self.last_write.get(t)
            if w is not None:
                deps.add(w)
            rs = self.readers.get(t)
            if rs:
                deps.update(rs)
        pb = self.pending_barrier.pop(eng, None)
        if pb:
            deps.update(pb)
        return deps

    def _commit(self, oid, reads, writes):
        for r in reads:
            self.readers.setdefault(r, []).append(oid)
        for t in writes:
            self.last_write[t] = oid
            self.readers[t] = []

    def op(self, eng, fn, reads=(), writes=()):
        o = _Op()
        o.eng = eng
        o.fn = fn
        o.is_dma = False
        o.signal = False
        o.sigval = None
        o.id = len(self.ops)
        o.deps = self._deps(eng, reads, writes)
        self.ops.append(o)
        self.by_eng[eng].append(o)
        self._commit(o.id, reads, writes)
        return o.id

    def dma(self, queue, out, in_, reads=(), writes=()):
        o = _Op()
        o.eng = queue
        o.is_dma = True
        o.signal = True
        o.sigval = None
        o.id = len(self.ops)
        o.deps = self._deps(queue, reads, writes)
        lane = self.lane_next.get(queue, 0)
        self.lane_next[queue] = (lane + 1) % self.n_lanes
        key = (queue, lane)
        prev = self.lane_last.get(key)
        if prev is not None:
            o.deps.add(prev)
        self.lane_count[key] = self.lane_count.get(key, 0) + 1
        o.lane = key
        o.laneval = 16 * self.lane_count[key]
        self.lane_last[key] = o.id
        o.fn = lambda e, out=out, in_=in_: e.dma_start(out=out, in_=in_)
        self.ops.append(o)
        self.by_eng[queue].append(o)
        self._commit(o.id, reads, writes)
        return o.id

    def dma_custom(self, queue, fn, reads=(), writes=()):
        oid = self.dma(queue, None, None, reads=reads, writes=writes)
        self.ops[oid].fn = fn
        return oid

    def barrier(self):
        pend = set()
        for e in self.ENGS:
            if self.by_eng[e]:
                pend.add(self.by_eng[e][-1].id)
        for oid in self.lane_last.values():
            pend.add(oid)
        self.pending_barrier = {e: set(pend) for e in self.ENGS}

    def emit(self, stack):
        nc = self.nc
        ops = self.ops
        for o in ops:
            for d in o.deps:
                p = ops[d]
                if p.is_dma:
                    continue
                if p.eng != o.eng or p.eng != "pe":
                    p.signal = True
        for e in self.ENGS:
            lst = [o for o in self.by_eng[e] if not o.is_dma]
            if lst:
                lst[-1].signal = True
        for e in self.ENGS:
            c = 0
            for o in self.by_eng[e]:
                if o.is_dma:
                    continue
                if o.signal:
                    c += 1
                    o.sigval = c
        esem = {e: stack.enter_context(nc.semaphore("s_" + e)) for e in self.ENGS}
        lsem = {key: stack.enter_context(nc.semaphore("l_%s%d" % key)) for key in self.lane_count}
        final_waits = [(lsem[key], 16 * cnt) for key, cnt in self.lane_count.items()]
        for e in self.ENGS:
            lst = [o for o in self.by_eng[e] if not o.is_dma and o.signal]
            if lst:
                final_waits.append((esem[e], lst[-1].sigval))

        def run(e, engine):
            waited = {}
            for o in self.by_eng[e]:
                need = {}
                for d in o.deps:
                    p = ops[d]
                    if p.is_dma:
                        s, v = lsem[p.lane], p.laneval
                    else:
                        if p.eng == e and e == "pe":
                            continue
                        s, v = esem[p.eng], p.sigval
                    if v > need.get(s, 0):
                        need[s] = v
                for s, v in need.items():
                    if waited.get(s, 0) < v:
                        engine.wait_ge(s, v)
                        waited[s] = v
                ins = o.fn(engine)
                if o.is_dma:
                    ins.then_inc(lsem[o.lane], 16)
                elif o.signal:
                    ins.then_inc(esem[e], 1)
            if e == "sp":
                for s, v in final_waits:
                    engine.wait_ge(s, v)

        block = stack.enter_context(nc.Block())
        if self.by_eng["pe"]:
            @block.tensor
            def _(eng):
                run("pe", eng)
        if self.by_eng["act"]:
            @block.scalar
            def _(eng):
                run("act", eng)
        if self.by_eng["dve"]:
            @block.vector
            def _(eng):
                run("dve", eng)
        if self.by_eng["pool"]:
            @block.gpsimd
            def _(eng):
                run("pool", eng)

        @block.sync
        def _(eng):
            run("sp", eng)


def _tiles(L, w):
    out = []
    t = 0
    while t < L:
        ww = min(w, L - t)
        out.append((t, ww))
        t += ww
    return out


def build_nc(SEQ, DEPTH, stop_after=None):
    T = SEQ + META
    L = ((T + 127) // 128) * 128
    NB = L // 128
    tiles = _tiles(L, 512)
    NT = len(tiles)
    tiles2 = _tiles(L, 256)
    tilesA = [(0, 128)] + [(128 + t0, w) for (t0, w) in _tiles(T - 128, 1024)]
    NTA = len(tilesA)

    nc = bass.Bass("TRN2", target_bir_lowering=False)

    def din(name, shape, dt=F32):
        return nc.dram_tensor(name, list(shape), dt, kind="ExternalInput").ap()

    def dscr(name, shape, dt):
        return nc.dram_tensor(name, list(shape), dt, kind="Internal").ap()

    res0 = din("res0", [8, 128, L])
    gv_d = din("gv", [128, 5 * DEPTH * 8])
    bfb_d = din("bfb", [128, DEPTH * 64])
    negm_d = din("negm", [128, 128], BF16)
    identb_d = din("identb", [128, 128], BF16)
    U_d = din("umat", [128, 128])
    sel_d = din("sel", [16, 16 * 128])
    invc_d = din("invc", [128, 64])
    cneg_d = din("cneg", [3, L], BF16)
    cpos_d = din("cpos", [3, L], BF16)
    cv_d = din("cv", [128, NB, 64], BF16)
    w_in_d = din("w_in", [DEPTH, D, NIN])
    w_pool_d = din("w_pool", [DEPTH, 4, 256, 256])
    w_out_d = din("w_out", [DEPTH, D, D])
    w_g_d = din("w_ffn_gate", [DEPTH, D, DFF])
    w_u_d = din("w_ffn_up", [DEPTH, D, DFF])
    w_d_d = din("w_ffn_down", [DEPTH, DFF, D])
    outT = nc.dram_tensor("outT", [8, 128, SEQ], F32, kind="ExternalOutput").ap()

    res = dscr("res", [8, 128, L], F32)
    qT = dscr("qT", [NH, KA, L], BF16)
    kT = dscr("kT", [NH, KA, L], BF16)
    vA = dscr("vA", [NH, 128, NB, 128], BF16)
    sga = dscr("sga", [NH, 64, L], BF16)
    pmT = dscr("pmT", [8, 128, L], BF16)
    amT = dscr("amT", [NH, 64, L], BF16)
    win_b = dscr("win_b", [DEPTH, D, NIN], BF16)
    wpool_b = dscr("wpool_b", [DEPTH, 4, 256, 256], BF16)
    wout_b = dscr("wout_b", [DEPTH, D, D], BF16)
    wg_b = dscr("wg_b", [DEPTH, D, DFF], BF16)
    wu_b = dscr("wu_b", [DEPTH, D, DFF], BF16)
    wd_b = dscr("wd_b", [DEPTH, DFF, D], BF16)

    WIN_GROUPS = [(C_POOL, C_Q), (C_GP, C_GA), (C_GA, NIN), (C_Q, C_K), (C_K, C_V), (C_V, C_GP)]

    def wtok(col0):
        for (c0, c1) in WIN_GROUPS:
            if c0 <= col0 < c1:
                return ("win", c0)
        raise ValueError(col0)

    def gcol(kind, l, c):
        return (kind * DEPTH + l) * 8 + c

    with ExitStack() as st:
        S = Sched(nc)

        uniq = [0]

        def sb(ctx, name, shape, dt):
            uniq[0] += 1
            return ctx.enter_context(nc.sbuf_tensor("%s_%d" % (name, uniq[0]), list(shape), dt))

        gv = sb(st, "gv_sb", [128, 5 * DEPTH * 8], F32)
        bfb = sb(st, "bfb_sb", [128, DEPTH * 64], F32)
        negm = sb(st, "negm_sb", [128, 128], BF16)
        identb = sb(st, "identb_sb", [128, 128], BF16)
        umat = sb(st, "umat_sb", [128, 128], F32)
        invc = sb(st, "invc_sb", [128, 64], F32)
        cst = sb(st, "cst_sb", [128, 4], F32)
        onesb = sb(st, "onesb_sb", [128, 128], BF16)
        onesf = sb(st, "onesf_sb", [128, 1], F32)
        LF = sb(st, "LF_sb", [128, NB, 16], F32)
        RB = sb(st, "RB_sb", [128, NH, NTA + NB], F32)
        uh = sb(st, "uh_sb", [128, 8, 16], F32)
        zpad = sb(st, "zpad_sb", [64, max(L - T, 1)], BF16)
        psw = [st.enter_context(nc.psum_tensor("psw%d" % i, [128, 1024], F32)) for i in range(4)]
        ps = [psw[i // 2][:, (i % 2) * 512:(i % 2 + 1) * 512] for i in range(8)]
        PS = ["ps%d" % i for i in range(8)]

        S.dma("sp", gv[:], gv_d, writes=["gv"])
        S.dma("sp", bfb[:], bfb_d, writes=["bfb"])
        S.dma("sp", negm[:], negm_d, writes=["negm"])
        S.dma("sp", identb[:], identb_d, writes=["identb"])
        S.dma("sp", umat[:], U_d, writes=["umat"])
        S.dma("sp", invc[:], invc_d, writes=["invc"])
        S.op("dve", lambda e: e.memset(cst[:, 0:1], EPS), writes=["cst0"])
        S.op("dve", lambda e: e.memset(cst[:, 1:2], 1.0), writes=["cst1"])
        S.op("dve", lambda e: e.memset(onesb[:], 1.0 / 1024.0), writes=["onesb"])
        S.op("dve", lambda e: e.memset(onesf[:], 1.0), writes=["onesf"])

        def const_rows():
            for h in range(NH):
                S.dma("pool", qT[h, 67:70, :], cneg_d, writes=[("qc", h)])
                S.dma("pool", kT[h, 64:67, :], cpos_d, writes=[("kc", h)])
                S.dma("pool", vA[h, :, :, 64:128], cv_d, writes=[("vc", h)])
        def late_consts():
            const_rows()
            if T < L:
                S.op("dve", lambda e: e.memset(zpad[:], 0.0), writes=["zpad"])
                for h in range(NH):
                    S.dma("pool", amT[h, :, T:L], zpad[:], reads=["zpad"],
                          writes=[("amT", k) for k in range(T // 512, (L + 511) // 512)])

        conv_queue = []

        def convert_layer(l, now):
            jobs = []
            for (c0, c1) in WIN_GROUPS:
                for r0 in range(0, D, 512):
                    jobs.append((win_b[l, r0:r0 + 512, c0:c1], w_in_d[l, r0:r0 + 512, c0:c1], ("win_b", l, c0), True))
            jobs.append((wpool_b[l].rearrange("g a b -> (g a) b"), w_pool_d[l].rearrange("g a b -> (g a) b"),
                         ("wpool_b", l), True))
            for r0 in range(0, D, 256):
                jobs.append((wout_b[l, r0:r0 + 256, :], w_out_d[l, r0:r0 + 256, :], ("wout_b", l), False))
            for r0 in range(0, D, 128):
                jobs.append((wg_b[l, r0:r0 + 128, :], w_g_d[l, r0:r0 + 128, :], ("wg_b", l), False))
                jobs.append((wu_b[l, r0:r0 + 128, :], w_u_d[l, r0:r0 + 128, :], ("wu_b", l), False))
            for r0 in range(0, DFF, 256):
                jobs.append((wd_b[l, r0:r0 + 256, :], w_d_d[l, r0:r0 + 256, :], ("wd_b", l), False))
            for (o_, i_, tok, first) in jobs:
                if now and first:
                    S.dma("pool", o_, i_, writes=[tok])
                else:
                    conv_queue.append(lambda o_=o_, i_=i_, tok=tok: S.dma("pool", o_, i_, writes=[tok]))

        def conv_trickle(n):
            for _ in range(n):
                if conv_queue:
                    conv_queue.pop(0)()

        convert_layer(0, True)
        late_consts()

        def rtok(t0, W):
            return [("res", k) for k in range(t0 // 256, (t0 + W + 255) // 256)]

        rot = {"i": 0}

        def next_ps(choices):
            i = choices[rot["i"] % len(choices)]
            rot["i"] += 1
            return i

        def rms_stats(src_sq, W, ps_i, lnv, rstd, sqtok):
            def f(e):
                ins = None
                for c in range(8):
                    ins = e.matmul(ps[ps_i][:, 0:W], onesb[:, :], src_sq[:, c, 0:W], start=(c == 0), stop=(c == 7))
                return ins
            S.op("pe", f, reads=["onesb", sqtok], writes=[PS[ps_i]])
            S.op("act", lambda e: e.activation(lnv[:, 0:W], ps[ps_i][:, 0:W], AF.Ln, bias=cst[:, 0:1]),
                 reads=[PS[ps_i], "cst0"], writes=["lnv"])
            S.op("act", lambda e: e.activation(rstd[:, 0:W], lnv[:, 0:W], AF.Exp, scale=-0.5),
                 reads=["lnv"], writes=["rstd"])

        def phase1(l):
            src = res0 if l == 0 else res
            with ExitStack() as ph:
                win = sb(ph, "win", [128, 8, NIN], BF16)
                wpl = sb(ph, "wpl", [128, 4, 2, 256], BF16)
                xt = sb(ph, "p1_xt", [128, 8, 512], F32)
                hnB = [sb(ph, "p1_hn%d" % i, [128, 8, 512], BF16) for i in range(2)]
                lnv = sb(ph, "p1_lnv", [128, 512], F32)
                ug = [sb(ph, "p1_ug%d" % i, [128, 2, 528], F32) for i in range(4)]
                wa = sb(ph, "p1_wa", [128, 2, 528], F32)
                wb = sb(ph, "p1_wb", [128, 2, 528], F32)
                dsb = [sb(ph, "p1_dsb%d" % i, [128, 2, 512], BF16) for i in range(4)]
                sgp = [sb(ph, "p1_sgp%d" % i, [128, 2, 512], BF16) for i in range(4)]
                pmg = [sb(ph, "p1_pm%d" % i, [128, 2, 512], BF16) for i in range(2)]
                stg = [sb(ph, "p1_stg%d" % i, [128, 4, 512], BF16) for i in range(2)]
                vsb = [sb(ph, "p1_vsb%d" % i, [128, 16, 64], BF16) for i in range(2)]
                tf = sb(ph, "p1_tf", [128, 64], F32)
                rstd = lnv
                ROT = [1, 2, 3, 4, 5, 6]
                if l == 0 and _DBG:
                    print("P1 sbuf bytes remaining", nc.sbuf_bytes_remaining)

                for (c0, c1) in WIN_GROUPS:
                    S.dma("sp", win[:, :, c0:c1], win_b[l, :, c0:c1].rearrange("(k p) n -> p k n", p=128),
                          reads=[("win_b", l, c0)], writes=[("win", c0)])
                S.dma("sp", wpl[:], wpool_b[l].rearrange("g (k p) n -> p g k n", p=128),
                      reads=[("wpool_b", l)], writes=["wpl"])
                S.op("dve", lambda e: e.memset(uh[:], 0.0), writes=["uh"])
                stg_i = [0]
                vcnt = [0]

                def prologue(ti):
                    t0, W = tiles[ti]
                    hn = hnB[ti % 2]
                    hnt = "hn%d" % (ti % 2)
                    S.dma("sp", xt[:, :, 0:W], src[:, :, t0:t0 + W].rearrange("c p t -> p c t"),
                          reads=rtok(t0, W), writes=["xt"])
                    S.op("dve", lambda e: e.tensor_tensor(hn[:, :, 0:W], xt[:, :, 0:W], xt[:, :, 0:W], ALU.mult),
                         reads=["xt"], writes=[hnt])

                    def fst(e):
                        ins = None
                        for c in range(8):
                            ins = e.matmul(ps[0][:, 0:W], onesb[:, :], hn[:, c, 0:W], start=(c == 0), stop=(c == 7))
                        return ins
                    S.op("pe", fst, reads=["onesb", hnt], writes=[PS[0]])
                    S.op("act", lambda e: e.activation(lnv[:, 0:W], ps[0][:, 0:W], AF.Ln, bias=cst[:, 0:1]),
                         reads=[PS[0], "cst0"], writes=["lnv"])
                    S.op("act", lambda e: e.activation(lnv[:, 0:W], lnv[:, 0:W], AF.Exp, scale=-0.5),
                         reads=["lnv"], writes=["lnv"])
                    for c in range(8):
                        S.op("dve", lambda e, c=c: e.scalar_tensor_tensor(
                            hn[:, c, 0:W], xt[:, c, 0:W], gv[:, gcol(0, l, c):gcol(0, l, c) + 1], rstd[:, 0:W],
                            ALU.mult, ALU.mult), reads=["xt", "lnv", "gv", PS[0]], writes=[hnt])

                prologue(0)
                for ti, (t0, W) in enumerate(tiles):
                    nsb = W // 128
                    hn = hnB[ti % 2]
                    hnt = "hn%d" % (ti % 2)

                    def proj(col0, ps_i, W=W, hn=hn, hnt=hnt):
                        def f(e):
                            ins = None
                            for kc in range(8):
                                ins = e.matmul(ps[ps_i][:, 0:W], win[:, kc, col0:col0 + 128], hn[:, kc, 0:W],
                                               start=(kc == 0), stop=(kc == 7))
                            return ins
                        S.op("pe", f, reads=[wtok(col0), hnt], writes=[PS[ps_i]])

                    for g in range(4):
                        u = ug[g]
                        ut = "ug%d" % g
                        S.op("dve", lambda e, u=u, g=g: e.tensor_copy(u[:, :, 0:16], uh[:, 2 * g:2 * g + 2, :]),
                             reads=["uh"], writes=[ut])
                        for k in range(2):
                            pi = next_ps(ROT)
                            proj(C_POOL + (2 * g + k) * 128, pi)
                            S.op("act", lambda e, u=u, k=k, pi=pi, W=W: e.activation(
                                u[:, k, 16:16 + W], ps[pi][:, 0:W], AF.Copy), reads=[PS[pi]], writes=[ut])
                        S.op("dve", lambda e, u=u, g=g, W=W: e.tensor_copy(uh[:, 2 * g:2 * g + 2, :], u[:, :, W:W + 16]),
                             reads=[ut], writes=["uh"])
                    for g in range(4):
                        for k in range(2):
                            pi = next_ps(ROT)
                            proj(C_GP + (2 * g + k) * 128, pi)
                            S.op("act", lambda e, g=g, k=k, pi=pi, W=W: e.activation(
                                sgp[g][:, k, 0:W], ps[pi][:, 0:W], AF.Sigmoid), reads=[PS[pi]], writes=["sgp%d" % g])

                    if ti + 1 < NT:
                        prologue(ti + 1)

                    for g in range(4):
                        w = 2 << g
                        u = ug[g]
                        ut = "ug%d" % g
                        cur, curt = u, ut
                        lo = 0
                        sh = 1
                        bufs = [(wa, "wa"), (wb, "wb")]
                        bi = 0
                        while sh < w:
                            dst, dstt = bufs[bi]
                            bi ^= 1
                            lo2 = lo + sh
                            S.op("dve", lambda e, dst=dst, cur=cur, lo2=lo2, sh=sh, W=W: e.tensor_tensor(
                                dst[:, :, lo2:16 + W], cur[:, :, lo2:16 + W], cur[:, :, lo2 - sh:16 + W - sh], ALU.add),
                                reads=[curt], writes=[dstt])
                            cur, curt = dst, dstt
                            lo = lo2
                            sh *= 2
                        d_ = dsb[g]
                        dt_ = "dsb%d" % g
                        S.op("dve", lambda e, d_=d_, cur=cur, u=u, w=w, W=W: e.scalar_tensor_tensor(
                            d_[:, :, 0:W], cur[:, :, 16:16 + W], 1.0 / w, u[:, :, 16:16 + W], ALU.mult, ALU.subtract),
                            reads=[curt, ut], writes=[dt_])
                        if ti == 0:
                            for k in range(2):
                                S.op("dve", lambda e, cur=cur, g=g, k=k: e.tensor_tensor(
                                    cur[:, k, 16:32], cur[:, k, 16:32], invc[:, g * 16:(g + 1) * 16], ALU.mult),
                                    reads=[curt, "invc", dt_], writes=[curt])
                            S.op("dve", lambda e, d_=d_, cur=cur, u=u: e.tensor_tensor(
                                d_[:, :, 0:16], cur[:, :, 16:32], u[:, :, 16:32], ALU.subtract),
                                reads=[curt, ut], writes=[dt_])

                    def fm_group(col0, dst, kind, tokname, W=W, t0=t0, ti=ti):
                        for half in range(2):
                            sgb = stg[stg_i[0] % 2]
                            sgtok = "stg%d" % (stg_i[0] % 2)
                            stg_i[0] += 1
                            for k in range(4):
                                c = half * 4 + k
                                pi = next_ps(ROT)
                                proj(col0 + c * 128, pi)
                                if kind == "q":
                                    S.op("act", lambda e, sgb=sgb, k=k, pi=pi: e.activation(
                                        sgb[:, k, 0:W], ps[pi][:, 0:W], AF.Copy, scale=0.125),
                                        reads=[PS[pi]], writes=[sgtok])
                                elif kind == "k":
                                    S.op("act", lambda e, sgb=sgb, k=k, pi=pi: e.activation(
                                        sgb[:, k, 0:W], ps[pi][:, 0:W], AF.Copy), reads=[PS[pi]], writes=[sgtok])
                                else:
                                    S.op("act", lambda e, sgb=sgb, k=k, pi=pi: e.activation(
                                        sgb[:, k, 0:W], ps[pi][:, 0:W], AF.Sigmoid), reads=[PS[pi]], writes=[sgtok])
                            h0 = half * 8
                            for par in range(2):
                                dview = dst[h0 + par:h0 + 8:2, 0:64, t0:t0 + W].rearrange("c r t -> r c t")
                                S.dma("pool", dview, sgb[par * 64:(par + 1) * 64, :, 0:W],
                                      reads=[sgtok], writes=[(tokname, ti)])
                    fm_group(C_GA, sga, "g", "sga")
                    fm_group(C_Q, qT, "q", "qT")
                    fm_group(C_K, kT, "k", "kT")

                    b0 = t0 // 128
                    for s_ in range(nsb):
                        vb = vsb[vcnt[0] % 2]
                        vtok = "vsb%d" % (vcnt[0] % 2)
                        vcnt[0] += 1
                        for half in range(2):
                            pi = next_ps(ROT)

                            def f(e, s_=s_, half=half, pi=pi, hn=hn):
                                ins = None
                                for kc in range(8):
                                    ins = e.matmul(ps[pi][:, :], hn[:, kc, s_ * 128:(s_ + 1) * 128],
                                                   win[:, kc, C_V + half * 512:C_V + (half + 1) * 512],
                                                   start=(kc == 0), stop=(kc == 7))
                                return ins
                            S.op("pe", f, reads=[wtok(C_V), hnt], writes=[PS[pi]])
                            S.op("act", lambda e, vb=vb, half=half, pi=pi: e.activation(
                                vb[:, half * 8:(half + 1) * 8, :],
                                ps[pi][:, :].rearrange("p (h d) -> p h d", d=64), AF.Copy),
                                reads=[PS[pi]], writes=[vtok])

                        def ff_(e, s_=s_, hn=hn):
                            ins = None
                            for kc in range(8):
                                ins = e.matmul(ps[7][:, s_ * 16:(s_ + 1) * 16], hn[:, kc, s_ * 128:(s_ + 1) * 128],
                                               win[:, kc, C_F:C_F + 16], start=(kc == 0), stop=(kc == 7))
                            return ins
                        S.op("pe", ff_, reads=[wtok(C_F), hnt], writes=[PS[7]])
                        S.dma("pool", vA[:, :, b0 + s_, 0:64].rearrange("h p d -> p h d"),
                              vb[:, :, :], reads=[vtok], writes=[("vA", ti)])

                    for g in range(4):
                        d_ = dsb[g]
                        for o2 in range(2):
                            pi = next_ps(ROT)

                            def f(e, g=g, o2=o2, pi=pi, d_=d_, W=W):
                                ins = None
                                for k2 in range(2):
                                    ins = e.matmul(ps[pi][:, 0:W], wpl[:, g, k2, o2 * 128:(o2 + 1) * 128],
                                                   d_[:, k2, 0:W], start=(k2 == 0), stop=(k2 == 1))
                                return ins
                            S.op("pe", f, reads=["wpl", "dsb%d" % g], writes=[PS[pi]])
                            c = 2 * g + o2
                            S.op("dve", lambda e, c=c, pi=pi, g=g, o2=o2, W=W: e.scalar_tensor_tensor(
                                pmg[g % 2][:, o2, 0:W], ps[pi][:, 0:W], gv[:, gcol(4, l, c):gcol(4, l, c) + 1],
                                sgp[g][:, o2, 0:W], ALU.mult, ALU.mult),
                                reads=[PS[pi], "sgp%d" % g, "gv"], writes=["pm%d" % (g % 2)])
                        S.dma("pool", pmT[2 * g:2 * g + 2, :, t0:t0 + W].rearrange("c p t -> p c t"),
                              pmg[g % 2][:, :, 0:W], reads=["pm%d" % (g % 2)], writes=[("pmT", ti)])

                    S.op("dve", lambda e, nsb=nsb: e.tensor_tensor(
                        tf[:, 0:nsb * 16], ps[7][:, 0:nsb * 16], bfb[:, l * 64:l * 64 + nsb * 16], ALU.add),
                        reads=[PS[7], "bfb"], writes=["tf"])
                    S.op("act", lambda e, nsb=nsb: e.activation(tf[:, 0:nsb * 16], tf[:, 0:nsb * 16], AF.Exp, scale=-1.0),
                         reads=["tf"], writes=["tf"])
                    S.op("act", lambda e, nsb=nsb: e.activation(tf[:, 0:nsb * 16], tf[:, 0:nsb * 16], AF.Ln, bias=cst[:, 1:2]),
                         reads=["tf", "cst1"], writes=["tf"])
                    S.op("dve", lambda e, nsb=nsb, b0=b0: e.tensor_scalar(
                        LF[:, b0:b0 + nsb, :], tf[:, 0:nsb * 16].rearrange("p (b h) -> p b h", h=16), -1.0, None, ALU.mult),
                        reads=["tf"], writes=["LF"])
            S.barrier()

        def phaseF(l):
            with ExitStack() as ph:
                FT = sb(ph, "f_FT", [16, L], F32)
                X = sb(ph, "f_X", [16, L], F32)
                R1 = sb(ph, "f_R1", [16, L], F32)
                A = sb(ph, "f_A", [16, 3, L], BF16)
                tot = sb(ph, "f_tot", [16, NB + 1], F32)
                car = sb(ph, "f_car", [16, NB + 1], F32)
                RS = sb(ph, "f_RS", [16, NTA + NB], F32)
                sel = sb(ph, "f_sel", [16, 16 * 128], F32)
                S.dma("sp", sel[:], sel_d, writes=["sel"])
                for b in range(NB):
                    S.op("pe", lambda e, b=b: e.matmul(ps[6][0:16, b:b + 1], LF[:, b, :], onesf[:, 0:1],
                                                       start=True, stop=True),
                         reads=["LF", "onesf"], writes=[PS[6]])
                S.op("dve", lambda e: e.tensor_copy(tot[:, 0:NB], ps[6][0:16, 0:NB]), reads=[PS[6]], writes=["tot"])
                S.op("dve", lambda e: e.memset(car[:, 0:1], 0.0), writes=["car"])
                for b in range(1, NB):
                    S.op("dve", lambda e, b=b: e.tensor_tensor(car[:, b:b + 1], car[:, b - 1:b], tot[:, b - 1:b], ALU.add),
                         reads=["car", "tot"], writes=["car"])
                for b in range(NB):
                    pi = b % 4
                    S.op("pe", lambda e, b=b, pi=pi: e.matmul(ps[pi][0:16, 0:128], LF[:, b, :], umat[:, :],
                                                              start=True, stop=True),
                         reads=["LF", "umat"], writes=[PS[pi]])
                    S.op("dve", lambda e, b=b, pi=pi: e.tensor_scalar(
                        FT[:, b * 128:(b + 1) * 128], ps[pi][0:16, 0:128], car[:, b:b + 1], None, ALU.add),
                        reads=[PS[pi], "car"], writes=["FT"])
                S.op("dve", lambda e: e.tensor_copy(RS[:, 0:1], FT[:, 0:1]), reads=["FT"], writes=["RS"])
                S.op("dve", lambda e: e.tensor_copy(RS[:, 1:NTA], FT[:, 128:T:1024]), reads=["FT"], writes=["RS"])
                S.op("dve", lambda e: e.tensor_copy(RS[:, NTA:NTA + NB], FT[:, 0:L:128]), reads=["FT"], writes=["RS"])
                for h in range(NH):
                    pi = 4 + (h % 2)
                    S.op("pe", lambda e, h=h, pi=pi: e.matmul(ps[pi][:, 0:NTA + NB], sel[:, h * 128:(h + 1) * 128],
                                                              RS[:, :], start=True, stop=True),
                         reads=["sel", "RS"], writes=[PS[pi]])
                    S.op("dve", lambda e, h=h, pi=pi: e.tensor_copy(RB[:, h, 0:NTA], ps[pi][:, 0:NTA]),
                         reads=[PS[pi]], writes=["RB"])
                    S.op("dve", lambda e, h=h, pi=pi: e.tensor_scalar(RB[:, h, NTA:NTA + NB], ps[pi][:, NTA:NTA + NB],
                                                                      -1.0, None, ALU.mult),
                         reads=[PS[pi]], writes=["RB"])

                def split3(dst_dram, row0, tokname):
                    S.op("dve", lambda e: e.tensor_copy(A[:, 0, :], X[:, :]), reads=["X"], writes=["A"])
                    S.op("dve", lambda e: e.tensor_tensor(R1[:, :], X[:, :], A[:, 0, :], ALU.subtract),
                         reads=["X", "A"], writes=["R1"])
                    S.op("dve", lambda e: e.tensor_copy(A[:, 1, :], R1[:, :]), reads=["R1"], writes=["A"])
                    S.op("dve", lambda e: e.tensor_tensor(R1[:, :], R1[:, :], A[:, 1, :], ALU.subtract),
                         reads=["R1", "A"], writes=["R1"])
                    S.op("dve", lambda e: e.tensor_copy(A[:, 2, :], R1[:, :]), reads=["R1"], writes=["A"])
                    S.dma("pool", dst_dram[:, row0:row0 + 3, :], A[:, :, :], reads=["A"], writes=[tokname])

                if T < L:
                    S.op("dve", lambda e: e.memset(X[:, T:L], 0.0), reads=["A"], writes=["X"])
                for ia, (t0, W) in enumerate(tilesA):
                    S.op("dve", lambda e, ia=ia, t0=t0, W=W: e.tensor_scalar(
                        X[:, t0:t0 + W], FT[:, t0:t0 + W], RS[:, ia:ia + 1], None, ALU.subtract),
                        reads=["FT", "RS", "A"], writes=["X"])
                split3(qT, 64, "qTa")
                for b in range(NB):
                    S.op("dve", lambda e, b=b: e.tensor_scalar(
                        X[:, b * 128:(b + 1) * 128], FT[:, b * 128:(b + 1) * 128], RS[:, NTA + b:NTA + b + 1], None,
                        ALU.subtract), reads=["FT", "RS", "A"], writes=["X"])
                split3(kT, 67, "kTa")
            S.barrier()

        def phase2(l):
            with ExitStack() as ph:
                Kh = [sb(ph, "a_K%d" % i, [KA, L], BF16) for i in range(2)]
                Qh = [sb(ph, "a_Q%d" % i, [KA, L], BF16) for i in range(2)]
                Vh = [sb(ph, "a_V%d" % i, [128, NB, 128], BF16) for i in range(2)]
                Gh = [sb(ph, "a_G%d" % i, [64, L], BF16) for i in range(2)]
                Bc = [sb(ph, "a_B%d" % i, [128, NTA, NB], F32) for i in range(2)]
                NPB = 4
                PT = [sb(ph, "a_P%d" % i, [128, 1024], BF16) for i in range(NPB)]
                rb = sb(ph, "a_rb", [64, 1024], F32)
                tt = sb(ph, "a_tt", [64, 1024], F32)
                amo = [sb(ph, "a_am%d" % i, [64, 1024], BF16) for i in range(2)]
                alltok = lambda name: [(name, ti) for ti in range(NT)]

                def load_head(h):
                    i = h % 2
                    S.dma("sp", Kh[i][:, :], kT[h, :, :], reads=alltok("kT") + ["kTa", ("kc", h)], writes=["Kh%d" % i])
                    S.dma("sp", Qh[i][:, :], qT[h, :, :], reads=alltok("qT") + ["qTa", ("qc", h)], writes=["Qh%d" % i])
                    S.dma("sp", Vh[i][:, :, :], vA[h, :, :, :], reads=alltok("vA") + [("vc", h)], writes=["Vh%d" % i])
                    S.dma("sp", Gh[i][:, :], sga[h, :, :], reads=alltok("sga"), writes=["Gh%d" % i])
                    for ia in range(NTA):
                        S.op("dve", lambda e, i=i, h=h, ia=ia: e.tensor_scalar(
                            Bc[i][:, ia, :], RB[:, h, NTA:NTA + NB], RB[:, h, ia:ia + 1], None, ALU.add),
                            reads=["RB"], writes=["Bc%d" % i])

                def pieces(lo, hi):
                    out = []
                    if lo < 512:
                        out.append((lo, min(hi, 512)))
                    if hi > 512:
                        out.append((max(lo, 512), hi))
                    return out

                LA = 2
                items = []
                for h in range(NH):
                    for ia, (t0, W) in enumerate(tilesA):
                        jmax = (t0 + W + 127) // 128 - 1
                        for j in range(jmax + 1):
                            items.append((h, ia, t0, W, j, jmax))

                def emit_front(n):
                    h, ia, t0, W, j, jmax = items[n]
                    i = h % 2
                    k0 = j * 128
                    q0 = max(t0, k0)
                    Wj = t0 + W - q0
                    sw = n % 2
                    pb = n % NPB
                    diag = k0 >= t0

                    def fS(e, i=i, sw=sw, k0=k0, q0=q0, Wj=Wj, diag=diag):
                        lo = 0
                        ins = None
                        if diag:
                            mw = min(128, Wj)
                            e.matmul(psw[sw][:, 0:mw], identb[:, :], negm[:, 0:mw], start=True, stop=False)
                            ins = e.matmul(psw[sw][:, 0:mw], Kh[i][:, k0:k0 + 128], Qh[i][:, q0:q0 + mw],
                                           start=False, stop=True)
                            lo = mw
                        if Wj > lo:
                            for (a_, b_) in pieces(lo, Wj):
                                ins = e.matmul(psw[sw][:, a_:b_], Kh[i][:, k0:k0 + 128], Qh[i][:, q0 + a_:q0 + b_],
                                               start=True, stop=True)
                        return ins
                    S.op("pe", fS, reads=["Kh%d" % i, "Qh%d" % i, "identb", "negm"], writes=[PS[2 * sw], PS[2 * sw + 1]])
                    S.op("act", lambda e, i=i, sw=sw, pb=pb, ia=ia, j=j, Wj=Wj: e.activation(
                        PT[pb][:, 0:Wj], psw[sw][:, 0:Wj], AF.Exp, bias=Bc[i][:, ia, j:j + 1]),
                        reads=[PS[2 * sw], PS[2 * sw + 1], "Bc%d" % i], writes=["PT%d" % pb])

                def emit_back(n):
                    h, ia, t0, W, j, jmax = items[n]
                    i = h % 2
                    if ia == 0 and j == 0:
                        conv_trickle(6)
                    if ia == 0 and j == 0 and h + 1 < NH:
                        load_head(h + 1)
                    k0 = j * 128
                    q0 = max(t0, k0)
                    c0 = q0 - t0
                    Wj = W - c0
                    pb = n % NPB
                    tcount = h * NTA + ia
                    ow = 2 + (tcount % 2)

                    def fO(e, i=i, ow=ow, pb=pb, j=j, c0=c0, Wj=Wj, jmax=jmax):
                        ins = None
                        for (a_, b_) in pieces(c0, c0 + Wj):
                            ins = e.matmul(psw[ow][:, a_:b_], Vh[i][:, j, :], PT[pb][:, a_ - c0:b_ - c0],
                                           start=(j == 0), stop=(j == jmax))
                        return ins
                    S.op("pe", fO, reads=["Vh%d" % i, "PT%d" % pb], writes=[PS[2 * ow], PS[2 * ow + 1]])
                    if j != jmax:
                        return
                    OT = [PS[2 * ow], PS[2 * ow + 1]]
                    S.op("dve", lambda e, ow=ow, W=W: e.reciprocal(rb[:, 0:W], psw[ow][64:128, 0:W]),
                         reads=OT, writes=["rb"])
                    S.op("dve", lambda e, ow=ow, W=W: e.tensor_tensor(tt[:, 0:W], psw[ow][0:64, 0:W], rb[:, 0:W], ALU.mult),
                         reads=OT + ["rb"], writes=["tt"])
                    ai = tcount % 2
                    S.op("dve", lambda e, ai=ai, i=i, t0=t0, W=W: e.tensor_tensor(
                        amo[ai][:, 0:W], tt[:, 0:W], Gh[i][:, t0:t0 + W], ALU.mult),
                        reads=["tt", "Gh%d" % i], writes=["amo%d" % ai])
                    S.dma("pool", amT[h, :, t0:t0 + W], amo[ai][:, 0:W], reads=["amo%d" % ai],
                          writes=[("amT", k) for k in range(t0 // 512, (t0 + W + 511) // 512)])

                load_head(0)
                for n in range(len(items) + LA):
                    if n < len(items):
                        emit_front(n)
                    if n - LA >= 0:
                        emit_back(n - LA)
                conv_trickle(len(conv_queue))
            S.barrier()

        def phase3a(l):
            with ExitStack() as ph:
                wop = sb(ph, "c_wop", [128, 8, D], BF16)
                woa = sb(ph, "c_woa", [64, NH, D], BF16)
                amB = [sb(ph, "c_am%d" % i, [64, NH, 512], BF16) for i in range(2)]
                pmB = [sb(ph, "c_pm%d" % i, [128, 8, 512], BF16) for i in range(2)]
                xtB = [sb(ph, "c_xt%d" % i, [128, 8, 512], F32) for i in range(2)]
                moB = [sb(ph, "c_mo%d" % i, [128, 8, 512], F32) for i in range(2)]
                sq = sb(ph, "c_sq", [128, 8, 512], BF16)
                lnv = sb(ph, "c_lnv", [128, 512], F32)
                rstd = sb(ph, "c_rstd", [128, 512], F32)
                ROT = [1, 2, 3, 4]
                S.dma("sp", wop[:], wout_b[l].rearrange("(c p) n -> p c n", p=128), reads=[("wout_b", l)], writes=["wop"])
                S.dma("sp", woa[:], wout_b[l].rearrange("(h p) n -> p h n", p=64), reads=[("wout_b", l)], writes=["woa"])
                src = res0 if l == 0 else res

                def epilogue(ti):
                    t0, W = tiles[ti]
                    bsel = ti % 2
                    mo, xt = moB[bsel], xtB[bsel]
                    mot, xtt = "mo%d" % bsel, "xt3%d" % bsel
                    S.op("dve", lambda e: e.tensor_tensor(sq[:, :, 0:W], mo[:, :, 0:W], mo[:, :, 0:W], ALU.mult),
                         reads=[mot], writes=["sq3"])
                    rms_stats(sq, W, 0, lnv, rstd, "sq3")
                    for c in range(8):
                        S.op("dve", lambda e, c=c: e.scalar_tensor_tensor(
                            mo[:, c, 0:W], mo[:, c, 0:W], gv[:, gcol(1, l, c):gcol(1, l, c) + 1], rstd[:, 0:W],
                            ALU.mult, ALU.mult), reads=[mot, "rstd", "gv"], writes=[mot])
                    S.op("dve", lambda e: e.tensor_tensor(mo[:, :, 0:W], mo[:, :, 0:W], xt[:, :, 0:W], ALU.add),
                         reads=[mot, xtt], writes=[mot])
                    S.dma("pool", res[:, :, t0:t0 + W].rearrange("c p t -> p c t"), mo[:, :, 0:W],
                          reads=[mot], writes=rtok(t0, W))

                for ti, (t0, W) in enumerate(tiles):
                    bsel = ti % 2
                    am, pm, xt, mo = amB[bsel], pmB[bsel], xtB[bsel], moB[bsel]
                    amt, pmt, xtt, mot = "am%d" % bsel, "pm3%d" % bsel, "xt3%d" % bsel, "mo%d" % bsel
                    S.dma("sp", am[:, :, 0:W], amT[:, :, t0:t0 + W].rearrange("h r t -> r h t"),
                          reads=[("amT", ti)], writes=[amt])
                    S.dma("sp", pm[:, :, 0:W], pmT[:, :, t0:t0 + W].rearrange("c p t -> p c t"),
                          reads=[("pmT", ti)], writes=[pmt])
                    S.dma("sp", xt[:, :, 0:W], src[:, :, t0:t0 + W].rearrange("c p t -> p c t"),
                          reads=rtok(t0, W), writes=[xtt])
                    for oc in range(8):
                        pi = next_ps(ROT)

                        def f(e, oc=oc, pi=pi, W=W, pm=pm, am=am):
                            ins = None
                            for c in range(8):
                                ins = e.matmul(ps[pi][:, 0:W], wop[:, c, oc * 128:(oc + 1) * 128], pm[:, c, 0:W],
                                               start=(c == 0), stop=False)
                            for h in range(NH):
                                ins = e.matmul(ps[pi][:, 0:W], woa[:, h, oc * 128:(oc + 1) * 128], am[:, h, 0:W],
                                               start=False, stop=(h == NH - 1))
                            return ins
                        S.op("pe", f, reads=["wop", "woa", pmt, amt], writes=[PS[pi]])
                        S.op("act", lambda e, oc=oc, pi=pi, W=W, mo=mo: e.activation(mo[:, oc, 0:W], ps[pi][:, 0:W], AF.Copy),
                             reads=[PS[pi]], writes=[mot])
                        if oc == 1 and ti > 0:
                            epilogue(ti - 1)
                epilogue(NT - 1)
            S.barrier()

        def phase3b(l, last):
            with ExitStack() as ph:
                wg = sb(ph, "d_wg", [128, 8, DFF], BF16)
                wu = sb(ph, "d_wu", [128, 8, DFF], BF16)
                wd = sb(ph, "d_wd", [128, NFC, D], BF16)
                xt = [sb(ph, "d_xt%d" % i, [128, 8, 256], F32) for i in range(2)]
                hn = [sb(ph, "d_hn%d" % i, [128, 8, 256], BF16) for i in range(2)]
                lnvA = sb(ph, "d_lnvA", [128, 256], F32)
                rstdA = sb(ph, "d_rstdA", [128, 256], F32)
                lnvB = sb(ph, "d_lnvB", [128, 256], F32)
                rstdB = sb(ph, "d_rstdB", [128, 256], F32)
                sg = [sb(ph, "d_sg%d" % i, [128, 256], F32) for i in range(2)]
                ff = sb(ph, "d_ff", [128, NFC, 256], BF16)
                fo = sb(ph, "d_fo", [128, 8, 256], F32)
                sq = sb(ph, "d_sq", [128, 8, 256], BF16)
                ROT = [1, 2, 3, 4, 5, 6]
                FQ = [(0, 256), (256, 1024), (1024, 1920), (1920, DFF)]
                for (c0, c1) in FQ:
                    S.dma("sp", wg[:, :, c0:c1], wg_b[l, :, c0:c1].rearrange("(k p) n -> p k n", p=128),
                          reads=[("wg_b", l)], writes=[("wg", c0)])
                    S.dma("sp", wu[:, :, c0:c1], wu_b[l, :, c0:c1].rearrange("(k p) n -> p k n", p=128),
                          reads=[("wu_b", l)], writes=[("wu", c0)])

                def fq(fc):
                    for (c0, c1) in FQ:
                        if c0 <= fc * 128 < c1:
                            return c0
                S.dma("sp", wd[:], wd_b[l].rearrange("(c p) n -> p c n", p=128), reads=[("wd_b", l)], writes=["wd"])

                def stats(src_sq, W, lnv, rstd, sqtok, sfx):
                    def f(e):
                        ins = None
                        for c in range(8):
                            ins = e.matmul(ps[0][:, 0:W], onesb[:, :], src_sq[:, c, 0:W], start=(c == 0), stop=(c == 7))
                        return ins
                    S.op("pe", f, reads=["onesb", sqtok], writes=[PS[0]])
                    S.op("act", lambda e: e.activation(lnv[:, 0:W], ps[0][:, 0:W], AF.Ln, bias=cst[:, 0:1]),
                         reads=[PS[0], "cst0"], writes=["lnv" + sfx])
                    S.op("act", lambda e: e.activation(rstd[:, 0:W], lnv[:, 0:W], AF.Exp, scale=-0.5),
                         reads=["lnv" + sfx], writes=["rstd" + sfx])

                def prologue(t2):
                    t0, W = tiles2[t2]
                    b = t2 % 2
                    S.dma("sp", xt[b][:, :, 0:W], res[:, :, t0:t0 + W].rearrange("c p t -> p c t"),
                          reads=rtok(t0, W), writes=["xt4%d" % b])
                    S.op("dve", lambda e: e.tensor_tensor(hn[b][:, :, 0:W], xt[b][:, :, 0:W], xt[b][:, :, 0:W], ALU.mult),
                         reads=["xt4%d" % b], writes=["hn4%d" % b])
                    stats(hn[b], W, lnvA, rstdA, "hn4%d" % b, "A")
                    for c in range(8):
                        S.op("dve", lambda e, c=c: e.scalar_tensor_tensor(
                            hn[b][:, c, 0:W], xt[b][:, c, 0:W], gv[:, gcol(2, l, c):gcol(2, l, c) + 1], rstdA[:, 0:W],
                            ALU.mult, ALU.mult), reads=["xt4%d" % b, "rstdA", "gv", PS[0]], writes=["hn4%d" % b])

                def gate_up(t2, fc0, fc1):
                    t0, W = tiles2[t2]
                    b = t2 % 2
                    for fc in range(fc0, fc1):
                        pg = next_ps(ROT)
                        pu = next_ps(ROT)

                        def fg(e, fc=fc, pg=pg):
                            ins = None
                            for kc in range(8):
                                ins = e.matmul(ps[pg][:, 0:W], wg[:, kc, fc * 128:(fc + 1) * 128], hn[b][:, kc, 0:W],
                                               start=(kc == 0), stop=(kc == 7))
                            return ins

                        def fu(e, fc=fc, pu=pu):
                            ins = None
                            for kc in range(8):
                                ins = e.matmul(ps[pu][:, 0:W], wu[:, kc, fc * 128:(fc + 1) * 128], hn[b][:, kc, 0:W],
                                               start=(kc == 0), stop=(kc == 7))
                            return ins
                        S.op("pe", fg, reads=[("wg", fq(fc)), "hn4%d" % b], writes=[PS[pg]])
                        S.op("pe", fu, reads=[("wu", fq(fc)), "hn4%d" % b], writes=[PS[pu]])
                        sgi = fc % 2
                        S.op("act", lambda e, sgi=sgi, pg=pg: e.activation(sg[sgi][:, 0:W], ps[pg][:, 0:W], AF.Silu),
                             reads=[PS[pg]], writes=["sg%d" % sgi])
                        S.op("dve", lambda e, fc=fc, sgi=sgi, pu=pu: e.tensor_tensor(
                            ff[:, fc, 0:W], sg[sgi][:, 0:W], ps[pu][:, 0:W], ALU.mult),
                            reads=["sg%d" % sgi, PS[pu]], writes=["ff"])

                def down(t2):
                    t0, W = tiles2[t2]
                    for oc in range(8):
                        pi = next_ps(ROT)

                        def fd(e, oc=oc, pi=pi):
                            ins = None
                            for fc in range(NFC):
                                ins = e.matmul(ps[pi][:, 0:W], wd[:, fc, oc * 128:(oc + 1) * 128], ff[:, fc, 0:W],
                                               start=(fc == 0), stop=(fc == NFC - 1))
                            return ins
                        S.op("pe", fd, reads=["wd", "ff"], writes=[PS[pi]])
                        S.op("act", lambda e, oc=oc, pi=pi: e.activation(fo[:, oc, 0:W], ps[pi][:, 0:W], AF.Copy),
                             reads=[PS[pi]], writes=["fo"])

                EPI = "pool"

                def epiA(t2):
                    t0, W = tiles2[t2]
                    S.op(EPI, lambda e: e.tensor_tensor(sq[:, :, 0:W], fo[:, :, 0:W], fo[:, :, 0:W], ALU.mult),
                         reads=["fo"], writes=["sq4"])

                def epiB(t2):
                    t0, W = tiles2[t2]
                    stats(sq, W, lnvB, rstdB, "sq4", "B")

                def epiC(t2):
                    t0, W = tiles2[t2]
                    b = t2 % 2
                    for c in range(8):
                        S.op("dve", lambda e, c=c: e.scalar_tensor_tensor(
                            fo[:, c, 0:W], fo[:, c, 0:W], gv[:, gcol(3, l, c):gcol(3, l, c) + 1], rstdB[:, 0:W],
                            ALU.mult, ALU.mult), reads=["fo", "rstdB", "gv"], writes=["fo"])
                    S.op(EPI, lambda e: e.tensor_tensor(fo[:, :, 0:W], fo[:, :, 0:W], xt[b][:, :, 0:W], ALU.add),
                         reads=["fo", "xt4%d" % b], writes=["fo"])
                    if not last:
                        S.dma("pool", res[:, :, t0:t0 + W].rearrange("c p t -> p c t"), fo[:, :, 0:W],
                              reads=["fo"], writes=rtok(t0, W))
                    else:
                        a = max(t0, META)
                        bb = min(t0 + W, T)
                        if bb > a:
                            S.dma("pool", outT[:, :, a - META:bb - META].rearrange("c p t -> p c t"),
                                  fo[:, :, a - t0:bb - t0], reads=["fo"], writes=[("out", t2)])

                n2 = len(tiles2)
                prologue(0)
                for t2 in range(n2):
                    if t2 > 0:
                        epiA(t2 - 1)
                    gate_up(t2, 0, 4)
                    if t2 > 0:
                        epiB(t2 - 1)
                        epiC(t2 - 1)
                    gate_up(t2, 4, NFC)
                    if t2 + 1 < n2:
                        prologue(t2 + 1)
                    down(t2)
                epiA(n2 - 1)
                epiB(n2 - 1)
                epiC(n2 - 1)
            S.barrier()

        for l in range(DEPTH):
            phase1(l)
            if stop_after in ("p1", "%d:p1" % l):
                break
            if l + 1 < DEPTH:
                convert_layer(l + 1, False)
            phaseF(l)
            if stop_after in ("pf", "%d:pf" % l):
                break
            phase2(l)
            if stop_after in ("p2", "%d:p2" % l):
                break
            phase3a(l)
            if stop_after in ("p3a", "%d:p3a" % l):
                break
            phase3b(l, last=(l == DEPTH - 1))
        S.emit(st)
    return nc


def _host_consts(DEPTH):
    negm = np.where(np.arange(128)[:, None] > np.arange(128)[None, :], -30000.0, 0.0).astype(ml_dtypes.bfloat16)
    identb = np.eye(128, dtype=np.float32).astype(ml_dtypes.bfloat16)
    umat = (np.arange(128)[:, None] <= np.arange(128)[None, :]).astype(np.float32)
    sel = np.zeros((16, 16, 128), np.float32)
    for h in range(16):
        sel[h, h, :] = 1.0
    sel = sel.reshape(16, 16 * 128)
    invc = np.zeros((128, 4, 16), np.float32)
    for g in range(4):
        w = 2 << g
        invc[:, g, :] = 1.0 / np.minimum(np.arange(16) + 1, w).astype(np.float32)[None, :]
    return negm, identb, umat, sel, invc.reshape(128, 64)


def make_in_maps(inputs, SEQ, DEPTH, n_cores):
    T = SEQ + META
    L = ((T + 127) // 128) * 128
    x = np.asarray(inputs["x"], np.float32)
    B = x.shape[0]
    meta = np.asarray(inputs["meta_tokens"], np.float32)
    negm, identb, umat, sel, invc = _host_consts(DEPTH)
    kinds = ["norm_mix_pre", "norm_mix_post", "norm_ffn_pre", "norm_ffn_post", "pool_scale"]
    gv = np.stack([np.asarray(inputs[k], np.float32) for k in kinds], 0)
    gv = gv.reshape(5, DEPTH, 8, 128).transpose(3, 0, 1, 2).reshape(128, 5 * DEPTH * 8)
    bf = np.asarray(inputs["b_forget"], np.float32)
    bfb = np.broadcast_to(bf[None, :, None, :], (128, DEPTH, 4, 16)).reshape(128, DEPTH * 64)
    common = {
        "gv": np.ascontiguousarray(gv), "bfb": np.ascontiguousarray(bfb),
        "negm": negm, "identb": identb, "umat": umat, "sel": sel, "invc": invc,
        "cneg": np.full((3, L), -1.0, ml_dtypes.bfloat16), "cpos": np.full((3, L), 1.0, ml_dtypes.bfloat16),
        "cv": np.full((128, L // 128, 64), 1.0, ml_dtypes.bfloat16),
        "w_in": np.asarray(inputs["w_in"], np.float32), "w_pool": np.asarray(inputs["w_pool"], np.float32),
        "w_out": np.asarray(inputs["w_out"], np.float32), "w_ffn_gate": np.asarray(inputs["w_ffn_gate"], np.float32),
        "w_ffn_up": np.asarray(inputs["w_ffn_up"], np.float32), "w_ffn_down": np.asarray(inputs["w_ffn_down"], np.float32),
    }
    maps = []
    for c in range(n_cores):
        b = c % B
        r = np.zeros((L, D), np.float32)
        r[0:META] = meta
        r[META:T] = x[b]
        r0 = np.ascontiguousarray(r.T.reshape(8, 128, L))
        m = dict(common)
        m["res0"] = r0
        maps.append(m)
    return maps


_NC_CACHE = {}


def kernel(x, meta_tokens, norm_mix_pre, norm_mix_post, norm_ffn_pre, norm_ffn_post,
           w_in, b_forget, w_pool, pool_scale, w_out, w_ffn_gate, w_ffn_up, w_ffn_down):
    inputs = dict(x=x, meta_tokens=meta_tokens, norm_mix_pre=norm_mix_pre, norm_mix_post=norm_mix_post,
                  norm_ffn_pre=norm_ffn_pre, norm_ffn_post=norm_ffn_post, w_in=w_in, b_forget=b_forget,
                  w_pool=w_pool, pool_scale=pool_scale, w_out=w_out, w_ffn_gate=w_ffn_gate,
                  w_ffn_up=w_ffn_up, w_ffn_down=w_ffn_down)
    x = np.asarray(x)
    B, SEQ, _ = x.shape
    DEPTH = np.asarray(w_in).shape[0]
    key = (SEQ, DEPTH)
    if key not in _NC_CACHE:
        _NC_CACHE[key] = build_nc(SEQ, DEPTH)
    nc = _NC_CACHE[key]
    maps = make_in_maps(inputs, SEQ, DEPTH, N_CORES)
    res = run_bass_kernel_spmd(nc, maps, core_ids=list(range(N_CORES)))
    out = np.empty((B, SEQ, D), np.float32)
    for b in range(B):
        o = np.asarray(res.results[b]["outT"], np.float32)
        out[b] = o.reshape(D, SEQ).T
    return out
```
